# Optimizing a Trainium2 kernel written in Bass

```python
import math
import jax, jax.numpy as jnp
from jax import lax
import numpy as np

D_MODEL = 1024
BATCH = 32
SEQ = 2048
DEPTH = 2

CHUNK = 64
Q_BLOCK = 128
HEAD_DIM = 64
N_HEADS_TOTAL = D_MODEL // HEAD_DIM
N_HEADS_B = N_HEADS_TOTAL // 4
N_HEADS_A = (N_HEADS_TOTAL - N_HEADS_B) // 2
N_HEADS_C = N_HEADS_TOTAL - N_HEADS_A - N_HEADS_B
DIFF_QK_DIM = HEAD_DIM // 2
DIFF_V_DIM = HEAD_DIM
WIDTH_A = N_HEADS_A * HEAD_DIM
WIDTH_B = N_HEADS_B * DIFF_V_DIM
WIDTH_C = N_HEADS_C * HEAD_DIM
B_QK_WIDTH = N_HEADS_B * 2 * DIFF_QK_DIM
D_MIX = WIDTH_A + WIDTH_B + WIDTH_C
IN_COLS = 3 * WIDTH_A + 2 * B_QK_WIDTH + WIDTH_B + 3 * WIDTH_C
LEFT_CHUNKS = 8
BAND = (LEFT_CHUNKS + 1) * CHUNK
REL_CLIP = 128
D_FF = 2816
NORM_EPS = 1e-6
NEG_INF = -1e30
FFN_RESIDUAL_WEIGHT = 0.5

kernel_name = "hybrid_chunk_causal_hymba_encoder"


def rmsnorm(x, g):
    xf = x.astype(jnp.float32)
    y = xf * lax.rsqrt(jnp.mean(xf * xf, axis=-1, keepdims=True) + NORM_EPS)
    return (y * g.astype(jnp.float32)).astype(x.dtype)


def swiglu(x, w_gate, w_up, w_down):
    return (jax.nn.silu(x @ w_gate) * (x @ w_up)) @ w_down


def chunk_band_attention(q, k, v, rel_bias):
    b, s, h, d = q.shape
    nc = s // CHUNK
    qc = q.reshape(b, nc, CHUNK, h, d)
    pad = ((0, 0), (LEFT_CHUNKS, 0), (0, 0), (0, 0), (0, 0))
    kp = jnp.pad(k.reshape(b, nc, CHUNK, h, d), pad)
    vp = jnp.pad(v.reshape(b, nc, CHUNK, h, d), pad)
    kband = jnp.concatenate([kp[:, w:w + nc] for w in range(LEFT_CHUNKS + 1)], axis=2)
    vband = jnp.concatenate([vp[:, w:w + nc] for w in range(LEFT_CHUNKS + 1)], axis=2)
    qi = jnp.arange(CHUNK)
    kk = jnp.arange(BAND)
    rel = jnp.clip(qi[:, None] + LEFT_CHUNKS * CHUNK - kk[None, :], -REL_CLIP, REL_CLIP) + REL_CLIP
    bias = rel_bias[:, rel].astype(jnp.float32)
    valid = (jnp.arange(nc)[:, None] - LEFT_CHUNKS + kk[None, :] // CHUNK) >= 0
    scores = jnp.einsum('bnqhd,bnkhd->bnhqk', qc, kband).astype(jnp.float32) * (d ** -0.5)
    scores = scores + bias[None, None]
    scores = jnp.where(valid[None, :, None, None, :], scores, NEG_INF)
    p = jax.nn.softmax(scores, axis=-1).astype(v.dtype)
    o = jnp.einsum('bnhqk,bnkhd->bnqhd', p, vband)
    return o.reshape(b, s, h, d)


def diff_attention(q, k, v, lam, subln_g, lambda_init):
    b, s, h, _, dq = q.shape
    scale = dq ** -0.5
    slopes = jnp.exp2(-8.0 * jnp.arange(1, h + 1, dtype=jnp.float32) / h)
    outs = []
    for blk in range(s // Q_BLOCK):
        q0 = blk * Q_BLOCK
        kend = q0 + Q_BLOCK
        tpos = q0 + jnp.arange(Q_BLOCK)
        spos = jnp.arange(kend)
        allowed = (spos[None, :] // CHUNK) <= (tpos[:, None] // CHUNK)
        dist = jnp.abs(tpos[:, None] - spos[None, :]).astype(jnp.float32)
        alibi = -slopes[:, None, None] * dist[None]
        sc = jnp.einsum('bqhmd,bkhmd->bhmqk', q[:, q0:kend], k[:, :kend]).astype(jnp.float32) * scale
        sc = sc + alibi[None, :, None]
        sc = jnp.where(allowed[None, None, None], sc, NEG_INF)
        p = jax.nn.softmax(sc, axis=-1)
        w = p[:, :, 0] - lam * p[:, :, 1]
        outs.append(jnp.einsum('bhqk,bkhd->bqhd', w.astype(v.dtype), v[:, :kend]))
    o = jnp.concatenate(outs, axis=1)
    return rmsnorm(o, subln_g) * (1.0 - lambda_init)


def stick_breaking_attention(q, k, v):
    b, s, h, d = q.shape
    scale = d ** -0.5
    outs = []
    for blk in range(s // Q_BLOCK):
        q0 = blk * Q_BLOCK
        kend = q0 + Q_BLOCK
        tpos = q0 + jnp.arange(Q_BLOCK)
        spos = jnp.arange(kend)
        strict = spos[None, :] < tpos[:, None]
        z = jnp.einsum('bqhd,bkhd->bhqk', q[:, q0:kend], k[:, :kend]).astype(jnp.float32) * scale
        log_beta = jax.nn.log_sigmoid(z)
        log_one_minus = jnp.where(strict, jax.nn.log_sigmoid(-z), 0.0)
        suffix = lax.cumsum(log_one_minus, axis=3, reverse=True) - log_one_minus
        a = jnp.where(strict, jnp.exp(log_beta + suffix), 0.0)
        outs.append(jnp.einsum('bhqk,bkhd->bqhd', a.astype(v.dtype), v[:, :kend]))
    return jnp.concatenate(outs, axis=1)


def token_mixing(h, w_in, rel_bias, lq1, lk1, lq2, lk2, subln_g, w_out, lambda_init):
    b, s, _ = h.shape
    proj = h @ w_in
    sizes = [WIDTH_A, WIDTH_A, WIDTH_A, B_QK_WIDTH, B_QK_WIDTH, WIDTH_B, WIDTH_C, WIDTH_C, WIDTH_C]
    qa, ka, va, qb, kb, vb, qc, kc, vc = jnp.split(proj, [int(c) for c in np.cumsum(sizes)[:-1]], axis=-1)
    hd_a = (b, s, N_HEADS_A, HEAD_DIM)
    o_a = chunk_band_attention(qa.reshape(hd_a), ka.reshape(hd_a), va.reshape(hd_a), rel_bias)
    lam = (jnp.exp(jnp.sum(lq1.astype(jnp.float32) * lk1.astype(jnp.float32)))
           - jnp.exp(jnp.sum(lq2.astype(jnp.float32) * lk2.astype(jnp.float32))) + lambda_init)
    hd_bqk = (b, s, N_HEADS_B, 2, DIFF_QK_DIM)
    o_b = diff_attention(qb.reshape(hd_bqk), kb.reshape(hd_bqk), vb.reshape(b, s, N_HEADS_B, DIFF_V_DIM),
                         lam, subln_g, lambda_init)
    hd_c = (b, s, N_HEADS_C, HEAD_DIM)
    o_c = stick_breaking_attention(qc.reshape(hd_c), kc.reshape(hd_c), vc.reshape(hd_c))
    y = jnp.concatenate([o_a.reshape(b, s, WIDTH_A), o_b.reshape(b, s, WIDTH_B),
                         o_c.reshape(b, s, WIDTH_C)], axis=-1)
    return y @ w_out


def setup_inputs(seed: int = 0) -> dict:
    key = jax.random.key(seed)
    ks = jax.random.split(key, 32)
    f32 = jnp.float32

    def nrm(k, shape, scale):
        return jax.random.normal(k, shape, f32) * scale

    def gain(k, shape):
        return 1.0 + 0.05 * jax.random.normal(k, shape, f32)

    L = DEPTH
    return {
        "x": jax.random.normal(ks[0], (BATCH, SEQ, D_MODEL), f32),
        "ffn1_pre_g": gain(ks[1], (L, D_MODEL)),
        "ffn1_w_gate": nrm(ks[2], (L, D_MODEL, D_FF), D_MODEL ** -0.5),
        "ffn1_w_up": nrm(ks[3], (L, D_MODEL, D_FF), D_MODEL ** -0.5),
        "ffn1_w_down": nrm(ks[4], (L, D_FF, D_MODEL), D_FF ** -0.5),
        "ffn1_post_g": gain(ks[5], (L, D_MODEL)),
        "mix_pre_g": gain(ks[6], (L, D_MODEL)),
        "w_in": nrm(ks[7], (L, D_MODEL, IN_COLS), D_MODEL ** -0.5),
        "rel_bias": nrm(ks[8], (L, N_HEADS_A, 2 * REL_CLIP + 1), 0.2),
        "diff_lambda_q1": nrm(ks[9], (L, DIFF_QK_DIM), 0.1),
        "diff_lambda_k1": nrm(ks[10], (L, DIFF_QK_DIM), 0.1),
        "diff_lambda_q2": nrm(ks[11], (L, DIFF_QK_DIM), 0.1),
        "diff_lambda_k2": nrm(ks[12], (L, DIFF_QK_DIM), 0.1),
        "diff_subln_g": gain(ks[13], (L, DIFF_V_DIM)),
        "w_out": nrm(ks[14], (L, D_MIX, D_MODEL), D_MIX ** -0.5),
        "mix_post_g": gain(ks[15], (L, D_MODEL)),
        "ffn2_pre_g": gain(ks[16], (L, D_MODEL)),
        "ffn2_w_gate": nrm(ks[17], (L, D_MODEL, D_FF), D_MODEL ** -0.5),
        "ffn2_w_up": nrm(ks[18], (L, D_MODEL, D_FF), D_MODEL ** -0.5),
        "ffn2_w_down": nrm(ks[19], (L, D_FF, D_MODEL), D_FF ** -0.5),
        "ffn2_post_g": gain(ks[20], (L, D_MODEL)),
    }


def reference(x, ffn1_pre_g, ffn1_w_gate, ffn1_w_up, ffn1_w_down, ffn1_post_g,
              mix_pre_g, w_in, rel_bias, diff_lambda_q1, diff_lambda_k1, diff_lambda_q2,
              diff_lambda_k2, diff_subln_g, w_out, mix_post_g,
              ffn2_pre_g, ffn2_w_gate, ffn2_w_up, ffn2_w_down, ffn2_post_g):
    for l in range(DEPTH):
        lambda_init = 0.8 - 0.6 * math.exp(-0.3 * l)
        f = swiglu(rmsnorm(x, ffn1_pre_g[l]), ffn1_w_gate[l], ffn1_w_up[l], ffn1_w_down[l])
        x = x + FFN_RESIDUAL_WEIGHT * rmsnorm(f, ffn1_post_g[l])
        m = token_mixing(rmsnorm(x, mix_pre_g[l]), w_in[l], rel_bias[l],
                         diff_lambda_q1[l], diff_lambda_k1[l], diff_lambda_q2[l], diff_lambda_k2[l],
                         diff_subln_g[l], w_out[l], lambda_init)
        x = x + rmsnorm(m, mix_post_g[l])
        f = swiglu(rmsnorm(x, ffn2_pre_g[l]), ffn2_w_gate[l], ffn2_w_up[l], ffn2_w_down[l])
        x = x + FFN_RESIDUAL_WEIGHT * rmsnorm(f, ffn2_post_g[l])
    return x
```

```python
import math
from contextlib import ExitStack

import numpy as np
import concourse.bass as bass
import concourse.mybir as mybir
from concourse.bass_utils import run_bass_kernel_spmd

F32 = mybir.dt.float32
BF16 = mybir.dt.bfloat16
AF = mybir.ActivationFunctionType
ALU = mybir.AluOpType
AX = mybir.AxisListType

D = 1024
S = 2048
DFF = 2816
NCH = 8
NFF = 22
INC = 3072
EPS = 1e-6
NEG = -30000.0
N_CORES = 8
ENGNAME = {"pe": "tensor", "act": "scalar", "dve": "vector", "pool": "gpsimd", "sp": "sync"}


class Chan:
    def __init__(self, name):
        self.name = name
        self.count = 0
        self.sem = None
        self.cur = []

    def close(self):
        if self.cur:
            last = self.cur[-1]
            for o in self.cur:
                o.group_last = last
        self.cur = []


class Op:
    __slots__ = ("eng", "fn", "deps", "dma", "chan", "signal", "val", "sem", "group_last")

    def __init__(self, eng, fn, dma, chan):
        self.eng = eng
        self.fn = fn
        self.deps = set()
        self.dma = dma
        self.chan = chan
        self.signal = False
        self.val = 0
        self.sem = None
        self.group_last = None


class Prog:
    def __init__(self):
        self.ops = []
        self.lastw = {}
        self.rd_eng = {}
        self.rd_dma = {}
        self.last_op = {}
        self.alias_dmas = []
        self.chans = []

    def chan(self, name):
        c = Chan(name)
        self.chans.append(c)
        return c

    def _dep(self, op, p, kind):
        if p is op:
            return
        if p.fn is None and not p.dma:
            assert p.eng == op.eng
            return
        if (not p.dma) and (not op.dma) and p.eng == op.eng:
            if op.eng == "pe":
                return
            if kind != "raw":
                return
        op.deps.add(p)

    def add(self, eng, fn, reads=(), writes=(), dma=False, chan=None, alias=False):
        op = Op(eng, fn, dma, chan)
        for k in reads:
            w = self.lastw.get(k)
            if w is not None:
                self._dep(op, w, "raw")
            if isinstance(k, tuple) and k[0] == "ps":
                for r in self.rd_eng.get(k, {}).values():
                    if r.eng != eng:
                        self._dep(op, r, "raw")
        for k in writes:
            w = self.lastw.get(k)
            if w is not None:
                self._dep(op, w, "waw")
            for r in self.rd_eng.get(k, {}).values():
                self._dep(op, r, "war")
            for r in self.rd_dma.get(k, ()):
                self._dep(op, r, "war")
        for k in reads:
            if dma:
                self.rd_dma.setdefault(k, []).append(op)
            else:
                self.rd_eng.setdefault(k, {})[eng] = op
        for k in writes:
            self.lastw[k] = op
            self.rd_eng[k] = {}
            self.rd_dma[k] = []
        if dma:
            chan.cur.append(op)
            if alias:
                self.alias_dmas.append(op)
        elif fn is not None:
            self.last_op[eng] = op
        self.ops.append(op)
        return op

    def barrier(self, engines=("pe", "act", "dve", "pool", "sp")):
        lasts = [o for o in self.last_op.values()]
        al = list(self.alias_dmas)
        self.alias_dmas = []
        for e in engines:
            op = Op(e, None, False, None)
            for o in lasts:
                if o.eng != e:
                    op.deps.add(o)
            for o in al:
                op.deps.add(o)
            self.ops.append(op)

    def emit(self, nc, es):
        for c in self.chans:
            c.close()
        for op in self.ops:
            for d in op.deps:
                d.signal = True
        sems = {e: es.enter_context(nc.semaphore("s_" + e)) for e in ENGNAME}
        for c in self.chans:
            c.sem = es.enter_context(nc.semaphore("c_" + c.name))
            c.count = 0
        counts = {e: 0 for e in ENGNAME}
        for op in self.ops:
            if op.dma:
                op.chan.count += 16
                op.val = op.chan.count
                op.sem = op.chan.sem
            elif op.signal:
                counts[op.eng] += 1
                op.val = counts[op.eng]
                op.sem = sems[op.eng]
        block = es.enter_context(nc.Block())
        for e in ENGNAME:
            myops = [op for op in self.ops if op.eng == e]

            def body(eng, myops=myops):
                waited = {}
                for op in myops:
                    need = {}
                    for d in op.deps:
                        if d.dma:
                            g = d.group_last if d.group_last is not None else d
                            sem, val = g.sem, g.val
                        else:
                            sem, val = d.sem, d.val
                        k = id(sem)
                        if k not in need or need[k][1] < val:
                            need[k] = (sem, val)
                    for k, (sem, val) in need.items():
                        if waited.get(k, 0) < val:
                            eng.wait_ge(sem, val)
                            waited[k] = val
                    if op.fn is not None:
                        ins = op.fn(eng)
                        if op.dma:
                            ins.then_inc(op.sem, 16)
                        elif op.signal:
                            ins.then_inc(op.sem, 1)

            getattr(block, ENGNAME[e])(body)


def build(NS, L, do_ffn=True, do_attn=True):
    import os
    STAGE = int(os.environ.get("KSTAGE", "9"))
    nc = bass.Bass("TRN2", target_bir_lowering=False)
    es = ExitStack()
    P = Prog()

    def din(name, shape):
        return nc.dram_tensor(name, list(shape), F32, kind="ExternalInput").ap()

    x_d = din("x", [NS, S, D])
    out_d = nc.dram_tensor("out", [NS, S, D], F32, kind="ExternalOutput").ap()
    wg_d = [din("wg1", [L, D, DFF]), din("wg2", [L, D, DFF])]
    wu_d = [din("wu1", [L, D, DFF]), din("wu2", [L, D, DFF])]
    wd_d = [din("wd1", [L, DFF, D]), din("wd2", [L, DFF, D])]
    win_d = din("w_in", [L, D, INC])
    wout_d = din("w_out", [L, D, D])
    gains_d = din("gains", [128, L * 6 * NCH])
    band_d = din("band", [L, 3, 128, 1280])
    idf_d = din("idf", [128, 128])
    cbf_d = din("cbf", [128, 1152])
    br_d = din("br", [128, 2048])
    bl_d = din("bl", [128, 512])
    lamv_d = din("lamv", [1, L * 128])
    subln_d = din("subln", [1, L * 64])

    def sb(name, shape, dt):
        return es.enter_context(nc.sbuf_tensor(name, list(shape), dt))

    XT = sb("XT", [128, NCH, S], F32)
    HT = sb("HT", [128, NCH, S], BF16)
    R1 = sb("R1", [128, 16384], BF16)
    R2 = sb("R2", [128, 5248], F32)
    W = sb("W", [128, 2, 5632], BF16)
    BAND = sb("BAND", [128, 1280], BF16)
    IDF = sb("IDF", [128, 128], F32)
    CBF = sb("CBF", [128, 1152], BF16)
    BR = sb("BR", [128, 2048], BF16)
    BL = sb("BL", [128, 512], BF16)
    GAINS = sb("GAINS", [128, L * 6 * NCH], F32)
    GSC = sb("GSC", [128, L * 6 * NCH], F32)
    LAMV = sb("LAMV", [128, L * 128], F32)
    SUBLN = sb("SUBLN", [128, L * 64], F32)
    GSUB = sb("GSUB", [128, L * 64], F32)
    NEGLAM = sb("NEGLAM", [128, L], F32)
    LTMP = sb("LTMP", [128, 64], F32)
    LS = sb("LS", [128, 4], F32)
    SF = [sb(f"SF{i}", [128, 512], F32) for i in range(4)]
    SB_ = [sb(f"SB{i}", [128, 512], BF16) for i in range(8)]
    SM = sb("SM", [128, 64], F32)
    ps = [es.enter_context(nc.psum_tensor(f"ps{i}", [128, 512], F32)) for i in range(8)]

    IDB = CBF[:, 0:128]
    ONES = CBF[:, 128:256]
    NEGTRI = CBF[:, 256:384]
    NEGONES = CBF[:, 384:512]
    MASKC = CBF[:, 512:640]

    def DIAGB(h):
        return CBF[:, 640 + h * 128:640 + (h + 1) * 128]

    YT = R1[:, :].rearrange("p (c t) -> p c t", c=NCH)
    AT = R1[:, 0:NFF * 512].rearrange("p (c t) -> p c t", c=NFF)
    FT = R2[:, 0:4096].rearrange("p (c t) -> p c t", c=NCH)
    XIO = [R2[:, i * 1024:(i + 1) * 1024] for i in range(4)]
    R2B = R2[:, :].bitcast(BF16)
    QT = R2B[:, 0:2048]
    KT = R2B[:, 2048:4096]
    VA = R2B[:, 4096:4096 + 2176].rearrange("p (i n) -> p i n", i=16)
    YP = R2B[:, 6272:6272 + 2048].rearrange("p (i n) -> p i n", i=16)

    ch_const = P.chan("const")
    ch_w = [P.chan("w0"), P.chan("w1")]
    ch_xin = [P.chan("xin0"), P.chan("xin1")]
    ch_xout = [P.chan("xout0"), P.chan("xout1")]
    ch_band = P.chan("band")

    def mm(out, lhsT, rhs, start, stop, reads, writes, tp=None):
        kw = {}
        if tp is not None:
            kw["tile_position"] = tp
        P.add("pe", lambda e: e.matmul(out, lhsT=lhsT, rhs=rhs, start=start, stop=stop, **kw), reads, writes)

    def act(out, in_, func, reads, writes, **kw):
        P.add("act", lambda e: e.activation(out=out, in_=in_, func=func, **kw), reads, writes)

    def dma(eng, out, in_, reads, writes, chan, alias=False):
        P.add(eng, lambda e: e.dma_start(out=out, in_=in_), reads, writes, dma=True, chan=chan, alias=alias)

    wslot_ctr = [0]

    def wload(parts):
        si = wslot_ctr[0] % 2
        wslot_ctr[0] += 1
        P.add("pool", None, [], [("w", si, k) for k in range(3)])
        for idx, (col0, nch, n, src) in enumerate(parts):
            dst = W[:, si, col0:col0 + nch * n].rearrange("p (c n) -> p c n", n=n)
            dma("pool", dst, src, [], [("w", si, idx)], ch_w[si])
        ch_w[si].close()
        return si

    psrot = {}

    def rot(name, banks):
        i = psrot.get(name, 0)
        psrot[name] = i + 1
        return banks[i % len(banks)]

    sfrot = {}

    def rots(name, items):
        i = sfrot.get(name, 0)
        sfrot[name] = i + 1
        return items[i % len(items)]

    dma("sp", IDF[:], idf_d, [], ["IDF"], ch_const)
    dma("pool", CBF[:], cbf_d, [], ["CBF"], ch_const)
    dma("pool", BR[:], br_d, [], ["BR"], ch_const)
    dma("pool", BL[:], bl_d, [], ["BL"], ch_const)
    dma("sp", GAINS[:], gains_d, [], ["GAINS"], ch_const)
    dma("sp", LAMV[:], lamv_d.partition_broadcast(128), [], ["LAMV"], ch_const)
    dma("sp", SUBLN[:], subln_d.partition_broadcast(128), [], ["SUBLN"], ch_const)
    ch_const.close()
    P.add("dve", lambda e: e.tensor_scalar(out=GSC[:], in0=GAINS[:], scalar1=32.0, scalar2=None, op0=ALU.mult), ["GAINS"], ["GSC"])
    for l in range(L):
        lam_init = 0.8 - 0.6 * math.exp(-0.3 * l)
        b = l * 128
        P.add("dve", lambda e, b=b: e.tensor_tensor(out=LTMP[:, 0:32], in0=LAMV[:, b:b + 32], in1=LAMV[:, b + 32:b + 64], op=ALU.mult), ["LAMV"], ["LTMP0"])
        P.add("dve", lambda e, b=b: e.tensor_tensor(out=LTMP[:, 32:64], in0=LAMV[:, b + 64:b + 96], in1=LAMV[:, b + 96:b + 128], op=ALU.mult), ["LAMV"], ["LTMP1"])
        P.add("dve", lambda e: e.tensor_reduce(out=LS[:, 0:2], in_=LTMP[:, :].rearrange("p (a b) -> p a b", a=2), axis=AX.X, op=ALU.add), ["LTMP0", "LTMP1"], ["LS01"])
        act(LS[:, 2:4], LS[:, 0:2], AF.Exp, ["LS01"], ["LS23"])
        P.add("dve", lambda e: e.tensor_tensor(out=LS[:, 0:1], in0=LS[:, 3:4], in1=LS[:, 2:3], op=ALU.subtract), ["LS23"], ["LS01"])
        P.add("dve", lambda e, l=l, li=lam_init: e.tensor_scalar(out=NEGLAM[:, l:l + 1], in0=LS[:, 0:1], scalar1=-li, scalar2=None, op0=ALU.add), ["LS01"], [("NEGLAM", l)])
        P.add("dve", lambda e, l=l, li=lam_init: e.tensor_scalar(out=GSUB[:, l * 64:(l + 1) * 64], in0=SUBLN[:, l * 64:(l + 1) * 64], scalar1=1.0 - li, scalar2=None, op0=ALU.mult), ["SUBLN"], [("GSUB", l)])

    def gcol(l, gi, c):
        return (l * 6 + gi) * NCH + c

    def norm_stats(src_fn, src_keys, ss_bank, rs_tile, rs_key):
        for c in range(NCH):
            sq = rots("sq", [0, 1])
            act(SB_[sq][:], src_fn(c), AF.Square, [src_keys(c)], [("SB", sq)])
            mm(ps[ss_bank][:], ONES, SB_[sq][:], c == 0, c == NCH - 1, ["CBF", ("SB", sq)], [("ps", ss_bank)])
        finish_stats(ss_bank, rs_tile, rs_key)

    def finish_stats(ss_bank, rs_tile, rs_key):
        act(rs_tile[:], ps[ss_bank][:], AF.Ln, [("ps", ss_bank)], [rs_key], bias=1024.0 * EPS)
        act(rs_tile[:], rs_tile[:], AF.Exp, [rs_key], [rs_key], scale=-0.5)

    def load_x(s):
        for i in range(16):
            sl = i % 2
            dma("sp", XIO[sl], x_d[s, i * 128:(i + 1) * 128, :], [], [("xio", sl)], ch_xin[sl], alias=True)
            ch_xin[sl].close()
            for hb in range(2):
                bank = rot("xl", [1, 2, 3, 4])
                for cc in range(4):
                    c = hb * 4 + cc
                    P.add("pe", lambda e, bank=bank, cc=cc, c=c, sl=sl: e.transpose(ps[bank][:, cc * 128:(cc + 1) * 128], XIO[sl][:, c * 128:(c + 1) * 128], IDF[:]),
                          [("xio", sl), "IDF"], [("ps", bank)])
                dst = XT[:, hb * 4:hb * 4 + 4, i * 128:(i + 1) * 128]
                src = ps[bank][:, :].rearrange("p (c t) -> p c t", c=4)
                wk = [("XT", c, i // 4) for c in range(hb * 4, hb * 4 + 4)]
                if hb == 0:
                    P.add("dve", lambda e, dst=dst, src=src: e.tensor_copy(out=dst, in_=src), [("ps", bank)], wk)
                else:
                    P.add("act", lambda e, dst=dst, src=src: e.activation(out=dst, in_=src, func=AF.Copy), [("ps", bank)], wk)

    def store_x(s):
        last = []
        for i in range(16):
            sl = 2 + (i % 2)
            for hb in range(2):
                bank = rot("xl", [1, 2, 3, 4])
                for cc in range(4):
                    c = hb * 4 + cc
                    P.add("pe", lambda e, bank=bank, cc=cc, c=c, i=i: e.transpose(ps[bank][:, cc * 128:(cc + 1) * 128], XT[:, c, i * 128:(i + 1) * 128], IDF[:]),
                          [("XT", c, i // 4), "IDF"], [("ps", bank)])
                dst = XIO[sl][:, hb * 512:(hb + 1) * 512]
                if hb == 0:
                    P.add("dve", lambda e, dst=dst, bank=bank: e.tensor_copy(out=dst, in_=ps[bank][:]), [("ps", bank)], [("xio", sl, hb)])
                else:
                    P.add("act", lambda e, dst=dst, bank=bank: e.activation(out=dst, in_=ps[bank][:], func=AF.Copy), [("ps", bank)], [("xio", sl, hb)])
            if STAGE >= 4:
                dma(os.environ.get("KSTQ", "sp"), out_d[s, i * 128:(i + 1) * 128, :], XIO[sl], [("xio", sl, 0), ("xio", sl, 1)], [], ch_xout[i % 2], alias=True)
                ch_xout[i % 2].close()
                last.append(P.ops[-1])
        return last

    def post_residual(l, gi, t, factor, rs_key):
        rs2 = SF[1]
        for c in range(NCH):
            tmp = rots("tmp", [2, 3])
            col = gcol(l, gi, c)
            P.add("dve", lambda e, tmp=tmp, c=c, col=col: e.scalar_tensor_tensor(out=SF[tmp][:], in0=FT[:, c, :], scalar=GSC[:, col:col + 1], in1=rs2[:], op0=ALU.mult, op1=ALU.mult),
                  [("FT", c), "GSC", rs_key], [("SF", tmp)])
            xs = XT[:, c, t * 512:(t + 1) * 512]
            P.add("dve", lambda e, tmp=tmp, xs=xs: e.scalar_tensor_tensor(out=xs, in0=SF[tmp][:], scalar=factor, in1=xs, op0=ALU.mult, op1=ALU.add),
                  [("SF", tmp), ("XT", c, t)], [("XT", c, t)])

    def ffn_tile(l, which, t):
        gi_pre, gi_post = (0, 1) if which == 0 else (4, 5)
        tok = slice(t * 512, (t + 1) * 512)
        HTt = HT[:, :, 0:512]
        norm_stats(lambda c: XT[:, c, tok], lambda c: ("XT", c, t), 0, SF[0], "rs")
        for c in range(NCH):
            col = gcol(l, gi_pre, c)
            P.add("dve", lambda e, c=c, col=col: e.scalar_tensor_tensor(out=HTt[:, c, :], in0=XT[:, c, tok], scalar=GSC[:, col:col + 1], in1=SF[0][:], op0=ALU.mult, op1=ALU.mult),
                  [("XT", c, t), "GSC", "rs"], [("HT", c)])
        if STAGE < 6:
            return
        wgv = wg_d[which][l].rearrange("(c p) n -> p c n", p=128)
        wuv = wu_d[which][l].rearrange("(c p) n -> p c n", p=128)
        wdv = wd_d[which][l].rearrange("(c p) n -> p c n", p=128)
        hkeys = [("HT", c) for c in range(NCH)]
        for grp in range(11):
            si = wload([(0, NCH, 256, wgv[:, :, grp * 256:(grp + 1) * 256]), (2048, NCH, 256, wuv[:, :, grp * 256:(grp + 1) * 256])])
            Wg = W[:, si, 0:2048].rearrange("p (c n) -> p c n", n=256)
            Wu = W[:, si, 2048:4096].rearrange("p (c n) -> p c n", n=256)
            for cc in range(2):
                chunk = grp * 2 + cc
                bg = rot("g", [1, 2])
                bu = rot("u", [3, 4])
                for c in range(NCH):
                    mm(ps[bg][:], Wg[:, c, cc * 128:(cc + 1) * 128], HTt[:, c, :], c == 0, c == NCH - 1, [("w", si, 0), ("HT", c)], [("ps", bg)])
                for c in range(NCH):
                    mm(ps[bu][:], Wu[:, c, cc * 128:(cc + 1) * 128], HTt[:, c, :], c == 0, c == NCH - 1, [("w", si, 1), ("HT", c)], [("ps", bu)])
                sg = rots("tmp", [2, 3])
                act(SF[sg][:], ps[bg][:], AF.Silu, [("ps", bg)], [("SF", sg)])
                P.add("dve", lambda e, sg=sg, bu=bu, chunk=chunk: e.tensor_tensor(out=AT[:, chunk, :], in0=SF[sg][:], in1=ps[bu][:], op=ALU.mult),
                      [("SF", sg), ("ps", bu)], [("AT", chunk)])
        if STAGE < 7:
            return
        for grp in range(4):
            si = wload([(0, 8, 256, wdv[:, 0:8, grp * 256:(grp + 1) * 256]), (2048, 8, 256, wdv[:, 8:16, grp * 256:(grp + 1) * 256]),
                        (4096, 6, 256, wdv[:, 16:22, grp * 256:(grp + 1) * 256])])
            Wd = W[:, si, 0:5632].rearrange("p (c n) -> p c n", n=256)
            for cc in range(2):
                dch = grp * 2 + cc
                bf = rot("f", [5, 6])
                for c in range(NFF):
                    mm(ps[bf][:], Wd[:, c, cc * 128:(cc + 1) * 128], AT[:, c, :], c == 0, c == NFF - 1, [("w", si, c // 8), ("AT", c)], [("ps", bf)])
                P.add("dve", lambda e, dch=dch, bf=bf: e.tensor_copy(out=FT[:, dch, :], in_=ps[bf][:]), [("ps", bf)], [("FT", dch)])
                sq = rots("sq", [0, 1])
                act(SB_[sq][:], ps[bf][:], AF.Square, [("ps", bf)], [("SB", sq)])
                mm(ps[7][:], ONES, SB_[sq][:], dch == 0, dch == NCH - 1, ["CBF", ("SB", sq)], [("ps", 7)])
        if STAGE < 8:
            return
        finish_stats(7, SF[1], "rs2")
        post_residual(l, gi_post, t, 0.5, "rs2")

    def attn_phase(l, s):
        for t in range(4):
            tok = slice(t * 512, (t + 1) * 512)
            norm_stats(lambda c: XT[:, c, tok], lambda c: ("XT", c, t), 0, SF[0], "rs")
            for c in range(NCH):
                col = gcol(l, 2, c)
                P.add("dve", lambda e, c=c, col=col, tok=tok: e.scalar_tensor_tensor(out=HT[:, c, tok], in0=XT[:, c, tok], scalar=GSC[:, col:col + 1], in1=SF[0][:], op0=ALU.mult, op1=ALU.mult),
                      [("XT", c, t), "GSC", "rs"], [("HTf", c, t)])
        P.add("dve", lambda e: e.memset(VA[:, :, 64:65], 1.0), [], [("VA1", 0)])
        P.add("dve", lambda e: e.memset(VA[:, :, 132:133], 1.0), [], [("VA1", 1)])
        winv = win_d[l].rearrange("(c p) n -> p c n", p=128)
        for p in range(8):
            if p < 3:
                typ, q0, k0, v0, qscale = "A", p * 128, 384 + p * 128, 768 + p * 128, 0.125
            elif p < 5:
                typ, q0, k0, v0, qscale = "B", 1152 + (p - 3) * 128, 1408 + (p - 3) * 128, 1664 + (p - 3) * 128, 32.0 ** -0.5
            else:
                typ, q0, k0, v0, qscale = "C", 1920 + (p - 5) * 128, 2304 + (p - 5) * 128, 2688 + (p - 5) * 128, 0.125
            si = wload([(0, NCH, 128, winv[:, :, q0:q0 + 128]), (1024, NCH, 128, winv[:, :, k0:k0 + 128]), (2048, NCH, 128, winv[:, :, v0:v0 + 128])])
            Wq = W[:, si, 0:1024].rearrange("p (c n) -> p c n", n=128)
            Wk = W[:, si, 1024:2048].rearrange("p (c n) -> p c n", n=128)
            Wv = W[:, si, 2048:3072].rearrange("p (c n) -> p c n", n=128)
            if typ == "A":
                dma("pool", BAND[:], band_d[l, p], [], ["BAND"], ch_band)
                ch_band.close()
            for t in range(4):
                tok = slice(t * 512, (t + 1) * 512)
                bq = rot("ip", [1, 2])
                for c in range(NCH):
                    mm(ps[bq][:], Wq[:, c, :], HT[:, c, tok], c == 0, c == NCH - 1, [("w", si, 0), ("HTf", c, t)], [("ps", bq)])
                act(QT[:, tok], ps[bq][:], AF.Copy, [("ps", bq)], [("QT", t)], scale=qscale)
                bk = rot("ip", [1, 2])
                for c in range(NCH):
                    mm(ps[bk][:], Wk[:, c, :], HT[:, c, tok], c == 0, c == NCH - 1, [("w", si, 1), ("HTf", c, t)], [("ps", bk)])
                P.add("dve", lambda e, tok=tok, bk=bk: e.tensor_copy(out=KT[:, tok], in_=ps[bk][:]), [("ps", bk)], [("KT", t)])
            for i4 in range(4):
                bv = rot("ip", [1, 2])
                for ii in range(4):
                    i = i4 * 4 + ii
                    for c in range(NCH):
                        mm(ps[bv][:, ii * 128:(ii + 1) * 128], HT[:, c, i * 128:(i + 1) * 128], Wv[:, c, :], c == 0, c == NCH - 1,
                           [("w", si, 2), ("HTf", c, i // 4)], [("ps", bv)])
                src = ps[bv][:, :].rearrange("p (i n) -> p i n", i=4)
                P.add("dve", lambda e, i4=i4, src=src: e.tensor_copy(out=VA[:, i4 * 4:i4 * 4 + 4, 0:64], in_=src[:, :, 0:64]), [("ps", bv)], [("VA", i4, 0)])
                act(VA[:, i4 * 4:i4 * 4 + 4, 68:132], src[:, :, 64:128], AF.Copy, [("ps", bv)], [("VA", i4, 1)])
            HEADS = os.environ.get("KHEADS", "ABC")
            for hh in range(2):
                if typ not in HEADS:
                    continue
                if typ == "A":
                    head_A(l, p, hh)
                elif typ == "B":
                    head_B(l, (p - 3) * 2 + hh, hh)
                else:
                    head_C(l, hh)
            for half in range(2):
                bank = rot("ip", [1, 2])
                psb = ps[bank][:, :].bitcast(BF16)
                for ii in range(8):
                    i = half * 8 + ii
                    P.add("pe", lambda e, psb=psb, ii=ii, i=i: e.transpose(psb[:, ii * 128:(ii + 1) * 128], YP[:, i, :], IDB),
                          [("YP", i // 4, 0), ("YP", i // 4, 1), "CBF"], [("ps", bank)])
                if half == 0:
                    P.add("dve", lambda e, psb=psb, p=p: e.tensor_copy(out=YT[:, p, 0:1024], in_=psb[:, 0:1024]), [("ps", bank)], [("YT", p, 0), ("YT", p, 1)])
                else:
                    act(YT[:, p, 1024:2048], psb[:, 0:1024], AF.Copy, [("ps", bank)], [("YT", p, 2), ("YT", p, 3)])
        P.barrier(engines=("pe", "act", "dve"))
        woutv = wout_d[l].rearrange("(c p) n -> p c n", p=128)
        for t in range(4):
            tok = slice(t * 512, (t + 1) * 512)
            for grp in range(4):
                si = wload([(0, NCH, 256, woutv[:, :, grp * 256:(grp + 1) * 256])])
                Wo = W[:, si, 0:2048].rearrange("p (c n) -> p c n", n=256)
                for cc in range(2):
                    dch = grp * 2 + cc
                    bf = rot("f", [5, 6])
                    for c in range(NCH):
                        mm(ps[bf][:], Wo[:, c, cc * 128:(cc + 1) * 128], YT[:, c, tok], c == 0, c == NCH - 1, [("w", si, 0), ("YT", c, t)], [("ps", bf)])
                    P.add("dve", lambda e, dch=dch, bf=bf: e.tensor_copy(out=FT[:, dch, :], in_=ps[bf][:]), [("ps", bf)], [("FT", dch)])
                    sq = rots("sq", [0, 1])
                    act(SB_[sq][:], ps[bf][:], AF.Square, [("ps", bf)], [("SB", sq)])
                    mm(ps[7][:], ONES, SB_[sq][:], dch == 0, dch == NCH - 1, ["CBF", ("SB", sq)], [("ps", 7)])
            finish_stats(7, SF[1], "rs2")
            post_residual(l, 3, t, 1.0, "rs2")

    def pv(obank, ocol, ow, Ptile, blk, j, hh, vw, first, last):
        mm(ps[obank][:, ocol:ocol + ow], SB_[Ptile][:, blk * 128:(blk + 1) * 128], VA[:, j, hh * 68:hh * 68 + vw], first, last,
           [("SB", Ptile), ("VA", j // 4, hh), ("VA1", hh)], [("ps", obank)])

    def head_A(l, p, hh):
        pr = slice(hh * 64, hh * 64 + 64)
        for qb in range(4):
            i0 = 4 * qb
            first = True
            js = list(range(max(i0 - 4, 0), i0 + 4))
            for j in js:
                a = max(j, i0)
                b = min(j + 4, i0 + 3)
                lo, hi = (a - i0) * 128, (b - i0 + 1) * 128
                sbk = rot("s", [3, 4, 5, 6])
                mm(ps[sbk][:, lo:hi], KT[pr, j * 128:(j + 1) * 128], QT[pr, i0 * 128 + lo:i0 * 128 + hi], True, False,
                   [("KT", j // 4), ("QT", qb)], [("ps", sbk)])
                mm(ps[sbk][:, lo:hi], IDB, BAND[:, hh * 640 + (a - j) * 128:hh * 640 + (b - j + 1) * 128], False, True, ["CBF", "BAND"], [("ps", sbk)])
                Pt = rots("P", [5, 6, 7])
                act(SB_[Pt][:, lo:hi], ps[sbk][:, lo:hi], AF.Exp, [("ps", sbk)], [("SB", Pt)])
                for blk in range(a - i0, b - i0 + 1):
                    lastmm = (j == js[-1]) and (blk == b - i0)
                    pv(7, blk * 65, 65, Pt, blk, j, hh, 65, first, lastmm)
                    first = False
            O = ps[7][:, 0:260].rearrange("p (b n) -> p b n", b=4)
            P.add("dve", lambda e, O=O: e.reciprocal(out=SM[:, 0:4], in_=O[:, :, 64]), [("ps", 7)], ["SMr"])
            P.add("dve", lambda e, O=O, i0=i0, hh=hh: e.tensor_tensor(out=YP[:, i0:i0 + 4, hh * 64:(hh + 1) * 64], in0=O[:, :, 0:64],
                                                                       in1=SM[:, 0:4].unsqueeze(2).to_broadcast([128, 4, 64]), op=ALU.mult),
                  [("ps", 7), "SMr"], [("YP", qb, hh)])

    def head_B(l, hB, hh):
        for qb in range(4):
            for m in range(2):
                base = hh * 64 + m * 32
                pr = slice(base, base + 32)
                tp = (96, 0) if base == 96 else None
                obank = 7 if m == 0 else 0
                first = True
                nj = 4 * qb + 4
                for j in range(nj):
                    diag = j >= 4 * qb
                    lo = max(j - 4 * qb, 0) * 128
                    sbk = rot("s", [3, 4, 5, 6])
                    rk = [("KT", j // 4), ("QT", qb)]
                    lo2 = lo
                    if diag:
                        mm(ps[sbk][:, lo:lo + 128], KT[pr, j * 128:(j + 1) * 128], QT[pr, qb * 512 + lo:qb * 512 + lo + 128], True, False, rk, [("ps", sbk)], tp)
                        mm(ps[sbk][:, lo:lo + 128], IDB, DIAGB(hB), False, True, ["CBF"], [("ps", sbk)])
                        lo2 = lo + 128
                    if lo2 < 512:
                        c0 = qb * 512 + lo2 - 128 * j
                        mm(ps[sbk][:, lo2:512], KT[pr, j * 128:(j + 1) * 128], QT[pr, qb * 512 + lo2:qb * 512 + 512], not diag, False, rk, [("ps", sbk)], tp)
                        mm(ps[sbk][:, lo2:512], BL[:, hB * 128:(hB + 1) * 128], BR[:, c0:c0 + 512 - lo2], False, True, ["BL", "BR"], [("ps", sbk)])
                    Pt = rots("P", [5, 6, 7])
                    act(SB_[Pt][:, lo:512], ps[sbk][:, lo:512], AF.Exp, [("ps", sbk)], [("SB", Pt)])
                    for blk in range(lo // 128, 4):
                        lastmm = (j == nj - 1) and (blk == 3)
                        pv(obank, blk * 65, 65, Pt, blk, j, hh, 65, first, lastmm)
                        first = False
            if os.environ.get("KB", "2") == "1":
                continue
            O1 = ps[7][:, 0:260].rearrange("p (b n) -> p b n", b=4)
            O2 = ps[0][:, 0:260].rearrange("p (b n) -> p b n", b=4)
            T1 = SF[2][:, 0:256].rearrange("p (b n) -> p b n", b=4)
            T2 = SF[3][:, 0:256].rearrange("p (b n) -> p b n", b=4)
            P.add("dve", lambda e, O1=O1: e.reciprocal(out=SM[:, 0:4], in_=O1[:, :, 64]), [("ps", 7)], ["SMr"])
            P.add("dve", lambda e, O2=O2: e.reciprocal(out=SM[:, 4:8], in_=O2[:, :, 64]), [("ps", 0)], ["SMr2"])
            P.add("dve", lambda e, l=l: e.tensor_scalar(out=SM[:, 8:12], in0=SM[:, 4:8], scalar1=NEGLAM[:, l:l + 1], scalar2=None, op0=ALU.mult), ["SMr2", ("NEGLAM", l)], ["SMr2n"])
            P.add("dve", lambda e, O1=O1, T1=T1: e.tensor_tensor(out=T1, in0=O1[:, :, 0:64], in1=SM[:, 0:4].unsqueeze(2).to_broadcast([128, 4, 64]), op=ALU.mult),
                  [("ps", 7), "SMr"], [("SF", 2)])
            P.add("dve", lambda e, O2=O2, T2=T2: e.tensor_tensor(out=T2, in0=O2[:, :, 0:64], in1=SM[:, 8:12].unsqueeze(2).to_broadcast([128, 4, 64]), op=ALU.mult),
                  [("ps", 0), "SMr2n"], [("SF", 3)])
            P.add("dve", lambda e, T1=T1, T2=T2: e.tensor_tensor(out=T1, in0=T1, in1=T2, op=ALU.add), [("SF", 2), ("SF", 3)], [("SF", 2)])
            act(T2, T1, AF.Square, [("SF", 2)], [("SF", 3)])
            P.add("dve", lambda e, T2=T2: e.tensor_reduce(out=SM[:, 12:16], in_=T2, axis=AX.X, op=ALU.add), [("SF", 3)], ["SMss"])
            act(SM[:, 16:20], SM[:, 12:16], AF.Ln, ["SMss"], ["SMln"], scale=1.0 / 64.0, bias=EPS)
            act(SM[:, 20:24], SM[:, 16:20], AF.Exp, ["SMln"], ["SMrstd"], scale=-0.5)
            P.add("dve", lambda e, T1=T1: e.tensor_tensor(out=T1, in0=T1, in1=SM[:, 20:24].unsqueeze(2).to_broadcast([128, 4, 64]), op=ALU.mult), [("SF", 2), "SMrstd"], [("SF", 2)])
            P.add("dve", lambda e, T1=T1, qb=qb, hh=hh, l=l: e.tensor_tensor(out=YP[:, 4 * qb:4 * qb + 4, hh * 64:(hh + 1) * 64], in0=T1,
                                                                              in1=GSUB[:, l * 64:(l + 1) * 64].unsqueeze(1).to_broadcast([128, 4, 64]), op=ALU.mult),
                  [("SF", 2), ("GSUB", l)], [("YP", qb, hh)])

    def head_C(l, hh):
        pr = slice(hh * 64, hh * 64 + 64)
        ACC = 4
        for qb in range(4):
            P.add("dve", lambda e: e.memset(SB_[ACC][:], 0.0), [], [("SB", ACC)])
            first = True
            jtop = 4 * qb + 3
            for j in range(jtop, -1, -1):
                diag = j >= 4 * qb
                lo = max(j - 4 * qb, 0) * 128
                qc = slice(qb * 512 + lo, qb * 512 + 512)
                kc = slice(j * 128, (j + 1) * 128)
                rk = [("KT", j // 4), ("QT", qb)]
                zs = rot("s", [3, 4])
                mm(ps[zs][:, lo:512], KT[pr, kc], QT[pr, qc], True, not diag, rk, [("ps", zs)])
                if diag:
                    mm(ps[zs][:, lo:lo + 128], IDB, MASKC, False, True, ["CBF"], [("ps", zs)])
                et = rots("e", [2, 3])
                act(SF[et][:, lo:512], ps[zs][:, lo:512], AF.Exp, [("ps", zs)], [("SF", et)])
                Lt = rots("L", [2, 3])
                act(SB_[Lt][:, lo:512], SF[et][:, lo:512], AF.Ln, [("SF", et)], [("SB", Lt)], bias=1.0)
                za = rot("za", [5, 6])
                mm(ps[za][:, lo:512], KT[pr, kc], QT[pr, qc], True, False, rk, [("ps", za)])
                mm(ps[za][:, lo:512], NEGTRI, SB_[Lt][:, lo:512], False, False, ["CBF", ("SB", Lt)], [("ps", za)])
                if j < jtop:
                    mm(ps[za][:, lo:512], NEGONES, SB_[ACC][:, lo:512], False, not diag, ["CBF", ("SB", ACC)], [("ps", za)])
                if diag:
                    mm(ps[za][:, lo:lo + 128], IDB, MASKC, False, True, ["CBF"], [("ps", za)])
                Pt = rots("P", [5, 6, 7])
                act(SB_[Pt][:, lo:512], ps[za][:, lo:512], AF.Exp, [("ps", za)], [("SB", Pt)])
                if j > 0:
                    P.add("dve", lambda e, Lt=Lt, lo=lo: e.tensor_tensor(out=SB_[ACC][:, lo:512], in0=SB_[ACC][:, lo:512], in1=SB_[Lt][:, lo:512], op=ALU.add),
                          [("SB", ACC), ("SB", Lt)], [("SB", ACC)])
                for blk in range(lo // 128, 4):
                    lastmm = (j == 0) and (blk == 3)
                    pv(7, blk * 64, 64, Pt, blk, j, hh, 64, first, lastmm)
                    first = False
            O = ps[7][:, 0:256].rearrange("p (b n) -> p b n", b=4)
            P.add("dve", lambda e, O=O, qb=qb, hh=hh: e.tensor_copy(out=YP[:, 4 * qb:4 * qb + 4, hh * 64:(hh + 1) * 64], in_=O), [("ps", 7)], [("YP", qb, hh)])

    last_stores = []
    for s in range(NS):
        P.barrier()
        if STAGE >= 2:
            load_x(s)
        for l in range(L):
            if do_ffn:
                P.barrier(engines=("pe", "act", "dve"))
                for t in range(4):
                    ffn_tile(l, 0, t)
            if do_attn:
                P.barrier(engines=("pe", "act", "dve"))
                attn_phase(l, s)
            if do_ffn:
                P.barrier(engines=("pe", "act", "dve"))
                for t in range(4):
                    ffn_tile(l, 1, t)
        P.barrier(engines=("pe", "act", "dve"))
        if STAGE >= 3:
            last_stores = store_x(s)
    fin = Op("sp", None, False, None)
    for o in P.alias_dmas:
        fin.deps.add(o)
    P.ops.append(fin)
    P.emit(nc, es)
    es.close()
    return nc


def host_consts(L, rel_bias):
    k = np.arange(128)[:, None]
    q = np.arange(128)[None, :]
    idf = np.eye(128, dtype=np.float32)
    cbf = np.zeros((128, 1152), np.float32)
    cbf[:, 0:128] = idf
    cbf[:, 128:256] = 1.0
    cbf[:, 256:384] = -(k >= q).astype(np.float32)
    cbf[:, 384:512] = -1.0
    cbf[:, 512:640] = np.where(k < q, 0.0, NEG)
    slopes = [2.0 ** (-8.0 * (h + 1) / 4.0) for h in range(4)]
    for h in range(4):
        t = -slopes[h] * np.abs(q - k).astype(np.float32)
        t = np.where((k >= 64) & (q < 64), NEG, t)
        cbf[:, 640 + h * 128:640 + (h + 1) * 128] = t
    c = np.arange(2048)
    br = np.zeros((128, 2048), np.float32)
    br[0:3] = np.stack([(c // 128) * 128, c % 128, np.ones_like(c)]).astype(np.float32)
    bl = np.zeros((128, 512), np.float32)
    for h in range(4):
        bl[0, h * 128:(h + 1) * 128] = -slopes[h]
        bl[1, h * 128:(h + 1) * 128] = -slopes[h]
        bl[2, h * 128:(h + 1) * 128] = slopes[h] * np.arange(128)
    cc = np.arange(640)[None, :]
    r = cc // 128
    qq = cc % 128
    kk = np.arange(128)[:, None]
    rel = 128 * r + qq - kk
    idx = np.clip(rel, -128, 128) + 128
    masked = ((r == 0) & (kk >= 64) & (qq < 64)) | ((r == 4) & (kk < 64) & (qq >= 64))
    band = np.zeros((L, 3, 128, 1280), np.float32)
    for l in range(L):
        for p in range(3):
            for hh in range(2):
                g = rel_bias[l, 2 * p + hh][idx]
                band[l, p, :, hh * 640:(hh + 1) * 640] = np.where(masked, np.float32(NEG), g)
    return idf, cbf, br, bl, band


_CACHE = {}
NS_PER_LAUNCH = 4


def kernel(**inputs):
    L = 2
    NS = 4
    x = np.ascontiguousarray(inputs["x"], dtype=np.float32)
    gl = [inputs[k] for k in ("ffn1_pre_g", "ffn1_post_g", "mix_pre_g", "mix_post_g", "ffn2_pre_g", "ffn2_post_g")]
    gains = np.stack([np.asarray(g, np.float32) for g in gl], axis=1)
    gains = np.ascontiguousarray(gains.reshape(L, 6, NCH, 128).transpose(3, 0, 1, 2).reshape(128, L * 6 * NCH))
    idf, cbf, br, bl, band = host_consts(L, np.asarray(inputs["rel_bias"], np.float32))
    lamv = np.concatenate([np.asarray(inputs[k], np.float32) for k in ("diff_lambda_q1", "diff_lambda_k1", "diff_lambda_q2", "diff_lambda_k2")], axis=1)
    lamv = np.ascontiguousarray(lamv.reshape(1, L * 128))
    subln = np.ascontiguousarray(np.asarray(inputs["diff_subln_g"], np.float32).reshape(1, L * 64))
    common = {
        "wg1": inputs["ffn1_w_gate"], "wu1": inputs["ffn1_w_up"], "wd1": inputs["ffn1_w_down"],
        "wg2": inputs["ffn2_w_gate"], "wu2": inputs["ffn2_w_up"], "wd2": inputs["ffn2_w_down"],
        "w_in": inputs["w_in"], "w_out": inputs["w_out"], "gains": gains, "band": band,
        "idf": idf, "cbf": cbf, "br": br, "bl": bl, "lamv": lamv, "subln": subln,
    }
    common = {k: np.ascontiguousarray(np.asarray(v, np.float32)) for k, v in common.items()}
    NS_L = NS_PER_LAUNCH
    if "nc" not in _CACHE:
        _CACHE["nc"] = build(NS_L, L)
    nc = _CACHE["nc"]
    out = np.empty_like(x)
    for k in range(NS // NS_L):
        in_maps = []
        for c in range(N_CORES):
            m = dict(common)
            m["x"] = x[c * NS + k * NS_L:c * NS + (k + 1) * NS_L]
            in_maps.append(m)
        res = run_bass_kernel_spmd(nc, in_maps, core_ids=list(range(N_CORES)))
        for c in range(N_CORES):
            out[c * NS + k * NS_L:c * NS + (k + 1) * NS_L] = res.results[c]["out"]
    return out
```

```python
import math
from contextlib import ExitStack

import numpy as np
import concourse.bass as bass
import concourse.mybir as mybir
from concourse.bass_utils import run_bass_kernel_spmd

F32 = mybir.dt.float32
BF16 = mybir.dt.bfloat16
AF = mybir.ActivationFunctionType
ALU = mybir.AluOpType
AX = mybir.AxisListType

D = 1024
S = 2048
DFF = 2816
NCH = 8
NFF = 22
INC = 3072
EPS = 1e-6
NEG = -30000.0
N_CORES = 8
ENGNAME = {"pe": "tensor", "act": "scalar", "dve": "vector", "pool": "gpsimd", "sp": "sync"}


class Chan:
    def __init__(self, name):
        self.name = name
        self.count = 0
        self.sem = None
        self.cur = []

    def close(self):
        if self.cur:
            last = self.cur[-1]
            for o in self.cur:
                o.group_last = last
        self.cur = []


class Op:
    __slots__ = ("eng", "fn", "deps", "dma", "chan", "signal", "val", "sem", "group_last")

    def __init__(self, eng, fn, dma, chan):
        self.eng = eng
        self.fn = fn
        self.deps = set()
        self.dma = dma
        self.chan = chan
        self.signal = False
        self.val = 0
        self.sem = None
        self.group_last = None


class Prog:
    def __init__(self):
        self.ops = []
        self.lastw = {}
        self.rd_eng = {}
        self.rd_dma = {}
        self.last_op = {}
        self.alias_dmas = []
        self.chans = []

    def chan(self, name):
        c = Chan(name)
        self.chans.append(c)
        return c

    def _dep(self, op, p, kind):
        if p is op:
            return
        if p.fn is None and not p.dma:
            assert p.eng == op.eng
            return
        if (not p.dma) and (not op.dma) and p.eng == op.eng:
            if op.eng == "pe":
                return
            if kind != "raw":
                return
        op.deps.add(p)

    def add(self, eng, fn, reads=(), writes=(), dma=False, chan=None, alias=False):
        op = Op(eng, fn, dma, chan)
        for k in reads:
            w = self.lastw.get(k)
            if w is not None:
                self._dep(op, w, "raw")
            if isinstance(k, tuple) and k[0] == "ps":
                for r in self.rd_eng.get(k, {}).values():
                    if r.eng != eng:
                        self._dep(op, r, "raw")
        for k in writes:
            w = self.lastw.get(k)
            if w is not None:
                self._dep(op, w, "waw")
            for r in self.rd_eng.get(k, {}).values():
                self._dep(op, r, "war")
            for r in self.rd_dma.get(k, ()):
                self._dep(op, r, "war")
        for k in reads:
            if dma:
                self.rd_dma.setdefault(k, []).append(op)
            else:
                self.rd_eng.setdefault(k, {})[eng] = op
        for k in writes:
            self.lastw[k] = op
            self.rd_eng[k] = {}
            self.rd_dma[k] = []
        if dma:
            chan.cur.append(op)
            if alias:
                self.alias_dmas.append(op)
        elif fn is not None:
            self.last_op[eng] = op
        self.ops.append(op)
        return op

    def barrier(self, engines=("pe", "act", "dve", "pool", "sp")):
        lasts = [o for o in self.last_op.values()]
        al = list(self.alias_dmas)
        self.alias_dmas = []
        for e in engines:
            op = Op(e, None, False, None)
            for o in lasts:
                if o.eng != e:
                    op.deps.add(o)
            for o in al:
                op.deps.add(o)
            self.ops.append(op)

    def emit(self, nc, es):
        for c in self.chans:
            c.close()
        for op in self.ops:
            for d in op.deps:
                d.signal = True
        sems = {e: es.enter_context(nc.semaphore("s_" + e)) for e in ENGNAME}
        for c in self.chans:
            c.sem = es.enter_context(nc.semaphore("c_" + c.name))
            c.count = 0
        counts = {e: 0 for e in ENGNAME}
        for op in self.ops:
            if op.dma:
                op.chan.count += 16
                op.val = op.chan.count
                op.sem = op.chan.sem
            elif op.signal:
                counts[op.eng] += 1
                op.val = counts[op.eng]
                op.sem = sems[op.eng]
        block = es.enter_context(nc.Block())
        for e in ENGNAME:
            myops = [op for op in self.ops if op.eng == e]

            def body(eng, myops=myops):
                waited = {}
                for op in myops:
                    need = {}
                    for d in op.deps:
                        if d.dma:
                            g = d.group_last if d.group_last is not None else d
                            sem, val = g.sem, g.val
                        else:
                            sem, val = d.sem, d.val
                        k = id(sem)
                        if k not in need or need[k][1] < val:
                            need[k] = (sem, val)
                    for k, (sem, val) in need.items():
                        if waited.get(k, 0) < val:
                            eng.wait_ge(sem, val)
                            waited[k] = val
                    if op.fn is not None:
                        ins = op.fn(eng)
                        if op.dma:
                            ins.then_inc(op.sem, 16)
                        elif op.signal:
                            ins.then_inc(op.sem, 1)

            getattr(block, ENGNAME[e])(body)


def build(NS, L, do_ffn=True, do_attn=True):
    import os
    STAGE = int(os.environ.get("KSTAGE", "9"))
    nc = bass.Bass("TRN2", target_bir_lowering=False)
    es = ExitStack()
    P = Prog()

    def din(name, shape):
        return nc.dram_tensor(name, list(shape), F32, kind="ExternalInput").ap()

    x_d = din("x", [NS, S, D])
    out_d = nc.dram_tensor("out", [NS, S, D], F32, kind="ExternalOutput").ap()
    wg_d = [din("wg1", [L, D, DFF]), din("wg2", [L, D, DFF])]
    wu_d = [din("wu1", [L, D, DFF]), din("wu2", [L, D, DFF])]
    wd_d = [din("wd1", [L, DFF, D]), din("wd2", [L, DFF, D])]
    win_d = din("w_in", [L, D, INC])
    wout_d = din("w_out", [L, D, D])
    gains_d = din("gains", [128, L * 6 * NCH])
    band_d = din("band", [L, 3, 128, 1280])
    idf_d = din("idf", [128, 128])
    cbf_d = din("cbf", [128, 1152])
    br_d = din("br", [128, 2048])
    bl_d = din("bl", [128, 512])
    lamv_d = din("lamv", [1, L * 128])
    subln_d = din("subln", [1, L * 64])

    def sb(name, shape, dt):
        return es.enter_context(nc.sbuf_tensor(name, list(shape), dt))

    XT = sb("XT", [128, NCH, S], F32)
    HT = sb("HT", [128, NCH, S], BF16)
    R1 = sb("R1", [128, 16384], BF16)
    R2 = sb("R2", [128, 5248], F32)
    W = sb("W", [128, 2, 5632], BF16)
    BAND = sb("BAND", [128, 1280], BF16)
    IDF = sb("IDF", [128, 128], F32)
    CBF = sb("CBF", [128, 1152], BF16)
    BR = sb("BR", [128, 2048], BF16)
    BL = sb("BL", [128, 512], BF16)
    GAINS = sb("GAINS", [128, L * 6 * NCH], F32)
    GSC = sb("GSC", [128, L * 6 * NCH], F32)
    LAMV = sb("LAMV", [128, L * 128], F32)
    SUBLN = sb("SUBLN", [128, L * 64], F32)
    GSUB = sb("GSUB", [128, L * 64], F32)
    NEGLAM = sb("NEGLAM", [128, L], F32)
    LTMP = sb("LTMP", [128, 64], F32)
    LS = sb("LS", [128, 4], F32)
    SF = [sb(f"SF{i}", [128, 512], F32) for i in range(4)]
    SB_ = [sb(f"SB{i}", [128, 512], BF16) for i in range(8)]
    SM = sb("SM", [128, 64], F32)
    ps = [es.enter_context(nc.psum_tensor(f"ps{i}", [128, 512], F32)) for i in range(8)]

    IDB = CBF[:, 0:128]
    ONES = CBF[:, 128:256]
    NEGTRI = CBF[:, 256:384]
    NEGONES = CBF[:, 384:512]
    MASKC = CBF[:, 512:640]

    def DIAGB(h):
        return CBF[:, 640 + h * 128:640 + (h + 1) * 128]

    YT = R1[:, :].rearrange("p (c t) -> p c t", c=NCH)
    AT = R1[:, 0:NFF * 512].rearrange("p (c t) -> p c t", c=NFF)
    FT = R2[:, 0:4096].rearrange("p (c t) -> p c t", c=NCH)
    XIO = [R2[:, i * 1024:(i + 1) * 1024] for i in range(4)]
    R2B = R2[:, :].bitcast(BF16)
    QT = R2B[:, 0:2048]
    KT = R2B[:, 2048:4096]
    VA = R2B[:, 4096:4096 + 2176].rearrange("p (i n) -> p i n", i=16)
    YP = R2B[:, 6272:6272 + 2048].rearrange("p (i n) -> p i n", i=16)

    ch_const = P.chan("const")
    ch_w = [P.chan("w0"), P.chan("w1")]
    ch_xin = [P.chan("xin0"), P.chan("xin1")]
    ch_xout = [P.chan("xout0"), P.chan("xout1")]
    ch_band = P.chan("band")

    def mm(out, lhsT, rhs, start, stop, reads, writes, tp=None):
        kw = {}
        if tp is not None:
            kw["tile_position"] = tp
        P.add("pe", lambda e: e.matmul(out, lhsT=lhsT, rhs=rhs, start=start, stop=stop, **kw), reads, writes)

    def act(out, in_, func, reads, writes, **kw):
        P.add("act", lambda e: e.activation(out=out, in_=in_, func=func, **kw), reads, writes)

    def dma(eng, out, in_, reads, writes, chan, alias=False):
        P.add(eng, lambda e: e.dma_start(out=out, in_=in_), reads, writes, dma=True, chan=chan, alias=alias)

    wslot_ctr = [0]

    def wload(parts):
        si = wslot_ctr[0] % 2
        wslot_ctr[0] += 1
        P.add("pool", None, [], [("w", si, k) for k in range(3)])
        for idx, (col0, nch, n, src) in enumerate(parts):
            dst = W[:, si, col0:col0 + nch * n].rearrange("p (c n) -> p c n", n=n)
            dma("pool", dst, src, [], [("w", si, idx)], ch_w[si])
        ch_w[si].close()
        return si

    psrot = {}

    def rot(name, banks):
        i = psrot.get(name, 0)
        psrot[name] = i + 1
        return banks[i % len(banks)]

    sfrot = {}

    def rots(name, items):
        i = sfrot.get(name, 0)
        sfrot[name] = i + 1
        return items[i % len(items)]

    dma("sp", IDF[:], idf_d, [], ["IDF"], ch_const)
    dma("pool", CBF[:], cbf_d, [], ["CBF"], ch_const)
    dma("pool", BR[:], br_d, [], ["BR"], ch_const)
    dma("pool", BL[:], bl_d, [], ["BL"], ch_const)
    dma("sp", GAINS[:], gains_d, [], ["GAINS"], ch_const)
    dma("sp", LAMV[:], lamv_d.partition_broadcast(128), [], ["LAMV"], ch_const)
    dma("sp", SUBLN[:], subln_d.partition_broadcast(128), [], ["SUBLN"], ch_const)
    ch_const.close()
    P.add("dve", lambda e: e.tensor_scalar(out=GSC[:], in0=GAINS[:], scalar1=32.0, scalar2=None, op0=ALU.mult), ["GAINS"], ["GSC"])
    for l in range(L):
        lam_init = 0.8 - 0.6 * math.exp(-0.3 * l)
        b = l * 128
        P.add("dve", lambda e, b=b: e.tensor_tensor(out=LTMP[:, 0:32], in0=LAMV[:, b:b + 32], in1=LAMV[:, b + 32:b + 64], op=ALU.mult), ["LAMV"], ["LTMP0"])
        P.add("dve", lambda e, b=b: e.tensor_tensor(out=LTMP[:, 32:64], in0=LAMV[:, b + 64:b + 96], in1=LAMV[:, b + 96:b + 128], op=ALU.mult), ["LAMV"], ["LTMP1"])
        P.add("dve", lambda e: e.tensor_reduce(out=LS[:, 0:2], in_=LTMP[:, :].rearrange("p (a b) -> p a b", a=2), axis=AX.X, op=ALU.add), ["LTMP0", "LTMP1"], ["LS01"])
        act(LS[:, 2:4], LS[:, 0:2], AF.Exp, ["LS01"], ["LS23"])
        P.add("dve", lambda e: e.tensor_tensor(out=LS[:, 0:1], in0=LS[:, 3:4], in1=LS[:, 2:3], op=ALU.subtract), ["LS23"], ["LS01"])
        P.add("dve", lambda e, l=l, li=lam_init: e.tensor_scalar(out=NEGLAM[:, l:l + 1], in0=LS[:, 0:1], scalar1=-li, scalar2=None, op0=ALU.add), ["LS01"], [("NEGLAM", l)])
        P.add("dve", lambda e, l=l, li=lam_init: e.tensor_scalar(out=GSUB[:, l * 64:(l + 1) * 64], in0=SUBLN[:, l * 64:(l + 1) * 64], scalar1=1.0 - li, scalar2=None, op0=ALU.mult), ["SUBLN"], [("GSUB", l)])

    def gcol(l, gi, c):
        return (l * 6 + gi) * NCH + c

    def norm_stats(src_fn, src_keys, ss_bank, rs_tile, rs_key):
        for c in range(NCH):
            sq = rots("sq", [0, 1])
            act(SB_[sq][:], src_fn(c), AF.Square, [src_keys(c)], [("SB", sq)])
            mm(ps[ss_bank][:], ONES, SB_[sq][:], c == 0, c == NCH - 1, ["CBF", ("SB", sq)], [("ps", ss_bank)])
        finish_stats(ss_bank, rs_tile, rs_key)

    def finish_stats(ss_bank, rs_tile, rs_key):
        act(rs_tile[:], ps[ss_bank][:], AF.Ln, [("ps", ss_bank)], [rs_key], bias=1024.0 * EPS)
        act(rs_tile[:], rs_tile[:], AF.Exp, [rs_key], [rs_key], scale=-0.5)

    def load_x(s):
        for i in range(16):
            sl = i % 2
            dma("sp", XIO[sl], x_d[s, i * 128:(i + 1) * 128, :], [], [("xio", sl)], ch_xin[sl], alias=True)
            ch_xin[sl].close()
            for hb in range(2):
                bank = rot("xl", [1, 2, 3, 4])
                for cc in range(4):
                    c = hb * 4 + cc
                    P.add("pe", lambda e, bank=bank, cc=cc, c=c, sl=sl: e.transpose(ps[bank][:, cc * 128:(cc + 1) * 128], XIO[sl][:, c * 128:(c + 1) * 128], IDF[:]),
                          [("xio", sl), "IDF"], [("ps", bank)])
                dst = XT[:, hb * 4:hb * 4 + 4, i * 128:(i + 1) * 128]
                src = ps[bank][:, :].rearrange("p (c t) -> p c t", c=4)
                wk = [("XT", c, i // 4) for c in range(hb * 4, hb * 4 + 4)]
                if hb == 0:
                    P.add("dve", lambda e, dst=dst, src=src: e.tensor_copy(out=dst, in_=src), [("ps", bank)], wk)
                else:
                    P.add("act", lambda e, dst=dst, src=src: e.activation(out=dst, in_=src, func=AF.Copy), [("ps", bank)], wk)

    def store_x(s):
        last = []
        for i in range(16):
            sl = 2 + (i % 2)
            for hb in range(2):
                bank = rot("xl", [1, 2, 3, 4])
                for cc in range(4):
                    c = hb * 4 + cc
                    P.add("pe", lambda e, bank=bank, cc=cc, c=c, i=i: e.transpose(ps[bank][:, cc * 128:(cc + 1) * 128], XT[:, c, i * 128:(i + 1) * 128], IDF[:]),
                          [("XT", c, i // 4), "IDF"], [("ps", bank)])
                dst = XIO[sl][:, hb * 512:(hb + 1) * 512]
                if hb == 0:
                    P.add("dve", lambda e, dst=dst, bank=bank: e.tensor_copy(out=dst, in_=ps[bank][:]), [("ps", bank)], [("xio", sl, hb)])
                else:
                    P.add("act", lambda e, dst=dst, bank=bank: e.activation(out=dst, in_=ps[bank][:], func=AF.Copy), [("ps", bank)], [("xio", sl, hb)])
            if STAGE >= 4:
                dma(os.environ.get("KSTQ", "sp"), out_d[s, i * 128:(i + 1) * 128, :], XIO[sl], [("xio", sl, 0), ("xio", sl, 1)], [], ch_xout[i % 2], alias=True)
                ch_xout[i % 2].close()
                last.append(P.ops[-1])
        return last

    def post_residual(l, gi, t, factor, rs_key):
        rs2 = SF[1]
        for c in range(NCH):
            tmp = rots("tmp", [2, 3])
            col = gcol(l, gi, c)
            P.add("dve", lambda e, tmp=tmp, c=c, col=col: e.scalar_tensor_tensor(out=SF[tmp][:], in0=FT[:, c, :], scalar=GSC[:, col:col + 1], in1=rs2[:], op0=ALU.mult, op1=ALU.mult),
                  [("FT", c), "GSC", rs_key], [("SF", tmp)])
            xs = XT[:, c, t * 512:(t + 1) * 512]
            P.add("dve", lambda e, tmp=tmp, xs=xs: e.scalar_tensor_tensor(out=xs, in0=SF[tmp][:], scalar=factor, in1=xs, op0=ALU.mult, op1=ALU.add),
                  [("SF", tmp), ("XT", c, t)], [("XT", c, t)])

    def ffn_tile(l, which, t):
        gi_pre, gi_post = (0, 1) if which == 0 else (4, 5)
        tok = slice(t * 512, (t + 1) * 512)
        HTt = HT[:, :, 0:512]
        norm_stats(lambda c: XT[:, c, tok], lambda c: ("XT", c, t), 0, SF[0], ("SF", 0))
        for c in range(NCH):
            col = gcol(l, gi_pre, c)
            P.add("dve", lambda e, c=c, col=col: e.scalar_tensor_tensor(out=HTt[:, c, :], in0=XT[:, c, tok], scalar=GSC[:, col:col + 1], in1=SF[0][:], op0=ALU.mult, op1=ALU.mult),
                  [("XT", c, t), "GSC", ("SF", 0)], [("HT", c)])
        if STAGE < 6:
            return
        wgv = wg_d[which][l].rearrange("(c p) n -> p c n", p=128)
        wuv = wu_d[which][l].rearrange("(c p) n -> p c n", p=128)
        wdv = wd_d[which][l].rearrange("(c p) n -> p c n", p=128)
        hkeys = [("HT", c) for c in range(NCH)]
        for grp in range(11):
            si = wload([(0, NCH, 256, wgv[:, :, grp * 256:(grp + 1) * 256]), (2048, NCH, 256, wuv[:, :, grp * 256:(grp + 1) * 256])])
            Wg = W[:, si, 0:2048].rearrange("p (c n) -> p c n", n=256)
            Wu = W[:, si, 2048:4096].rearrange("p (c n) -> p c n", n=256)
            for cc in range(2):
                chunk = grp * 2 + cc
                bg = rot("g", [1, 2])
                bu = rot("u", [3, 4])
                for c in range(NCH):
                    mm(ps[bg][:], Wg[:, c, cc * 128:(cc + 1) * 128], HTt[:, c, :], c == 0, c == NCH - 1, [("w", si, 0), ("HT", c)], [("ps", bg)])
                for c in range(NCH):
                    mm(ps[bu][:], Wu[:, c, cc * 128:(cc + 1) * 128], HTt[:, c, :], c == 0, c == NCH - 1, [("w", si, 1), ("HT", c)], [("ps", bu)])
                sg = rots("tmp", [2, 3])
                act(SF[sg][:], ps[bg][:], AF.Silu, [("ps", bg)], [("SF", sg)])
                P.add("dve", lambda e, sg=sg, bu=bu, chunk=chunk: e.tensor_tensor(out=AT[:, chunk, :], in0=SF[sg][:], in1=ps[bu][:], op=ALU.mult),
                      [("SF", sg), ("ps", bu)], [("AT", chunk)])
        if STAGE < 7:
            return
        for grp in range(4):
            si = wload([(0, 8, 256, wdv[:, 0:8, grp * 256:(grp + 1) * 256]), (2048, 8, 256, wdv[:, 8:16, grp * 256:(grp + 1) * 256]),
                        (4096, 6, 256, wdv[:, 16:22, grp * 256:(grp + 1) * 256])])
            Wd = W[:, si, 0:5632].rearrange("p (c n) -> p c n", n=256)
            for cc in range(2):
                dch = grp * 2 + cc
                bf = rot("f", [5, 6])
                for c in range(NFF):
                    mm(ps[bf][:], Wd[:, c, cc * 128:(cc + 1) * 128], AT[:, c, :], c == 0, c == NFF - 1, [("w", si, c // 8), ("AT", c)], [("ps", bf)])
                P.add("dve", lambda e, dch=dch, bf=bf: e.tensor_copy(out=FT[:, dch, :], in_=ps[bf][:]), [("ps", bf)], [("FT", dch)])
                sq = rots("sq", [0, 1])
                act(SB_[sq][:], ps[bf][:], AF.Square, [("ps", bf)], [("SB", sq)])
                mm(ps[7][:], ONES, SB_[sq][:], dch == 0, dch == NCH - 1, ["CBF", ("SB", sq)], [("ps", 7)])
        if STAGE < 8:
            return
        finish_stats(7, SF[1], ("SF", 1))
        post_residual(l, gi_post, t, 0.5, ("SF", 1))

    def attn_phase(l, s):
        for t in range(4):
            tok = slice(t * 512, (t + 1) * 512)
            norm_stats(lambda c: XT[:, c, tok], lambda c: ("XT", c, t), 0, SF[0], ("SF", 0))
            for c in range(NCH):
                col = gcol(l, 2, c)
                P.add("dve", lambda e, c=c, col=col, tok=tok: e.scalar_tensor_tensor(out=HT[:, c, tok], in0=XT[:, c, tok], scalar=GSC[:, col:col + 1], in1=SF[0][:], op0=ALU.mult, op1=ALU.mult),
                      [("XT", c, t), "GSC", ("SF", 0)], [("HTf", c, t)])
        P.add("dve", lambda e: e.memset(VA[:, :, 64:65], 1.0), [], [("VA1", 0)])
        P.add("dve", lambda e: e.memset(VA[:, :, 132:133], 1.0), [], [("VA1", 1)])
        winv = win_d[l].rearrange("(c p) n -> p c n", p=128)
        for p in range(8):
            if p < 3:
                typ, q0, k0, v0, qscale = "A", p * 128, 384 + p * 128, 768 + p * 128, 0.125
            elif p < 5:
                typ, q0, k0, v0, qscale = "B", 1152 + (p - 3) * 128, 1408 + (p - 3) * 128, 1664 + (p - 3) * 128, 32.0 ** -0.5
            else:
                typ, q0, k0, v0, qscale = "C", 1920 + (p - 5) * 128, 2304 + (p - 5) * 128, 2688 + (p - 5) * 128, 0.125
            si = wload([(0, NCH, 128, winv[:, :, q0:q0 + 128]), (1024, NCH, 128, winv[:, :, k0:k0 + 128]), (2048, NCH, 128, winv[:, :, v0:v0 + 128])])
            Wq = W[:, si, 0:1024].rearrange("p (c n) -> p c n", n=128)
            Wk = W[:, si, 1024:2048].rearrange("p (c n) -> p c n", n=128)
            Wv = W[:, si, 2048:3072].rearrange("p (c n) -> p c n", n=128)
            if typ == "A":
                dma("pool", BAND[:], band_d[l, p], [], ["BAND"], ch_band)
                ch_band.close()
            for t in range(4):
                tok = slice(t * 512, (t + 1) * 512)
                bq = rot("ip", [1, 2])
                for c in range(NCH):
                    mm(ps[bq][:], Wq[:, c, :], HT[:, c, tok], c == 0, c == NCH - 1, [("w", si, 0), ("HTf", c, t)], [("ps", bq)])
                act(QT[:, tok], ps[bq][:], AF.Copy, [("ps", bq)], [("QT", t)], scale=qscale)
                bk = rot("ip", [1, 2])
                for c in range(NCH):
                    mm(ps[bk][:], Wk[:, c, :], HT[:, c, tok], c == 0, c == NCH - 1, [("w", si, 1), ("HTf", c, t)], [("ps", bk)])
                P.add("dve", lambda e, tok=tok, bk=bk: e.tensor_copy(out=KT[:, tok], in_=ps[bk][:]), [("ps", bk)], [("KT", t)])
            for i4 in range(4):
                bv = rot("ip", [1, 2])
                for ii in range(4):
                    i = i4 * 4 + ii
                    for c in range(NCH):
                        mm(ps[bv][:, ii * 128:(ii + 1) * 128], HT[:, c, i * 128:(i + 1) * 128], Wv[:, c, :], c == 0, c == NCH - 1,
                           [("w", si, 2), ("HTf", c, i // 4)], [("ps", bv)])
                src = ps[bv][:, :].rearrange("p (i n) -> p i n", i=4)
                P.add("dve", lambda e, i4=i4, src=src: e.tensor_copy(out=VA[:, i4 * 4:i4 * 4 + 4, 0:64], in_=src[:, :, 0:64]), [("ps", bv)], [("VA", i4, 0)])
                act(VA[:, i4 * 4:i4 * 4 + 4, 68:132], src[:, :, 64:128], AF.Copy, [("ps", bv)], [("VA", i4, 1)])
            HEADS = os.environ.get("KHEADS", "ABC")
            for hh in range(2):
                if typ not in HEADS:
                    continue
                if typ == "A":
                    head_A(l, p, hh)
                elif typ == "B":
                    head_B(l, (p - 3) * 2 + hh, hh)
                else:
                    head_C(l, hh)
            for half in range(2):
                bank = rot("ip", [1, 2])
                psb = ps[bank][:, :].bitcast(BF16)
                for ii in range(8):
                    i = half * 8 + ii
                    P.add("pe", lambda e, psb=psb, ii=ii, i=i: e.transpose(psb[:, ii * 128:(ii + 1) * 128], YP[:, i, :], IDB),
                          [("YP", i // 4, 0), ("YP", i // 4, 1), "CBF"], [("ps", bank)])
                if half == 0:
                    P.add("dve", lambda e, psb=psb, p=p: e.tensor_copy(out=YT[:, p, 0:1024], in_=psb[:, 0:1024]), [("ps", bank)], [("YT", p, 0), ("YT", p, 1)])
                else:
                    act(YT[:, p, 1024:2048], psb[:, 0:1024], AF.Copy, [("ps", bank)], [("YT", p, 2), ("YT", p, 3)])
        P.barrier(engines=("pe", "act", "dve"))
        woutv = wout_d[l].rearrange("(c p) n -> p c n", p=128)
        for t in range(4):
            tok = slice(t * 512, (t + 1) * 512)
            for grp in range(4):
                si = wload([(0, NCH, 256, woutv[:, :, grp * 256:(grp + 1) * 256])])
                Wo = W[:, si, 0:2048].rearrange("p (c n) -> p c n", n=256)
                for cc in range(2):
                    dch = grp * 2 + cc
                    bf = rot("f", [5, 6])
                    for c in range(NCH):
                        mm(ps[bf][:], Wo[:, c, cc * 128:(cc + 1) * 128], YT[:, c, tok], c == 0, c == NCH - 1, [("w", si, 0), ("YT", c, t)], [("ps", bf)])
                    P.add("dve", lambda e, dch=dch, bf=bf: e.tensor_copy(out=FT[:, dch, :], in_=ps[bf][:]), [("ps", bf)], [("FT", dch)])
                    sq = rots("sq", [0, 1])
                    act(SB_[sq][:], ps[bf][:], AF.Square, [("ps", bf)], [("SB", sq)])
                    mm(ps[7][:], ONES, SB_[sq][:], dch == 0, dch == NCH - 1, ["CBF", ("SB", sq)], [("ps", 7)])
            finish_stats(7, SF[1], ("SF", 1))
            post_residual(l, 3, t, 1.0, ("SF", 1))

    def pv(obank, ocol, ow, Ptile, blk, j, hh, vw, first, last):
        mm(ps[obank][:, ocol:ocol + ow], SB_[Ptile][:, blk * 128:(blk + 1) * 128], VA[:, j, hh * 68:hh * 68 + vw], first, last,
           [("SB", Ptile), ("VA", j // 4, hh), ("VA1", hh)], [("ps", obank)])

    def pipeline(tiles, stA, stB, depth=2):
        n = len(tiles)
        for idx in range(n + depth):
            if idx < n:
                stA(tiles[idx])
            if idx - depth >= 0:
                stB(tiles[idx - depth])

    def head_A(l, p, hh):
        pr = slice(hh * 64, hh * 64 + 64)
        tiles = []
        for qb in range(4):
            i0 = 4 * qb
            js = list(range(max(i0 - 4, 0), i0 + 4))
            for j in js:
                a = max(j, i0)
                b = min(j + 4, i0 + 3)
                tiles.append(dict(qb=qb, i0=i0, j=j, a=a, b=b, lo=(a - i0) * 128, hi=(b - i0 + 1) * 128, first=(j == js[0]), last=(j == js[-1])))

        def stA(t):
            j, i0, lo, hi, a, b = t["j"], t["i0"], t["lo"], t["hi"], t["a"], t["b"]
            sbk = rot("s", [3, 4, 5, 6])
            mm(ps[sbk][:, lo:hi], KT[pr, j * 128:(j + 1) * 128], QT[pr, i0 * 128 + lo:i0 * 128 + hi], True, False,
               [("KT", j // 4), ("QT", t["qb"])], [("ps", sbk)])
            mm(ps[sbk][:, lo:hi], IDB, BAND[:, hh * 640 + (a - j) * 128:hh * 640 + (b - j + 1) * 128], False, True, ["CBF", "BAND"], [("ps", sbk)])
            Pt = rots("P", [5, 6, 7])
            act(SB_[Pt][:, lo:hi], ps[sbk][:, lo:hi], AF.Exp, [("ps", sbk)], [("SB", Pt)])
            t["Pt"] = Pt

        def stB(t):
            j, i0, a, b, qb = t["j"], t["i0"], t["a"], t["b"], t["qb"]
            ob = (7, 0)[qb % 2]
            for blk in range(a - i0, b - i0 + 1):
                pv(ob, blk * 65, 65, t["Pt"], blk, j, hh, 65, t["first"] and blk == a - i0, t["last"] and blk == b - i0)
            if t["last"]:
                O = ps[ob][:, 0:260].rearrange("p (b n) -> p b n", b=4)
                P.add("dve", lambda e, O=O: e.reciprocal(out=SM[:, 0:4], in_=O[:, :, 64]), [("ps", ob)], ["SMr"])
                P.add("dve", lambda e, O=O, i0=i0: e.tensor_tensor(out=YP[:, i0:i0 + 4, hh * 64:(hh + 1) * 64], in0=O[:, :, 0:64],
                                                                    in1=SM[:, 0:4].unsqueeze(2).to_broadcast([128, 4, 64]), op=ALU.mult),
                      [("ps", ob), "SMr"], [("YP", qb, hh)])

        pipeline(tiles, stA, stB)

    def head_B(l, hB, hh):
        tiles = []
        for qb in range(4):
            nj = 4 * qb + 4
            for m in range(2):
                for j in range(nj):
                    tiles.append(dict(qb=qb, m=m, j=j, first=(j == 0), last=(j == nj - 1)))

        def stA(t):
            qb, m, j = t["qb"], t["m"], t["j"]
            base = hh * 64 + m * 32
            pr = slice(base, base + 32)
            tp = (96, 0) if base == 96 else None
            diag = j >= 4 * qb
            lo = max(j - 4 * qb, 0) * 128
            sbk = rot("s", [3, 4, 5, 6])
            rk = [("KT", j // 4), ("QT", qb)]
            lo2 = lo
            if diag:
                mm(ps[sbk][:, lo:lo + 128], KT[pr, j * 128:(j + 1) * 128], QT[pr, qb * 512 + lo:qb * 512 + lo + 128], True, False, rk, [("ps", sbk)], tp)
                mm(ps[sbk][:, lo:lo + 128], IDB, DIAGB(hB), False, True, ["CBF"], [("ps", sbk)])
                lo2 = lo + 128
            if lo2 < 512:
                c0 = qb * 512 + lo2 - 128 * j
                mm(ps[sbk][:, lo2:512], KT[pr, j * 128:(j + 1) * 128], QT[pr, qb * 512 + lo2:qb * 512 + 512], not diag, False, rk, [("ps", sbk)], tp)
                mm(ps[sbk][:, lo2:512], BL[:, hB * 128:(hB + 1) * 128], BR[:, c0:c0 + 512 - lo2], False, True, ["BL", "BR"], [("ps", sbk)])
            Pt = rots("P", [5, 6, 7])
            act(SB_[Pt][:, lo:512], ps[sbk][:, lo:512], AF.Exp, [("ps", sbk)], [("SB", Pt)])
            t["Pt"] = Pt
            t["lo"] = lo

        def stB(t):
            qb, m, j, lo = t["qb"], t["m"], t["j"], t["lo"]
            obank = 7 if m == 0 else 0
            for blk in range(lo // 128, 4):
                pv(obank, blk * 65, 65, t["Pt"], blk, j, hh, 65, t["first"] and blk == lo // 128, t["last"] and blk == 3)
            if not (t["last"] and m == 1):
                return
            O1 = ps[7][:, 0:260].rearrange("p (b n) -> p b n", b=4)
            O2 = ps[0][:, 0:260].rearrange("p (b n) -> p b n", b=4)
            T1 = SF[0][:, 0:256].rearrange("p (b n) -> p b n", b=4)
            T2 = SF[1][:, 0:256].rearrange("p (b n) -> p b n", b=4)
            P.add("dve", lambda e, O1=O1: e.reciprocal(out=SM[:, 0:4], in_=O1[:, :, 64]), [("ps", 7)], ["SMr"])
            P.add("dve", lambda e, O2=O2: e.reciprocal(out=SM[:, 4:8], in_=O2[:, :, 64]), [("ps", 0)], ["SMr2"])
            P.add("dve", lambda e: e.tensor_scalar(out=SM[:, 8:12], in0=SM[:, 4:8], scalar1=NEGLAM[:, l:l + 1], scalar2=None, op0=ALU.mult), ["SMr2", ("NEGLAM", l)], ["SMr2n"])
            P.add("dve", lambda e, O1=O1, T1=T1: e.tensor_tensor(out=T1, in0=O1[:, :, 0:64], in1=SM[:, 0:4].unsqueeze(2).to_broadcast([128, 4, 64]), op=ALU.mult),
                  [("ps", 7), "SMr"], [("SF", 0)])
            P.add("dve", lambda e, O2=O2, T2=T2: e.tensor_tensor(out=T2, in0=O2[:, :, 0:64], in1=SM[:, 8:12].unsqueeze(2).to_broadcast([128, 4, 64]), op=ALU.mult),
                  [("ps", 0), "SMr2n"], [("SF", 1)])
            P.add("dve", lambda e, T1=T1, T2=T2: e.tensor_tensor(out=T1, in0=T1, in1=T2, op=ALU.add), [("SF", 0), ("SF", 1)], [("SF", 0)])
            act(T2, T1, AF.Square, [("SF", 0)], [("SF", 1)])
            P.add("dve", lambda e, T2=T2: e.tensor_reduce(out=SM[:, 12:16], in_=T2, axis=AX.X, op=ALU.add), [("SF", 1)], ["SMss"])
            act(SM[:, 16:20], SM[:, 12:16], AF.Ln, ["SMss"], ["SMln"], scale=1.0 / 64.0, bias=EPS)
            act(SM[:, 20:24], SM[:, 16:20], AF.Exp, ["SMln"], ["SMrstd"], scale=-0.5)
            P.add("dve", lambda e, T1=T1: e.tensor_tensor(out=T1, in0=T1, in1=SM[:, 20:24].unsqueeze(2).to_broadcast([128, 4, 64]), op=ALU.mult), [("SF", 0), "SMrstd"], [("SF", 0)])
            P.add("dve", lambda e, T1=T1, qb=qb: e.tensor_tensor(out=YP[:, 4 * qb:4 * qb + 4, hh * 64:(hh + 1) * 64], in0=T1,
                                                                 in1=GSUB[:, l * 64:(l + 1) * 64].unsqueeze(1).to_broadcast([128, 4, 64]), op=ALU.mult),
                  [("SF", 0), ("GSUB", l)], [("YP", qb, hh)])

        pipeline(tiles, stA, stB)

    def head_C(l, hh):
        pr = slice(hh * 64, hh * 64 + 64)
        ACCS = [4, 0]
        tiles = []
        for qb in range(4):
            jtop = 4 * qb + 3
            for j in range(jtop, -1, -1):
                tiles.append(dict(qb=qb, j=j, jtop=jtop, diag=(j >= 4 * qb), lo=max(j - 4 * qb, 0) * 128))
        accstate = [0]

        def stA(t):
            qb, j, lo, diag = t["qb"], t["j"], t["lo"], t["diag"]
            qc = slice(qb * 512 + lo, qb * 512 + 512)
            kc = slice(j * 128, (j + 1) * 128)
            rk = [("KT", j // 4), ("QT", qb)]
            zs = rot("zs", [3, 4, 1])
            mm(ps[zs][:, lo:512], KT[pr, kc], QT[pr, qc], True, not diag, rk, [("ps", zs)])
            if diag:
                mm(ps[zs][:, lo:lo + 128], IDB, MASKC, False, True, ["CBF"], [("ps", zs)])
            et = rots("e", [2, 3])
            act(SF[et][:, lo:512], ps[zs][:, lo:512], AF.Exp, [("ps", zs)], [("SF", et)])
            Lt = rots("L", [1, 2, 3])
            act(SB_[Lt][:, lo:512], SF[et][:, lo:512], AF.Ln, [("SF", et)], [("SB", Lt)], bias=1.0)
            t["Lt"] = Lt

        def stB(t):
            qb, j, lo, diag, jtop, Lt = t["qb"], t["j"], t["lo"], t["diag"], t["jtop"], t["Lt"]
            qc = slice(qb * 512 + lo, qb * 512 + 512)
            kc = slice(j * 128, (j + 1) * 128)
            rk = [("KT", j // 4), ("QT", qb)]
            ob = (7, 0)[qb % 2]
            if j == jtop:
                for a_ in ACCS:
                    P.add("dve", lambda e, a_=a_: e.memset(SB_[a_][:], 0.0), [], [("SB", a_)])
            cur = ACCS[accstate[0] % 2]
            nxt = ACCS[(accstate[0] + 1) % 2]
            za = rot("za", [5, 6])
            mm(ps[za][:, lo:512], KT[pr, kc], QT[pr, qc], True, False, rk, [("ps", za)])
            mm(ps[za][:, lo:512], NEGTRI, SB_[Lt][:, lo:512], False, False, ["CBF", ("SB", Lt)], [("ps", za)])
            if j < jtop:
                mm(ps[za][:, lo:512], NEGONES, SB_[cur][:, lo:512], False, not diag, ["CBF", ("SB", cur)], [("ps", za)])
            if diag:
                mm(ps[za][:, lo:lo + 128], IDB, MASKC, False, True, ["CBF"], [("ps", za)])
            Pt = rots("P", [5, 6, 7])
            act(SB_[Pt][:, lo:512], ps[za][:, lo:512], AF.Exp, [("ps", za)], [("SB", Pt)])
            if j > 0:
                P.add("dve", lambda e, cur=cur, nxt=nxt: e.tensor_tensor(out=SB_[nxt][:, lo:512], in0=SB_[cur][:, lo:512], in1=SB_[Lt][:, lo:512], op=ALU.add),
                      [("SB", cur), ("SB", Lt)], [("SB", nxt)])
                accstate[0] += 1
            for blk in range(lo // 128, 4):
                pv(ob, blk * 64, 64, Pt, blk, j, hh, 64, (j == jtop) and blk == lo // 128, (j == 0) and (blk == 3))
            if j == 0:
                O = ps[ob][:, 0:256].rearrange("p (b n) -> p b n", b=4)
                P.add("dve", lambda e, O=O: e.tensor_copy(out=YP[:, 4 * qb:4 * qb + 4, hh * 64:(hh + 1) * 64], in_=O), [("ps", ob)], [("YP", qb, hh)])

        pipeline(tiles, stA, stB)


    last_stores = []
    for s in range(NS):
        P.barrier()
        if STAGE >= 2:
            load_x(s)
        for l in range(L):
            if do_ffn:
                P.barrier(engines=("pe", "act", "dve"))
                for t in range(4):
                    ffn_tile(l, 0, t)
            if do_attn:
                P.barrier(engines=("pe", "act", "dve"))
                attn_phase(l, s)
            if do_ffn:
                P.barrier(engines=("pe", "act", "dve"))
                for t in range(4):
                    ffn_tile(l, 1, t)
        P.barrier(engines=("pe", "act", "dve"))
        if STAGE >= 3:
            last_stores = store_x(s)
    fin = Op("sp", None, False, None)
    for o in P.alias_dmas:
        fin.deps.add(o)
    P.ops.append(fin)
    P.emit(nc, es)
    es.close()
    return nc


def host_consts(L, rel_bias):
    k = np.arange(128)[:, None]
    q = np.arange(128)[None, :]
    idf = np.eye(128, dtype=np.float32)
    cbf = np.zeros((128, 1152), np.float32)
    cbf[:, 0:128] = idf
    cbf[:, 128:256] = 1.0
    cbf[:, 256:384] = -(k >= q).astype(np.float32)
    cbf[:, 384:512] = -1.0
    cbf[:, 512:640] = np.where(k < q, 0.0, NEG)
    slopes = [2.0 ** (-8.0 * (h + 1) / 4.0) for h in range(4)]
    for h in range(4):
        t = -slopes[h] * np.abs(q - k).astype(np.float32)
        t = np.where((k >= 64) & (q < 64), NEG, t)
        cbf[:, 640 + h * 128:640 + (h + 1) * 128] = t
    c = np.arange(2048)
    br = np.zeros((128, 2048), np.float32)
    br[0:3] = np.stack([(c // 128) * 128, c % 128, np.ones_like(c)]).astype(np.float32)
    bl = np.zeros((128, 512), np.float32)
    for h in range(4):
        bl[0, h * 128:(h + 1) * 128] = -slopes[h]
        bl[1, h * 128:(h + 1) * 128] = -slopes[h]
        bl[2, h * 128:(h + 1) * 128] = slopes[h] * np.arange(128)
    cc = np.arange(640)[None, :]
    r = cc // 128
    qq = cc % 128
    kk = np.arange(128)[:, None]
    rel = 128 * r + qq - kk
    idx = np.clip(rel, -128, 128) + 128
    masked = ((r == 0) & (kk >= 64) & (qq < 64)) | ((r == 4) & (kk < 64) & (qq >= 64))
    band = np.zeros((L, 3, 128, 1280), np.float32)
    for l in range(L):
        for p in range(3):
            for hh in range(2):
                g = rel_bias[l, 2 * p + hh][idx]
                band[l, p, :, hh * 640:(hh + 1) * 640] = np.where(masked, np.float32(NEG), g)
    return idf, cbf, br, bl, band


_CACHE = {}
NS_PER_LAUNCH = 4


def kernel(**inputs):
    L = 2
    NS = 4
    x = np.ascontiguousarray(inputs["x"], dtype=np.float32)
    gl = [inputs[k] for k in ("ffn1_pre_g", "ffn1_post_g", "mix_pre_g", "mix_post_g", "ffn2_pre_g", "ffn2_post_g")]
    gains = np.stack([np.asarray(g, np.float32) for g in gl], axis=1)
    gains = np.ascontiguousarray(gains.reshape(L, 6, NCH, 128).transpose(3, 0, 1, 2).reshape(128, L * 6 * NCH))
    idf, cbf, br, bl, band = host_consts(L, np.asarray(inputs["rel_bias"], np.float32))
    lamv = np.concatenate([np.asarray(inputs[k], np.float32) for k in ("diff_lambda_q1", "diff_lambda_k1", "diff_lambda_q2", "diff_lambda_k2")], axis=1)
    lamv = np.ascontiguousarray(lamv.reshape(1, L * 128))
    subln = np.ascontiguousarray(np.asarray(inputs["diff_subln_g"], np.float32).reshape(1, L * 64))
    common = {
        "wg1": inputs["ffn1_w_gate"], "wu1": inputs["ffn1_w_up"], "wd1": inputs["ffn1_w_down"],
        "wg2": inputs["ffn2_w_gate"], "wu2": inputs["ffn2_w_up"], "wd2": inputs["ffn2_w_down"],
        "w_in": inputs["w_in"], "w_out": inputs["w_out"], "gains": gains, "band": band,
        "idf": idf, "cbf": cbf, "br": br, "bl": bl, "lamv": lamv, "subln": subln,
    }
    common = {k: np.ascontiguousarray(np.asarray(v, np.float32)) for k, v in common.items()}
    NS_L = NS_PER_LAUNCH
    if "nc" not in _CACHE:
        _CACHE["nc"] = build(NS_L, L)
    nc = _CACHE["nc"]
    out = np.empty_like(x)
    for k in range(NS // NS_L):
        in_maps = []
        for c in range(N_CORES):
            m = dict(common)
            m["x"] = x[c * NS + k * NS_L:c * NS + (k + 1) * NS_L]
            in_maps.append(m)
        res = run_bass_kernel_spmd(nc, in_maps, core_ids=list(range(N_CORES)))
        for c in range(N_CORES):
            out[c * NS + k * NS_L:c * NS + (k + 1) * NS_L] = res.results[c]["out"]
    return out
```

```python
import math
from contextlib import ExitStack

import numpy as np
import concourse.bass as bass
import concourse.mybir as mybir
from concourse.bass_utils import run_bass_kernel_spmd

F32 = mybir.dt.float32
BF16 = mybir.dt.bfloat16
AF = mybir.ActivationFunctionType
ALU = mybir.AluOpType
AX = mybir.AxisListType

D = 1024
S = 2048
DFF = 2816
NCH = 8
NFF = 22
INC = 3072
EPS = 1e-6
NEG = -30000.0
N_CORES = 8
ENGNAME = {"pe": "tensor", "act": "scalar", "dve": "vector", "pool": "gpsimd", "sp": "sync"}


class Chan:
    def __init__(self, name):
        self.name = name
        self.count = 0
        self.sem = None
        self.cur = []

    def close(self):
        if self.cur:
            last = self.cur[-1]
            for o in self.cur:
                o.group_last = last
        self.cur = []


class Op:
    __slots__ = ("eng", "fn", "deps", "dma", "chan", "signal", "val", "sem", "group_last")

    def __init__(self, eng, fn, dma, chan):
        self.eng = eng
        self.fn = fn
        self.deps = set()
        self.dma = dma
        self.chan = chan
        self.signal = False
        self.val = 0
        self.sem = None
        self.group_last = None


class Prog:
    def __init__(self):
        self.ops = []
        self.lastw = {}
        self.rd_eng = {}
        self.rd_dma = {}
        self.last_op = {}
        self.alias_dmas = []
        self.chans = []

    def chan(self, name):
        c = Chan(name)
        self.chans.append(c)
        return c

    def _dep(self, op, p, kind):
        if p is op:
            return
        if p.fn is None and not p.dma:
            assert p.eng == op.eng
            return
        if (not p.dma) and (not op.dma) and p.eng == op.eng:
            if op.eng == "pe":
                return
            if kind != "raw":
                return
        op.deps.add(p)

    def add(self, eng, fn, reads=(), writes=(), dma=False, chan=None, alias=False):
        op = Op(eng, fn, dma, chan)
        for k in reads:
            w = self.lastw.get(k)
            if w is not None:
                self._dep(op, w, "raw")
            if isinstance(k, tuple) and k[0] == "ps":
                for r in self.rd_eng.get(k, {}).values():
                    if r.eng != eng:
                        self._dep(op, r, "raw")
        for k in writes:
            w = self.lastw.get(k)
            if w is not None:
                self._dep(op, w, "waw")
            for r in self.rd_eng.get(k, {}).values():
                self._dep(op, r, "war")
            for r in self.rd_dma.get(k, ()):
                self._dep(op, r, "war")
        for k in reads:
            if dma:
                self.rd_dma.setdefault(k, []).append(op)
            else:
                self.rd_eng.setdefault(k, {})[eng] = op
        for k in writes:
            self.lastw[k] = op
            self.rd_eng[k] = {}
            self.rd_dma[k] = []
        if dma:
            chan.cur.append(op)
            if alias:
                self.alias_dmas.append(op)
        elif fn is not None:
            self.last_op[eng] = op
        self.ops.append(op)
        return op

    def barrier(self, engines=("pe", "act", "dve", "pool", "sp")):
        lasts = [o for o in self.last_op.values()]
        al = list(self.alias_dmas)
        self.alias_dmas = []
        for e in engines:
            op = Op(e, None, False, None)
            for o in lasts:
                if o.eng != e:
                    op.deps.add(o)
            for o in al:
                op.deps.add(o)
            self.ops.append(op)

    def emit(self, nc, es):
        for c in self.chans:
            c.close()
        for op in self.ops:
            for d in op.deps:
                d.signal = True
        sems = {e: es.enter_context(nc.semaphore("s_" + e)) for e in ENGNAME}
        for c in self.chans:
            c.sem = es.enter_context(nc.semaphore("c_" + c.name))
            c.count = 0
        counts = {e: 0 for e in ENGNAME}
        for op in self.ops:
            if op.dma:
                op.chan.count += 16
                op.val = op.chan.count
                op.sem = op.chan.sem
            elif op.signal:
                counts[op.eng] += 1
                op.val = counts[op.eng]
                op.sem = sems[op.eng]
        block = es.enter_context(nc.Block())
        for e in ENGNAME:
            myops = [op for op in self.ops if op.eng == e]

            def body(eng, myops=myops):
                waited = {}
                for op in myops:
                    need = {}
                    for d in op.deps:
                        if d.dma:
                            g = d.group_last if d.group_last is not None else d
                            sem, val = g.sem, g.val
                        else:
                            sem, val = d.sem, d.val
                        k = id(sem)
                        if k not in need or need[k][1] < val:
                            need[k] = (sem, val)
                    for k, (sem, val) in need.items():
                        if waited.get(k, 0) < val:
                            eng.wait_ge(sem, val)
                            waited[k] = val
                    if op.fn is not None:
                        ins = op.fn(eng)
                        if op.dma:
                            ins.then_inc(op.sem, 16)
                        elif op.signal:
                            ins.then_inc(op.sem, 1)

            getattr(block, ENGNAME[e])(body)


def build(NS, L, do_ffn=True, do_attn=True):
    import os
    STAGE = int(os.environ.get("KSTAGE", "9"))
    nc = bass.Bass("TRN2", target_bir_lowering=False)
    es = ExitStack()
    P = Prog()

    def din(name, shape):
        return nc.dram_tensor(name, list(shape), F32, kind="ExternalInput").ap()

    x_d = din("x", [NS, S, D])
    out_d = nc.dram_tensor("out", [NS, S, D], F32, kind="ExternalOutput").ap()
    wg_d = [din("wg1", [L, D, DFF]), din("wg2", [L, D, DFF])]
    wu_d = [din("wu1", [L, D, DFF]), din("wu2", [L, D, DFF])]
    wd_d = [din("wd1", [L, DFF, D]), din("wd2", [L, DFF, D])]
    win_d = din("w_in", [L, D, INC])
    wout_d = din("w_out", [L, D, D])
    gains_d = din("gains", [128, L * 6 * NCH])
    band_d = din("band", [L, 3, 128, 1280])
    idf_d = din("idf", [128, 128])
    cbf_d = din("cbf", [128, 1152])
    br_d = din("br", [128, 2048])
    bl_d = din("bl", [128, 512])
    lamv_d = din("lamv", [1, L * 128])
    subln_d = din("subln", [1, L * 64])

    def sb(name, shape, dt):
        return es.enter_context(nc.sbuf_tensor(name, list(shape), dt))

    XT = sb("XT", [128, NCH, S], F32)
    HTFLAT = sb("HT", [128, NCH * S], BF16)
    HT = HTFLAT[:, :].rearrange("p (c t) -> p c t", c=NCH)
    HTB = [HTFLAT[:, i * 4096:(i + 1) * 4096].rearrange("p (c t) -> p c t", c=NCH) for i in range(2)]
    R1 = sb("R1", [128, 16384], BF16)
    R2 = sb("R2", [128, 5248], F32)
    W = sb("W", [128, 2, 5632], BF16)
    BAND = sb("BAND", [128, 1280], BF16)
    IDF = sb("IDF", [128, 128], F32)
    CBF = sb("CBF", [128, 1152], BF16)
    BR = sb("BR", [128, 2048], BF16)
    BL = sb("BL", [128, 512], BF16)
    GAINS = sb("GAINS", [128, L * 6 * NCH], F32)
    GSC = sb("GSC", [128, L * 6 * NCH], F32)
    LAMV = sb("LAMV", [128, L * 128], F32)
    SUBLN = sb("SUBLN", [128, L * 64], F32)
    GSUB = sb("GSUB", [128, L * 64], F32)
    NEGLAM = sb("NEGLAM", [128, L], F32)
    LTMP = sb("LTMP", [128, 64], F32)
    LS = sb("LS", [128, 4], F32)
    SF = [sb(f"SF{i}", [128, 512], F32) for i in range(4)]
    SB_ = [sb(f"SB{i}", [128, 512], BF16) for i in range(8)]
    SM = sb("SM", [128, 64], F32)
    ps = [es.enter_context(nc.psum_tensor(f"ps{i}", [128, 512], F32)) for i in range(8)]

    IDB = CBF[:, 0:128]
    ONES = CBF[:, 128:256]
    NEGTRI = CBF[:, 256:384]
    NEGONES = CBF[:, 384:512]
    MASKC = CBF[:, 512:640]

    def DIAGB(h):
        return CBF[:, 640 + h * 128:640 + (h + 1) * 128]

    YT = R1[:, :].rearrange("p (c t) -> p c t", c=NCH)
    AT = R1[:, 0:NFF * 512].rearrange("p (c t) -> p c t", c=NFF)
    FT = R2[:, 0:4096].rearrange("p (c t) -> p c t", c=NCH)
    XIO = [R2[:, i * 1024:(i + 1) * 1024] for i in range(4)]
    R2B = R2[:, :].bitcast(BF16)
    QT = R2B[:, 0:2048]
    KT = R2B[:, 2048:4096]
    VA = R2B[:, 4096:4096 + 2176].rearrange("p (i n) -> p i n", i=16)
    YP = R2B[:, 6272:6272 + 2048].rearrange("p (i n) -> p i n", i=16)

    ch_const = P.chan("const")
    ch_w = [P.chan("w0"), P.chan("w1"), P.chan("w2")]
    ch_xin = [P.chan("xin0"), P.chan("xin1")]
    ch_xout = [P.chan("xout0"), P.chan("xout1")]
    ch_band = P.chan("band")

    def mm(out, lhsT, rhs, start, stop, reads, writes, tp=None):
        kw = {}
        if tp is not None:
            kw["tile_position"] = tp
        P.add("pe", lambda e: e.matmul(out, lhsT=lhsT, rhs=rhs, start=start, stop=stop, **kw), reads, writes)

    def act(out, in_, func, reads, writes, **kw):
        P.add("act", lambda e: e.activation(out=out, in_=in_, func=func, **kw), reads, writes)

    def dma(eng, out, in_, reads, writes, chan, alias=False):
        P.add(eng, lambda e: e.dma_start(out=out, in_=in_), reads, writes, dma=True, chan=chan, alias=alias)

    wslot_ctr = [0]

    def WS(si):
        return W[:, si, :] if si < 2 else HTFLAT[:, 8192:8192 + 5632]

    def wload(parts, nslots=2):
        si = wslot_ctr[0] % nslots
        wslot_ctr[0] += 1
        P.add("pool", None, [], [("w", si, k) for k in range(3)])
        for idx, (col0, nch, n, src) in enumerate(parts):
            dst = WS(si)[:, col0:col0 + nch * n].rearrange("p (c n) -> p c n", n=n)
            wk = [("w", si, idx)]
            if si == 2 and idx == 0:
                wk += [("HTf", c, t) for c in range(4, 7) for t in range(4)]
            dma("pool", dst, src, [], wk, ch_w[si])
        ch_w[si].close()
        return si

    psrot = {}

    def rot(name, banks):
        i = psrot.get(name, 0)
        psrot[name] = i + 1
        return banks[i % len(banks)]

    sfrot = {}

    def rots(name, items):
        i = sfrot.get(name, 0)
        sfrot[name] = i + 1
        return items[i % len(items)]

    dma("sp", IDF[:], idf_d, [], ["IDF"], ch_const)
    dma("pool", CBF[:], cbf_d, [], ["CBF"], ch_const)
    dma("pool", BR[:], br_d, [], ["BR"], ch_const)
    dma("pool", BL[:], bl_d, [], ["BL"], ch_const)
    dma("sp", GAINS[:], gains_d, [], ["GAINS"], ch_const)
    dma("sp", LAMV[:], lamv_d.partition_broadcast(128), [], ["LAMV"], ch_const)
    dma("sp", SUBLN[:], subln_d.partition_broadcast(128), [], ["SUBLN"], ch_const)
    ch_const.close()
    P.add("dve", lambda e: e.tensor_scalar(out=GSC[:], in0=GAINS[:], scalar1=32.0, scalar2=None, op0=ALU.mult), ["GAINS"], ["GSC"])
    for l in range(L):
        lam_init = 0.8 - 0.6 * math.exp(-0.3 * l)
        b = l * 128
        P.add("dve", lambda e, b=b: e.tensor_tensor(out=LTMP[:, 0:32], in0=LAMV[:, b:b + 32], in1=LAMV[:, b + 32:b + 64], op=ALU.mult), ["LAMV"], ["LTMP0"])
        P.add("dve", lambda e, b=b: e.tensor_tensor(out=LTMP[:, 32:64], in0=LAMV[:, b + 64:b + 96], in1=LAMV[:, b + 96:b + 128], op=ALU.mult), ["LAMV"], ["LTMP1"])
        P.add("dve", lambda e: e.tensor_reduce(out=LS[:, 0:2], in_=LTMP[:, :].rearrange("p (a b) -> p a b", a=2), axis=AX.X, op=ALU.add), ["LTMP0", "LTMP1"], ["LS01"])
        act(LS[:, 2:4], LS[:, 0:2], AF.Exp, ["LS01"], ["LS23"])
        P.add("dve", lambda e: e.tensor_tensor(out=LS[:, 0:1], in0=LS[:, 3:4], in1=LS[:, 2:3], op=ALU.subtract), ["LS23"], ["LS01"])
        P.add("dve", lambda e, l=l, li=lam_init: e.tensor_scalar(out=NEGLAM[:, l:l + 1], in0=LS[:, 0:1], scalar1=-li, scalar2=None, op0=ALU.add), ["LS01"], [("NEGLAM", l)])
        P.add("dve", lambda e, l=l, li=lam_init: e.tensor_scalar(out=GSUB[:, l * 64:(l + 1) * 64], in0=SUBLN[:, l * 64:(l + 1) * 64], scalar1=1.0 - li, scalar2=None, op0=ALU.mult), ["SUBLN"], [("GSUB", l)])

    def gcol(l, gi, c):
        return (l * 6 + gi) * NCH + c

    def norm_stats(src_fn, src_keys, ss_bank, rs_tile, rs_key):
        for c in range(NCH):
            sq = rots("sq", [0, 1])
            act(SB_[sq][:], src_fn(c), AF.Square, [src_keys(c)], [("SB", sq)])
            mm(ps[ss_bank][:], ONES, SB_[sq][:], c == 0, c == NCH - 1, ["CBF", ("SB", sq)], [("ps", ss_bank)])
        finish_stats(ss_bank, rs_tile, rs_key)

    def finish_stats(ss_bank, rs_tile, rs_key):
        act(rs_tile[:], ps[ss_bank][:], AF.Ln, [("ps", ss_bank)], [rs_key], bias=1024.0 * EPS)
        act(rs_tile[:], rs_tile[:], AF.Exp, [rs_key], [rs_key], scale=-0.5)

    def load_x(s):
        for i in range(16):
            sl = i % 2
            dma("sp", XIO[sl], x_d[s, i * 128:(i + 1) * 128, :], [], [("xio", sl)], ch_xin[sl], alias=True)
            ch_xin[sl].close()
            for hb in range(2):
                bank = rot("xl", [1, 2, 3, 4])
                for cc in range(4):
                    c = hb * 4 + cc
                    P.add("pe", lambda e, bank=bank, cc=cc, c=c, sl=sl: e.transpose(ps[bank][:, cc * 128:(cc + 1) * 128], XIO[sl][:, c * 128:(c + 1) * 128], IDF[:]),
                          [("xio", sl), "IDF"], [("ps", bank)])
                dst = XT[:, hb * 4:hb * 4 + 4, i * 128:(i + 1) * 128]
                src = ps[bank][:, :].rearrange("p (c t) -> p c t", c=4)
                wk = [("XT", c, i // 4) for c in range(hb * 4, hb * 4 + 4)]
                if hb == 0:
                    P.add("dve", lambda e, dst=dst, src=src: e.tensor_copy(out=dst, in_=src), [("ps", bank)], wk)
                else:
                    P.add("act", lambda e, dst=dst, src=src: e.activation(out=dst, in_=src, func=AF.Copy), [("ps", bank)], wk)

    def store_x(s):
        last = []
        for i in range(16):
            sl = 2 + (i % 2)
            for hb in range(2):
                bank = rot("xl", [1, 2, 3, 4])
                for cc in range(4):
                    c = hb * 4 + cc
                    P.add("pe", lambda e, bank=bank, cc=cc, c=c, i=i: e.transpose(ps[bank][:, cc * 128:(cc + 1) * 128], XT[:, c, i * 128:(i + 1) * 128], IDF[:]),
                          [("XT", c, i // 4), "IDF"], [("ps", bank)])
                dst = XIO[sl][:, hb * 512:(hb + 1) * 512]
                if hb == 0:
                    P.add("dve", lambda e, dst=dst, bank=bank: e.tensor_copy(out=dst, in_=ps[bank][:]), [("ps", bank)], [("xio", sl, hb)])
                else:
                    P.add("act", lambda e, dst=dst, bank=bank: e.activation(out=dst, in_=ps[bank][:], func=AF.Copy), [("ps", bank)], [("xio", sl, hb)])
            if STAGE >= 4:
                dma(os.environ.get("KSTQ", "sp"), out_d[s, i * 128:(i + 1) * 128, :], XIO[sl], [("xio", sl, 0), ("xio", sl, 1)], [], ch_xout[i % 2], alias=True)
                ch_xout[i % 2].close()
                last.append(P.ops[-1])
        return last

    def post_residual(l, gi, t, factor, rs_key):
        rs2 = SF[1]
        for c in range(NCH):
            tmp = rots("tmp", [2, 3])
            col = gcol(l, gi, c)
            P.add("dve", lambda e, tmp=tmp, c=c, col=col: e.scalar_tensor_tensor(out=SF[tmp][:], in0=FT[:, c, :], scalar=GSC[:, col:col + 1], in1=rs2[:], op0=ALU.mult, op1=ALU.mult),
                  [("FT", c), "GSC", rs_key], [("SF", tmp)])
            xs = XT[:, c, t * 512:(t + 1) * 512]
            P.add("dve", lambda e, tmp=tmp, xs=xs: e.scalar_tensor_tensor(out=xs, in0=SF[tmp][:], scalar=factor, in1=xs, op0=ALU.mult, op1=ALU.add),
                  [("SF", tmp), ("XT", c, t)], [("XT", c, t)])

    def ffn_phase(l, which):
        gi_pre, gi_post = (0, 1) if which == 0 else (4, 5)
        wgv = wg_d[which][l].rearrange("(c p) n -> p c n", p=128)
        wuv = wu_d[which][l].rearrange("(c p) n -> p c n", p=128)
        wdv = wd_d[which][l].rearrange("(c p) n -> p c n", p=128)

        def toks(t):
            return slice(t * 512, (t + 1) * 512)

        def pro_sq(t, c):
            sq = rots("sq", [0, 1])
            act(SB_[sq][:], XT[:, c, toks(t)], AF.Square, [("XT", c, t)], [("SB", sq)])
            return sq

        def pro_mm(c, sq):
            mm(ps[0][:], ONES, SB_[sq][:], c == 0, c == NCH - 1, ["CBF", ("SB", sq)], [("ps", 0)])

        def pro_h(t, c):
            col = gcol(l, gi_pre, c)
            hb = t % 2
            P.add("dve", lambda e: e.scalar_tensor_tensor(out=HTB[hb][:, c, :], in0=XT[:, c, toks(t)], scalar=GSC[:, col:col + 1], in1=SF[0][:], op0=ALU.mult, op1=ALU.mult),
                  [("XT", c, t), "GSC", ("SF", 0)], [("HT", hb, c)])

        for c in range(NCH):
            pro_mm(c, pro_sq(0, c))
        finish_stats(0, SF[0], ("SF", 0))
        for c in range(NCH):
            pro_h(0, c)
        for t in range(4):
            hb = t % 2
            HTt = HTB[hb]
            nxt = t + 1 if t + 1 < 4 else None
            pend_sq = {}
            for grp in range(11):
                si = wload([(0, NCH, 256, wgv[:, :, grp * 256:(grp + 1) * 256]), (2048, NCH, 256, wuv[:, :, grp * 256:(grp + 1) * 256])], 3)
                Wg = WS(si)[:, 0:2048].rearrange("p (c n) -> p c n", n=256)
                Wu = WS(si)[:, 2048:4096].rearrange("p (c n) -> p c n", n=256)
                for cc in range(2):
                    chunk = grp * 2 + cc
                    bg = rot("g", [1, 2])
                    bu = rot("u", [3, 4])
                    for c in range(NCH):
                        mm(ps[bg][:], Wg[:, c, cc * 128:(cc + 1) * 128], HTt[:, c, :], c == 0, c == NCH - 1, [("w", si, 0), ("HT", hb, c)], [("ps", bg)])
                    for c in range(NCH):
                        mm(ps[bu][:], Wu[:, c, cc * 128:(cc + 1) * 128], HTt[:, c, :], c == 0, c == NCH - 1, [("w", si, 1), ("HT", hb, c)], [("ps", bu)])
                    sg = rots("tmp", [2, 3])
                    act(SF[sg][:], ps[bg][:], AF.Silu, [("ps", bg)], [("SF", sg)])
                    P.add("dve", lambda e, sg=sg, bu=bu, chunk=chunk: e.tensor_tensor(out=AT[:, chunk, :], in0=SF[sg][:], in1=ps[bu][:], op=ALU.mult),
                          [("SF", sg), ("ps", bu)], [("AT", chunk)])
                    if nxt is not None:
                        if 3 <= chunk < 11:
                            pro_mm(chunk - 3, pend_sq[chunk - 3])
                        if 2 <= chunk < 10:
                            pend_sq[chunk - 2] = pro_sq(nxt, chunk - 2)
                        if chunk == 11:
                            finish_stats(0, SF[0], ("SF", 0))
                        if 12 <= chunk < 20:
                            pro_h(nxt, chunk - 12)
            for grp in range(4):
                si = wload([(0, 8, 256, wdv[:, 0:8, grp * 256:(grp + 1) * 256]), (2048, 8, 256, wdv[:, 8:16, grp * 256:(grp + 1) * 256]),
                            (4096, 6, 256, wdv[:, 16:22, grp * 256:(grp + 1) * 256])], 3)
                Wd = WS(si)[:, 0:5632].rearrange("p (c n) -> p c n", n=256)
                for cc in range(2):
                    dch = grp * 2 + cc
                    bf = rot("f", [5, 6])
                    for c in range(NFF):
                        mm(ps[bf][:], Wd[:, c, cc * 128:(cc + 1) * 128], AT[:, c, :], c == 0, c == NFF - 1, [("w", si, c // 8), ("AT", c)], [("ps", bf)])
                    P.add("dve", lambda e, dch=dch, bf=bf: e.tensor_copy(out=FT[:, dch, :], in_=ps[bf][:]), [("ps", bf)], [("FT", dch)])
                    sq = rots("sq", [0, 1])
                    act(SB_[sq][:], ps[bf][:], AF.Square, [("ps", bf)], [("SB", sq)])
                    mm(ps[7][:], ONES, SB_[sq][:], dch == 0, dch == NCH - 1, ["CBF", ("SB", sq)], [("ps", 7)])
            finish_stats(7, SF[1], ("SF", 1))
            post_residual(l, gi_post, t, 0.5, ("SF", 1))

    def attn_phase(l, s):
        for t in range(4):
            tok = slice(t * 512, (t + 1) * 512)
            norm_stats(lambda c: XT[:, c, tok], lambda c: ("XT", c, t), 0, SF[0], ("SF", 0))
            for c in range(NCH):
                col = gcol(l, 2, c)
                P.add("dve", lambda e, c=c, col=col, tok=tok: e.scalar_tensor_tensor(out=HT[:, c, tok], in0=XT[:, c, tok], scalar=GSC[:, col:col + 1], in1=SF[0][:], op0=ALU.mult, op1=ALU.mult),
                      [("XT", c, t), "GSC", ("SF", 0)], [("HTf", c, t)])
        P.add("dve", lambda e: e.memset(VA[:, :, 64:65], 1.0), [], [("VA1", 0)])
        P.add("dve", lambda e: e.memset(VA[:, :, 132:133], 1.0), [], [("VA1", 1)])
        winv = win_d[l].rearrange("(c p) n -> p c n", p=128)
        for p in range(8):
            if p < 3:
                typ, q0, k0, v0, qscale = "A", p * 128, 384 + p * 128, 768 + p * 128, 0.125
            elif p < 5:
                typ, q0, k0, v0, qscale = "B", 1152 + (p - 3) * 128, 1408 + (p - 3) * 128, 1664 + (p - 3) * 128, 32.0 ** -0.5
            else:
                typ, q0, k0, v0, qscale = "C", 1920 + (p - 5) * 128, 2304 + (p - 5) * 128, 2688 + (p - 5) * 128, 0.125
            si = wload([(0, NCH, 128, winv[:, :, q0:q0 + 128]), (1024, NCH, 128, winv[:, :, k0:k0 + 128]), (2048, NCH, 128, winv[:, :, v0:v0 + 128])])
            Wq = W[:, si, 0:1024].rearrange("p (c n) -> p c n", n=128)
            Wk = W[:, si, 1024:2048].rearrange("p (c n) -> p c n", n=128)
            Wv = W[:, si, 2048:3072].rearrange("p (c n) -> p c n", n=128)
            if typ == "A":
                dma("pool", BAND[:], band_d[l, p], [], ["BAND"], ch_band)
                ch_band.close()
            for t in range(4):
                tok = slice(t * 512, (t + 1) * 512)
                bq = rot("ip", [1, 2])
                for c in range(NCH):
                    mm(ps[bq][:], Wq[:, c, :], HT[:, c, tok], c == 0, c == NCH - 1, [("w", si, 0), ("HTf", c, t)], [("ps", bq)])
                act(QT[:, tok], ps[bq][:], AF.Copy, [("ps", bq)], [("QT", t)], scale=qscale)
                bk = rot("ip", [1, 2])
                for c in range(NCH):
                    mm(ps[bk][:], Wk[:, c, :], HT[:, c, tok], c == 0, c == NCH - 1, [("w", si, 1), ("HTf", c, t)], [("ps", bk)])
                P.add("dve", lambda e, tok=tok, bk=bk: e.tensor_copy(out=KT[:, tok], in_=ps[bk][:]), [("ps", bk)], [("KT", t)])
            for i4 in range(4):
                bv = rot("ip", [1, 2])
                for ii in range(4):
                    i = i4 * 4 + ii
                    for c in range(NCH):
                        mm(ps[bv][:, ii * 128:(ii + 1) * 128], HT[:, c, i * 128:(i + 1) * 128], Wv[:, c, :], c == 0, c == NCH - 1,
                           [("w", si, 2), ("HTf", c, i // 4)], [("ps", bv)])
                src = ps[bv][:, :].rearrange("p (i n) -> p i n", i=4)
                P.add("dve", lambda e, i4=i4, src=src: e.tensor_copy(out=VA[:, i4 * 4:i4 * 4 + 4, 0:64], in_=src[:, :, 0:64]), [("ps", bv)], [("VA", i4, 0)])
                act(VA[:, i4 * 4:i4 * 4 + 4, 68:132], src[:, :, 64:128], AF.Copy, [("ps", bv)], [("VA", i4, 1)])
            HEADS = os.environ.get("KHEADS", "ABC")
            for hh in range(2):
                if typ not in HEADS:
                    continue
                if typ == "A":
                    head_A(l, p, hh)
                elif typ == "B":
                    head_B(l, (p - 3) * 2 + hh, hh)
                else:
                    head_C(l, hh)
            for half in range(2):
                bank = rot("ip", [1, 2])
                psb = ps[bank][:, :].bitcast(BF16)
                for ii in range(8):
                    i = half * 8 + ii
                    P.add("pe", lambda e, psb=psb, ii=ii, i=i: e.transpose(psb[:, ii * 128:(ii + 1) * 128], YP[:, i, :], IDB),
                          [("YP", i // 4, 0), ("YP", i // 4, 1), "CBF"], [("ps", bank)])
                if half == 0:
                    P.add("dve", lambda e, psb=psb, p=p: e.tensor_copy(out=YT[:, p, 0:1024], in_=psb[:, 0:1024]), [("ps", bank)], [("YT", p, 0), ("YT", p, 1)])
                else:
                    act(YT[:, p, 1024:2048], psb[:, 0:1024], AF.Copy, [("ps", bank)], [("YT", p, 2), ("YT", p, 3)])
        P.barrier(engines=("pe", "act", "dve"))
        woutv = wout_d[l].rearrange("(c p) n -> p c n", p=128)
        for t in range(4):
            tok = slice(t * 512, (t + 1) * 512)
            for grp in range(4):
                si = wload([(0, NCH, 256, woutv[:, :, grp * 256:(grp + 1) * 256])])
                Wo = W[:, si, 0:2048].rearrange("p (c n) -> p c n", n=256)
                for cc in range(2):
                    dch = grp * 2 + cc
                    bf = rot("f", [5, 6])
                    for c in range(NCH):
                        mm(ps[bf][:], Wo[:, c, cc * 128:(cc + 1) * 128], YT[:, c, tok], c == 0, c == NCH - 1, [("w", si, 0), ("YT", c, t)], [("ps", bf)])
                    P.add("dve", lambda e, dch=dch, bf=bf: e.tensor_copy(out=FT[:, dch, :], in_=ps[bf][:]), [("ps", bf)], [("FT", dch)])
                    sq = rots("sq", [0, 1])
                    act(SB_[sq][:], ps[bf][:], AF.Square, [("ps", bf)], [("SB", sq)])
                    mm(ps[7][:], ONES, SB_[sq][:], dch == 0, dch == NCH - 1, ["CBF", ("SB", sq)], [("ps", 7)])
            finish_stats(7, SF[1], ("SF", 1))
            post_residual(l, 3, t, 1.0, ("SF", 1))

    def pv(obank, ocol, ow, Ptile, blk, j, hh, vw, first, last):
        mm(ps[obank][:, ocol:ocol + ow], SB_[Ptile][:, blk * 128:(blk + 1) * 128], VA[:, j, hh * 68:hh * 68 + vw], first, last,
           [("SB", Ptile), ("VA", j // 4, hh), ("VA1", hh)], [("ps", obank)])

    def pipeline(tiles, stA, stB, depth=2):
        n = len(tiles)
        for idx in range(n + depth):
            if idx < n:
                stA(tiles[idx])
            if idx - depth >= 0:
                stB(tiles[idx - depth])

    def head_A(l, p, hh):
        pr = slice(hh * 64, hh * 64 + 64)
        tiles = []
        for qb in range(4):
            i0 = 4 * qb
            js = list(range(max(i0 - 4, 0), i0 + 4))
            for j in js:
                a = max(j, i0)
                b = min(j + 4, i0 + 3)
                tiles.append(dict(qb=qb, i0=i0, j=j, a=a, b=b, lo=(a - i0) * 128, hi=(b - i0 + 1) * 128, first=(j == js[0]), last=(j == js[-1])))

        def stA(t):
            j, i0, lo, hi, a, b = t["j"], t["i0"], t["lo"], t["hi"], t["a"], t["b"]
            sbk = rot("s", [3, 4, 5, 6])
            mm(ps[sbk][:, lo:hi], KT[pr, j * 128:(j + 1) * 128], QT[pr, i0 * 128 + lo:i0 * 128 + hi], True, False,
               [("KT", j // 4), ("QT", t["qb"])], [("ps", sbk)])
            mm(ps[sbk][:, lo:hi], IDB, BAND[:, hh * 640 + (a - j) * 128:hh * 640 + (b - j + 1) * 128], False, True, ["CBF", "BAND"], [("ps", sbk)])
            Pt = rots("P", [5, 6, 7])
            act(SB_[Pt][:, lo:hi], ps[sbk][:, lo:hi], AF.Exp, [("ps", sbk)], [("SB", Pt)])
            t["Pt"] = Pt

        def stB(t):
            j, i0, a, b, qb = t["j"], t["i0"], t["a"], t["b"], t["qb"]
            ob = (7, 0)[qb % 2]
            for blk in range(a - i0, b - i0 + 1):
                pv(ob, blk * 65, 65, t["Pt"], blk, j, hh, 65, t["first"] and blk == a - i0, t["last"] and blk == b - i0)
            if t["last"]:
                O = ps[ob][:, 0:260].rearrange("p (b n) -> p b n", b=4)
                P.add("dve", lambda e, O=O: e.reciprocal(out=SM[:, 0:4], in_=O[:, :, 64]), [("ps", ob)], ["SMr"])
                P.add("dve", lambda e, O=O, i0=i0: e.tensor_tensor(out=YP[:, i0:i0 + 4, hh * 64:(hh + 1) * 64], in0=O[:, :, 0:64],
                                                                    in1=SM[:, 0:4].unsqueeze(2).to_broadcast([128, 4, 64]), op=ALU.mult),
                      [("ps", ob), "SMr"], [("YP", qb, hh)])

        pipeline(tiles, stA, stB)

    def head_B(l, hB, hh):
        tiles = []
        for qb in range(4):
            nj = 4 * qb + 4
            for m in range(2):
                for j in range(nj):
                    tiles.append(dict(qb=qb, m=m, j=j, first=(j == 0), last=(j == nj - 1)))

        def stA(t):
            qb, m, j = t["qb"], t["m"], t["j"]
            base = hh * 64 + m * 32
            pr = slice(base, base + 32)
            tp = (96, 0) if base == 96 else None
            diag = j >= 4 * qb
            lo = max(j - 4 * qb, 0) * 128
            sbk = rot("s", [3, 4, 5, 6])
            rk = [("KT", j // 4), ("QT", qb)]
            lo2 = lo
            if diag:
                mm(ps[sbk][:, lo:lo + 128], KT[pr, j * 128:(j + 1) * 128], QT[pr, qb * 512 + lo:qb * 512 + lo + 128], True, False, rk, [("ps", sbk)], tp)
                mm(ps[sbk][:, lo:lo + 128], IDB, DIAGB(hB), False, True, ["CBF"], [("ps", sbk)])
                lo2 = lo + 128
            if lo2 < 512:
                c0 = qb * 512 + lo2 - 128 * j
                mm(ps[sbk][:, lo2:512], KT[pr, j * 128:(j + 1) * 128], QT[pr, qb * 512 + lo2:qb * 512 + 512], not diag, False, rk, [("ps", sbk)], tp)
                mm(ps[sbk][:, lo2:512], BL[:, hB * 128:(hB + 1) * 128], BR[:, c0:c0 + 512 - lo2], False, True, ["BL", "BR"], [("ps", sbk)])
            Pt = rots("P", [5, 6, 7])
            act(SB_[Pt][:, lo:512], ps[sbk][:, lo:512], AF.Exp, [("ps", sbk)], [("SB", Pt)])
            t["Pt"] = Pt
            t["lo"] = lo

        def stB(t):
            qb, m, j, lo = t["qb"], t["m"], t["j"], t["lo"]
            obank = 7 if m == 0 else 0
            for blk in range(lo // 128, 4):
                pv(obank, blk * 65, 65, t["Pt"], blk, j, hh, 65, t["first"] and blk == lo // 128, t["last"] and blk == 3)
            if not (t["last"] and m == 1):
                return
            O1 = ps[7][:, 0:260].rearrange("p (b n) -> p b n", b=4)
            O2 = ps[0][:, 0:260].rearrange("p (b n) -> p b n", b=4)
            T1 = SF[0][:, 0:256].rearrange("p (b n) -> p b n", b=4)
            T2 = SF[1][:, 0:256].rearrange("p (b n) -> p b n", b=4)
            P.add("dve", lambda e, O1=O1: e.reciprocal(out=SM[:, 0:4], in_=O1[:, :, 64]), [("ps", 7)], ["SMr"])
            P.add("dve", lambda e, O2=O2: e.reciprocal(out=SM[:, 4:8], in_=O2[:, :, 64]), [("ps", 0)], ["SMr2"])
            P.add("dve", lambda e: e.tensor_scalar(out=SM[:, 8:12], in0=SM[:, 4:8], scalar1=NEGLAM[:, l:l + 1], scalar2=None, op0=ALU.mult), ["SMr2", ("NEGLAM", l)], ["SMr2n"])
            P.add("dve", lambda e, O1=O1, T1=T1: e.tensor_tensor(out=T1, in0=O1[:, :, 0:64], in1=SM[:, 0:4].unsqueeze(2).to_broadcast([128, 4, 64]), op=ALU.mult),
                  [("ps", 7), "SMr"], [("SF", 0)])
            P.add("dve", lambda e, O2=O2, T2=T2: e.tensor_tensor(out=T2, in0=O2[:, :, 0:64], in1=SM[:, 8:12].unsqueeze(2).to_broadcast([128, 4, 64]), op=ALU.mult),
                  [("ps", 0), "SMr2n"], [("SF", 1)])
            P.add("dve", lambda e, T1=T1, T2=T2: e.tensor_tensor(out=T1, in0=T1, in1=T2, op=ALU.add), [("SF", 0), ("SF", 1)], [("SF", 0)])
            act(T2, T1, AF.Square, [("SF", 0)], [("SF", 1)])
            P.add("dve", lambda e, T2=T2: e.tensor_reduce(out=SM[:, 12:16], in_=T2, axis=AX.X, op=ALU.add), [("SF", 1)], ["SMss"])
            act(SM[:, 16:20], SM[:, 12:16], AF.Ln, ["SMss"], ["SMln"], scale=1.0 / 64.0, bias=EPS)
            act(SM[:, 20:24], SM[:, 16:20], AF.Exp, ["SMln"], ["SMrstd"], scale=-0.5)
            P.add("dve", lambda e, T1=T1: e.tensor_tensor(out=T1, in0=T1, in1=SM[:, 20:24].unsqueeze(2).to_broadcast([128, 4, 64]), op=ALU.mult), [("SF", 0), "SMrstd"], [("SF", 0)])
            P.add("dve", lambda e, T1=T1, qb=qb: e.tensor_tensor(out=YP[:, 4 * qb:4 * qb + 4, hh * 64:(hh + 1) * 64], in0=T1,
                                                                 in1=GSUB[:, l * 64:(l + 1) * 64].unsqueeze(1).to_broadcast([128, 4, 64]), op=ALU.mult),
                  [("SF", 0), ("GSUB", l)], [("YP", qb, hh)])

        pipeline(tiles, stA, stB)

    def head_C(l, hh):
        pr = slice(hh * 64, hh * 64 + 64)
        ACCS = [4, 0]
        tiles = []
        for qb in range(4):
            jtop = 4 * qb + 3
            for j in range(jtop, -1, -1):
                tiles.append(dict(qb=qb, j=j, jtop=jtop, diag=(j >= 4 * qb), lo=max(j - 4 * qb, 0) * 128))
        accstate = [0]

        def stA(t):
            qb, j, lo, diag = t["qb"], t["j"], t["lo"], t["diag"]
            qc = slice(qb * 512 + lo, qb * 512 + 512)
            kc = slice(j * 128, (j + 1) * 128)
            rk = [("KT", j // 4), ("QT", qb)]
            zs = rot("zs", [3, 4, 1])
            mm(ps[zs][:, lo:512], KT[pr, kc], QT[pr, qc], True, not diag, rk, [("ps", zs)])
            if diag:
                mm(ps[zs][:, lo:lo + 128], IDB, MASKC, False, True, ["CBF"], [("ps", zs)])
            et = rots("e", [2, 3])
            act(SF[et][:, lo:512], ps[zs][:, lo:512], AF.Exp, [("ps", zs)], [("SF", et)])
            Lt = rots("L", [1, 2, 3])
            act(SB_[Lt][:, lo:512], SF[et][:, lo:512], AF.Ln, [("SF", et)], [("SB", Lt)], bias=1.0)
            t["Lt"] = Lt

        def stB(t):
            qb, j, lo, diag, jtop, Lt = t["qb"], t["j"], t["lo"], t["diag"], t["jtop"], t["Lt"]
            qc = slice(qb * 512 + lo, qb * 512 + 512)
            kc = slice(j * 128, (j + 1) * 128)
            rk = [("KT", j // 4), ("QT", qb)]
            ob = (7, 0)[qb % 2]
            if j == jtop:
                for a_ in ACCS:
                    P.add("dve", lambda e, a_=a_: e.memset(SB_[a_][:], 0.0), [], [("SB", a_)])
            cur = ACCS[accstate[0] % 2]
            nxt = ACCS[(accstate[0] + 1) % 2]
            za = rot("za", [5, 6])
            mm(ps[za][:, lo:512], KT[pr, kc], QT[pr, qc], True, False, rk, [("ps", za)])
            mm(ps[za][:, lo:512], NEGTRI, SB_[Lt][:, lo:512], False, False, ["CBF", ("SB", Lt)], [("ps", za)])
            if j < jtop:
                mm(ps[za][:, lo:512], NEGONES, SB_[cur][:, lo:512], False, not diag, ["CBF", ("SB", cur)], [("ps", za)])
            if diag:
                mm(ps[za][:, lo:lo + 128], IDB, MASKC, False, True, ["CBF"], [("ps", za)])
            Pt = rots("P", [5, 6, 7])
            act(SB_[Pt][:, lo:512], ps[za][:, lo:512], AF.Exp, [("ps", za)], [("SB", Pt)])
            if j > 0:
                P.add("dve", lambda e, cur=cur, nxt=nxt: e.tensor_tensor(out=SB_[nxt][:, lo:512], in0=SB_[cur][:, lo:512], in1=SB_[Lt][:, lo:512], op=ALU.add),
                      [("SB", cur), ("SB", Lt)], [("SB", nxt)])
                accstate[0] += 1
            for blk in range(lo // 128, 4):
                pv(ob, blk * 64, 64, Pt, blk, j, hh, 64, (j == jtop) and blk == lo // 128, (j == 0) and (blk == 3))
            if j == 0:
                O = ps[ob][:, 0:256].rearrange("p (b n) -> p b n", b=4)
                P.add("dve", lambda e, O=O: e.tensor_copy(out=YP[:, 4 * qb:4 * qb + 4, hh * 64:(hh + 1) * 64], in_=O), [("ps", ob)], [("YP", qb, hh)])

        pipeline(tiles, stA, stB)


    last_stores = []
    for s in range(NS):
        P.barrier()
        if STAGE >= 2:
            load_x(s)
        for l in range(L):
            if do_ffn:
                P.barrier(engines=("pe", "act", "dve"))
                ffn_phase(l, 0)
            if do_attn:
                P.barrier(engines=("pe", "act", "dve"))
                attn_phase(l, s)
            if do_ffn:
                P.barrier(engines=("pe", "act", "dve"))
                ffn_phase(l, 1)
        P.barrier(engines=("pe", "act", "dve"))
        if STAGE >= 3:
            last_stores = store_x(s)
    fin = Op("sp", None, False, None)
    for o in P.alias_dmas:
        fin.deps.add(o)
    P.ops.append(fin)
    P.emit(nc, es)
    es.close()
    return nc


def host_consts(L, rel_bias):
    k = np.arange(128)[:, None]
    q = np.arange(128)[None, :]
    idf = np.eye(128, dtype=np.float32)
    cbf = np.zeros((128, 1152), np.float32)
    cbf[:, 0:128] = idf
    cbf[:, 128:256] = 1.0
    cbf[:, 256:384] = -(k >= q).astype(np.float32)
    cbf[:, 384:512] = -1.0
    cbf[:, 512:640] = np.where(k < q, 0.0, NEG)
    slopes = [2.0 ** (-8.0 * (h + 1) / 4.0) for h in range(4)]
    for h in range(4):
        t = -slopes[h] * np.abs(q - k).astype(np.float32)
        t = np.where((k >= 64) & (q < 64), NEG, t)
        cbf[:, 640 + h * 128:640 + (h + 1) * 128] = t
    c = np.arange(2048)
    br = np.zeros((128, 2048), np.float32)
    br[0:3] = np.stack([(c // 128) * 128, c % 128, np.ones_like(c)]).astype(np.float32)
    bl = np.zeros((128, 512), np.float32)
    for h in range(4):
        bl[0, h * 128:(h + 1) * 128] = -slopes[h]
        bl[1, h * 128:(h + 1) * 128] = -slopes[h]
        bl[2, h * 128:(h + 1) * 128] = slopes[h] * np.arange(128)
    cc = np.arange(640)[None, :]
    r = cc // 128
    qq = cc % 128
    kk = np.arange(128)[:, None]
    rel = 128 * r + qq - kk
    idx = np.clip(rel, -128, 128) + 128
    masked = ((r == 0) & (kk >= 64) & (qq < 64)) | ((r == 4) & (kk < 64) & (qq >= 64))
    band = np.zeros((L, 3, 128, 1280), np.float32)
    for l in range(L):
        for p in range(3):
            for hh in range(2):
                g = rel_bias[l, 2 * p + hh][idx]
                band[l, p, :, hh * 640:(hh + 1) * 640] = np.where(masked, np.float32(NEG), g)
    return idf, cbf, br, bl, band


_CACHE = {}
NS_PER_LAUNCH = 4


def kernel(**inputs):
    L = 2
    NS = 4
    x = np.ascontiguousarray(inputs["x"], dtype=np.float32)
    gl = [inputs[k] for k in ("ffn1_pre_g", "ffn1_post_g", "mix_pre_g", "mix_post_g", "ffn2_pre_g", "ffn2_post_g")]
    gains = np.stack([np.asarray(g, np.float32) for g in gl], axis=1)
    gains = np.ascontiguousarray(gains.reshape(L, 6, NCH, 128).transpose(3, 0, 1, 2).reshape(128, L * 6 * NCH))
    idf, cbf, br, bl, band = host_consts(L, np.asarray(inputs["rel_bias"], np.float32))
    lamv = np.concatenate([np.asarray(inputs[k], np.float32) for k in ("diff_lambda_q1", "diff_lambda_k1", "diff_lambda_q2", "diff_lambda_k2")], axis=1)
    lamv = np.ascontiguousarray(lamv.reshape(1, L * 128))
    subln = np.ascontiguousarray(np.asarray(inputs["diff_subln_g"], np.float32).reshape(1, L * 64))
    common = {
        "wg1": inputs["ffn1_w_gate"], "wu1": inputs["ffn1_w_up"], "wd1": inputs["ffn1_w_down"],
        "wg2": inputs["ffn2_w_gate"], "wu2": inputs["ffn2_w_up"], "wd2": inputs["ffn2_w_down"],
        "w_in": inputs["w_in"], "w_out": inputs["w_out"], "gains": gains, "band": band,
        "idf": idf, "cbf": cbf, "br": br, "bl": bl, "lamv": lamv, "subln": subln,
    }
    common = {k: np.ascontiguousarray(np.asarray(v, np.float32)) for k, v in common.items()}
    NS_L = NS_PER_LAUNCH
    if "nc" not in _CACHE:
        _CACHE["nc"] = build(NS_L, L)
    nc = _CACHE["nc"]
    out = np.empty_like(x)
    for k in range(NS // NS_L):
        in_maps = []
        for c in range(N_CORES):
            m = dict(common)
            m["x"] = x[c * NS + k * NS_L:c * NS + (k + 1) * NS_L]
            in_maps.append(m)
        res = run_bass_kernel_spmd(nc, in_maps, core_ids=list(range(N_CORES)))
        for c in range(N_CORES):
            out[c * NS + k * NS_L:c * NS + (k + 1) * NS_L] = res.results[c]["out"]
    return out
```

```python
import math
from contextlib import ExitStack

import numpy as np
import concourse.bass as bass
import concourse.mybir as mybir
from concourse.bass_utils import run_bass_kernel_spmd

F32 = mybir.dt.float32
BF16 = mybir.dt.bfloat16
AF = mybir.ActivationFunctionType
ALU = mybir.AluOpType
AX = mybir.AxisListType

D = 1024
S = 2048
DFF = 2816
NCH = 8
NFF = 22
INC = 3072
EPS = 1e-6
NEG = -30000.0
N_CORES = 8
ENGNAME = {"pe": "tensor", "act": "scalar", "dve": "vector", "pool": "gpsimd", "sp": "sync"}


class Chan:
    def __init__(self, name):
        self.name = name
        self.count = 0
        self.sem = None
        self.cur = []

    def close(self):
        if self.cur:
            last = self.cur[-1]
            for o in self.cur:
                o.group_last = last
        self.cur = []


class Op:
    __slots__ = ("eng", "fn", "deps", "dma", "chan", "signal", "val", "sem", "group_last")

    def __init__(self, eng, fn, dma, chan):
        self.eng = eng
        self.fn = fn
        self.deps = set()
        self.dma = dma
        self.chan = chan
        self.signal = False
        self.val = 0
        self.sem = None
        self.group_last = None


class Prog:
    def __init__(self):
        self.ops = []
        self.lastw = {}
        self.rd_eng = {}
        self.rd_dma = {}
        self.last_op = {}
        self.alias_dmas = []
        self.chans = []

    def chan(self, name):
        c = Chan(name)
        self.chans.append(c)
        return c

    def _dep(self, op, p, kind):
        if p is op:
            return
        if p.fn is None and not p.dma:
            assert p.eng == op.eng
            return
        if (not p.dma) and (not op.dma) and p.eng == op.eng:
            if op.eng == "pe":
                return
            if kind != "raw":
                return
        op.deps.add(p)

    def add(self, eng, fn, reads=(), writes=(), dma=False, chan=None, alias=False):
        op = Op(eng, fn, dma, chan)
        for k in reads:
            w = self.lastw.get(k)
            if w is not None:
                self._dep(op, w, "raw")
            if isinstance(k, tuple) and k[0] == "ps":
                for r in self.rd_eng.get(k, {}).values():
                    if r.eng != eng:
                        self._dep(op, r, "raw")
        for k in writes:
            w = self.lastw.get(k)
            if w is not None:
                self._dep(op, w, "waw")
            for r in self.rd_eng.get(k, {}).values():
                self._dep(op, r, "war")
            for r in self.rd_dma.get(k, ()):
                self._dep(op, r, "war")
        for k in reads:
            if dma:
                self.rd_dma.setdefault(k, []).append(op)
            else:
                self.rd_eng.setdefault(k, {})[eng] = op
        for k in writes:
            self.lastw[k] = op
            self.rd_eng[k] = {}
            self.rd_dma[k] = []
        if dma:
            chan.cur.append(op)
            if alias:
                self.alias_dmas.append(op)
        elif fn is not None:
            self.last_op[eng] = op
        self.ops.append(op)
        return op

    def barrier(self, engines=("pe", "act", "dve", "pool", "sp")):
        lasts = [o for o in self.last_op.values()]
        al = list(self.alias_dmas)
        self.alias_dmas = []
        for e in engines:
            op = Op(e, None, False, None)
            for o in lasts:
                if o.eng != e:
                    op.deps.add(o)
            for o in al:
                op.deps.add(o)
            self.ops.append(op)

    def emit(self, nc, es):
        for c in self.chans:
            c.close()
        for op in self.ops:
            for d in op.deps:
                d.signal = True
        sems = {e: es.enter_context(nc.semaphore("s_" + e)) for e in ENGNAME}
        for c in self.chans:
            c.sem = es.enter_context(nc.semaphore("c_" + c.name))
            c.count = 0
        counts = {e: 0 for e in ENGNAME}
        for op in self.ops:
            if op.dma:
                op.chan.count += 16
                op.val = op.chan.count
                op.sem = op.chan.sem
            elif op.signal:
                counts[op.eng] += 1
                op.val = counts[op.eng]
                op.sem = sems[op.eng]
        block = es.enter_context(nc.Block())
        for e in ENGNAME:
            myops = [op for op in self.ops if op.eng == e]

            def body(eng, myops=myops):
                waited = {}
                for op in myops:
                    need = {}
                    for d in op.deps:
                        if d.dma:
                            g = d.group_last if d.group_last is not None else d
                            sem, val = g.sem, g.val
                        else:
                            sem, val = d.sem, d.val
                        k = id(sem)
                        if k not in need or need[k][1] < val:
                            need[k] = (sem, val)
                    for k, (sem, val) in need.items():
                        if waited.get(k, 0) < val:
                            eng.wait_ge(sem, val)
                            waited[k] = val
                    if op.fn is not None:
                        ins = op.fn(eng)
                        if op.dma:
                            ins.then_inc(op.sem, 16)
                        elif op.signal:
                            ins.then_inc(op.sem, 1)

            getattr(block, ENGNAME[e])(body)


def build(NS, L, do_ffn=True, do_attn=True):
    import os
    STAGE = int(os.environ.get("KSTAGE", "9"))
    nc = bass.Bass("TRN2", target_bir_lowering=False)
    es = ExitStack()
    P = Prog()

    def din(name, shape):
        return nc.dram_tensor(name, list(shape), F32, kind="ExternalInput").ap()

    x_d = din("x", [NS, S, D])
    out_d = nc.dram_tensor("out", [NS, S, D], F32, kind="ExternalOutput").ap()
    wg_d = [din("wg1", [L, D, DFF]), din("wg2", [L, D, DFF])]
    wu_d = [din("wu1", [L, D, DFF]), din("wu2", [L, D, DFF])]
    wd_d = [din("wd1", [L, DFF, D]), din("wd2", [L, DFF, D])]
    win_d = din("w_in", [L, D, INC])
    wout_d = din("w_out", [L, D, D])
    gains_d = din("gains", [128, L * 6 * NCH])
    band_d = din("band", [L, 3, 128, 1280])
    idf_d = din("idf", [128, 128])
    cbf_d = din("cbf", [128, 1152])
    br_d = din("br", [128, 2048])
    bl_d = din("bl", [128, 512])
    lamv_d = din("lamv", [1, L * 128])
    subln_d = din("subln", [1, L * 64])

    def sb(name, shape, dt):
        return es.enter_context(nc.sbuf_tensor(name, list(shape), dt))

    XT = sb("XT", [128, NCH, S], F32)
    HTFLAT = sb("HT", [128, NCH * S], BF16)
    HT = HTFLAT[:, :].rearrange("p (c t) -> p c t", c=NCH)
    HTB = [HTFLAT[:, i * 4096:(i + 1) * 4096].rearrange("p (c t) -> p c t", c=NCH) for i in range(2)]
    R1 = sb("R1", [128, 16384], BF16)
    R2 = sb("R2", [128, 5248], F32)
    W = sb("W", [128, 2, 5632], BF16)
    BAND = sb("BAND", [128, 1280], BF16)
    IDF = sb("IDF", [128, 128], F32)
    CBF = sb("CBF", [128, 1152], BF16)
    BR = sb("BR", [128, 2048], BF16)
    BL = sb("BL", [128, 512], BF16)
    GAINS = sb("GAINS", [128, L * 6 * NCH], F32)
    GSC = sb("GSC", [128, L * 6 * NCH], F32)
    LAMV = sb("LAMV", [128, L * 128], F32)
    SUBLN = sb("SUBLN", [128, L * 64], F32)
    GSUB = sb("GSUB", [128, L * 64], F32)
    NEGLAM = sb("NEGLAM", [128, L], F32)
    LTMP = sb("LTMP", [128, 64], F32)
    LS = sb("LS", [128, 4], F32)
    SF = [sb(f"SF{i}", [128, 512], F32) for i in range(4)]
    SB_ = [sb(f"SB{i}", [128, 512], BF16) for i in range(8)]
    SM = sb("SM", [128, 64], F32)
    ps = [es.enter_context(nc.psum_tensor(f"ps{i}", [128, 512], F32)) for i in range(8)]

    IDB = CBF[:, 0:128]
    ONES = CBF[:, 128:256]
    NEGTRI = CBF[:, 256:384]
    NEGONES = CBF[:, 384:512]
    MASKC = CBF[:, 512:640]

    def DIAGB(h):
        return CBF[:, 640 + h * 128:640 + (h + 1) * 128]

    YT = R1[:, :].rearrange("p (c t) -> p c t", c=NCH)
    AT = R1[:, 0:NFF * 512].rearrange("p (c t) -> p c t", c=NFF)
    FT = R2[:, 0:4096].rearrange("p (c t) -> p c t", c=NCH)
    XIO = [R2[:, i * 1024:(i + 1) * 1024] for i in range(4)]
    R2B = R2[:, :].bitcast(BF16)
    QT = R2B[:, 0:2048]
    KT = R2B[:, 2048:4096]
    VA = R2B[:, 4096:4096 + 2176].rearrange("p (i n) -> p i n", i=16)
    YP = R2B[:, 6272:6272 + 2048].rearrange("p (i n) -> p i n", i=16)

    ch_const = P.chan("const")
    ch_w = [P.chan("w0"), P.chan("w1"), P.chan("w2")]
    ch_xin = [P.chan("xin0"), P.chan("xin1")]
    ch_xout = [P.chan("xout0"), P.chan("xout1")]
    ch_band = P.chan("band")

    def mm(out, lhsT, rhs, start, stop, reads, writes, tp=None):
        kw = {}
        if tp is not None:
            kw["tile_position"] = tp
        P.add("pe", lambda e: e.matmul(out, lhsT=lhsT, rhs=rhs, start=start, stop=stop, **kw), reads, writes)

    def act(out, in_, func, reads, writes, **kw):
        P.add("act", lambda e: e.activation(out=out, in_=in_, func=func, **kw), reads, writes)

    def dma(eng, out, in_, reads, writes, chan, alias=False):
        P.add(eng, lambda e: e.dma_start(out=out, in_=in_), reads, writes, dma=True, chan=chan, alias=alias)

    wslot_ctr = [0]

    def WS(si):
        return W[:, si, :] if si < 2 else HTFLAT[:, 8192:8192 + 5632]

    def wload(parts, nslots=2):
        si = wslot_ctr[0] % nslots
        wslot_ctr[0] += 1
        P.add("pool", None, [], [("w", si, k) for k in range(3)])
        for idx, (col0, nch, n, src) in enumerate(parts):
            dst = WS(si)[:, col0:col0 + nch * n].rearrange("p (c n) -> p c n", n=n)
            wk = [("w", si, idx)]
            if si == 2 and idx == 0:
                wk += [("HTf", c, t) for c in range(4, 7) for t in range(4)]
            dma("pool", dst, src, [], wk, ch_w[si])
        ch_w[si].close()
        return si

    psrot = {}

    def rot(name, banks):
        i = psrot.get(name, 0)
        psrot[name] = i + 1
        return banks[i % len(banks)]

    sfrot = {}

    def rots(name, items):
        i = sfrot.get(name, 0)
        sfrot[name] = i + 1
        return items[i % len(items)]

    dma("sp", IDF[:], idf_d, [], ["IDF"], ch_const)
    dma("pool", CBF[:], cbf_d, [], ["CBF"], ch_const)
    dma("pool", BR[:], br_d, [], ["BR"], ch_const)
    dma("pool", BL[:], bl_d, [], ["BL"], ch_const)
    dma("sp", GAINS[:], gains_d, [], ["GAINS"], ch_const)
    dma("sp", LAMV[:], lamv_d.partition_broadcast(128), [], ["LAMV"], ch_const)
    dma("sp", SUBLN[:], subln_d.partition_broadcast(128), [], ["SUBLN"], ch_const)
    ch_const.close()
    P.add("dve", lambda e: e.tensor_scalar(out=GSC[:], in0=GAINS[:], scalar1=32.0, scalar2=None, op0=ALU.mult), ["GAINS"], ["GSC"])
    for l in range(L):
        lam_init = 0.8 - 0.6 * math.exp(-0.3 * l)
        b = l * 128
        P.add("dve", lambda e, b=b: e.tensor_tensor(out=LTMP[:, 0:32], in0=LAMV[:, b:b + 32], in1=LAMV[:, b + 32:b + 64], op=ALU.mult), ["LAMV"], ["LTMP0"])
        P.add("dve", lambda e, b=b: e.tensor_tensor(out=LTMP[:, 32:64], in0=LAMV[:, b + 64:b + 96], in1=LAMV[:, b + 96:b + 128], op=ALU.mult), ["LAMV"], ["LTMP1"])
        P.add("dve", lambda e: e.tensor_reduce(out=LS[:, 0:2], in_=LTMP[:, :].rearrange("p (a b) -> p a b", a=2), axis=AX.X, op=ALU.add), ["LTMP0", "LTMP1"], ["LS01"])
        act(LS[:, 2:4], LS[:, 0:2], AF.Exp, ["LS01"], ["LS23"])
        P.add("dve", lambda e: e.tensor_tensor(out=LS[:, 0:1], in0=LS[:, 3:4], in1=LS[:, 2:3], op=ALU.subtract), ["LS23"], ["LS01"])
        P.add("dve", lambda e, l=l, li=lam_init: e.tensor_scalar(out=NEGLAM[:, l:l + 1], in0=LS[:, 0:1], scalar1=-li, scalar2=None, op0=ALU.add), ["LS01"], [("NEGLAM", l)])
        P.add("dve", lambda e, l=l, li=lam_init: e.tensor_scalar(out=GSUB[:, l * 64:(l + 1) * 64], in0=SUBLN[:, l * 64:(l + 1) * 64], scalar1=1.0 - li, scalar2=None, op0=ALU.mult), ["SUBLN"], [("GSUB", l)])

    def gcol(l, gi, c):
        return (l * 6 + gi) * NCH + c

    def norm_stats(src_fn, src_keys, ss_bank, rs_tile, rs_key):
        for c in range(NCH):
            sq = rots("sq", [0, 1])
            act(SB_[sq][:], src_fn(c), AF.Square, [src_keys(c)], [("SB", sq)])
            mm(ps[ss_bank][:], ONES, SB_[sq][:], c == 0, c == NCH - 1, ["CBF", ("SB", sq)], [("ps", ss_bank)])
        finish_stats(ss_bank, rs_tile, rs_key)

    def finish_stats(ss_bank, rs_tile, rs_key):
        act(rs_tile[:], ps[ss_bank][:], AF.Ln, [("ps", ss_bank)], [rs_key], bias=1024.0 * EPS)
        act(rs_tile[:], rs_tile[:], AF.Exp, [rs_key], [rs_key], scale=-0.5)

    def load_x(s):
        for i in range(16):
            sl = i % 2
            dma("sp", XIO[sl], x_d[s, i * 128:(i + 1) * 128, :], [], [("xio", sl)], ch_xin[sl], alias=True)
            ch_xin[sl].close()
            for hb in range(2):
                bank = rot("xl", [1, 2, 3, 4])
                for cc in range(4):
                    c = hb * 4 + cc
                    P.add("pe", lambda e, bank=bank, cc=cc, c=c, sl=sl: e.transpose(ps[bank][:, cc * 128:(cc + 1) * 128], XIO[sl][:, c * 128:(c + 1) * 128], IDF[:]),
                          [("xio", sl), "IDF"], [("ps", bank)])
                dst = XT[:, hb * 4:hb * 4 + 4, i * 128:(i + 1) * 128]
                src = ps[bank][:, :].rearrange("p (c t) -> p c t", c=4)
                wk = [("XT", c, i // 4) for c in range(hb * 4, hb * 4 + 4)]
                if hb == 0:
                    P.add("dve", lambda e, dst=dst, src=src: e.tensor_copy(out=dst, in_=src), [("ps", bank)], wk)
                else:
                    P.add("act", lambda e, dst=dst, src=src: e.activation(out=dst, in_=src, func=AF.Copy), [("ps", bank)], wk)

    def store_x(s):
        last = []
        for i in range(16):
            sl = 2 + (i % 2)
            for hb in range(2):
                bank = rot("xl", [1, 2, 3, 4])
                for cc in range(4):
                    c = hb * 4 + cc
                    P.add("pe", lambda e, bank=bank, cc=cc, c=c, i=i: e.transpose(ps[bank][:, cc * 128:(cc + 1) * 128], XT[:, c, i * 128:(i + 1) * 128], IDF[:]),
                          [("XT", c, i // 4), "IDF"], [("ps", bank)])
                dst = XIO[sl][:, hb * 512:(hb + 1) * 512]
                if hb == 0:
                    P.add("dve", lambda e, dst=dst, bank=bank: e.tensor_copy(out=dst, in_=ps[bank][:]), [("ps", bank)], [("xio", sl, hb)])
                else:
                    P.add("act", lambda e, dst=dst, bank=bank: e.activation(out=dst, in_=ps[bank][:], func=AF.Copy), [("ps", bank)], [("xio", sl, hb)])
            if STAGE >= 4:
                dma(os.environ.get("KSTQ", "sp"), out_d[s, i * 128:(i + 1) * 128, :], XIO[sl], [("xio", sl, 0), ("xio", sl, 1)], [], ch_xout[i % 2], alias=True)
                ch_xout[i % 2].close()
                last.append(P.ops[-1])
        return last

    def post_residual(l, gi, t, factor, rs_key):
        rs2 = SF[1]
        for c in range(NCH):
            tmp = rots("tmp", [2, 3])
            col = gcol(l, gi, c)
            P.add("dve", lambda e, tmp=tmp, c=c, col=col: e.scalar_tensor_tensor(out=SF[tmp][:], in0=FT[:, c, :], scalar=GSC[:, col:col + 1], in1=rs2[:], op0=ALU.mult, op1=ALU.mult),
                  [("FT", c), "GSC", rs_key], [("SF", tmp)])
            xs = XT[:, c, t * 512:(t + 1) * 512]
            P.add("dve", lambda e, tmp=tmp, xs=xs: e.scalar_tensor_tensor(out=xs, in0=SF[tmp][:], scalar=factor, in1=xs, op0=ALU.mult, op1=ALU.add),
                  [("SF", tmp), ("XT", c, t)], [("XT", c, t)])

    def ffn_phase(l, which):
        gi_pre, gi_post = (0, 1) if which == 0 else (4, 5)
        wgv = wg_d[which][l].rearrange("(c p) n -> p c n", p=128)
        wuv = wu_d[which][l].rearrange("(c p) n -> p c n", p=128)
        wdv = wd_d[which][l].rearrange("(c p) n -> p c n", p=128)

        def toks(t):
            return slice(t * 512, (t + 1) * 512)

        def pro_sq(t, c):
            sq = rots("sq", [0, 1])
            act(SB_[sq][:], XT[:, c, toks(t)], AF.Square, [("XT", c, t)], [("SB", sq)])
            return sq

        def pro_mm(c, sq):
            mm(ps[0][:], ONES, SB_[sq][:], c == 0, c == NCH - 1, ["CBF", ("SB", sq)], [("ps", 0)])

        def pro_h(t, c):
            col = gcol(l, gi_pre, c)
            hb = t % 2
            P.add("dve", lambda e: e.scalar_tensor_tensor(out=HTB[hb][:, c, :], in0=XT[:, c, toks(t)], scalar=GSC[:, col:col + 1], in1=SF[0][:], op0=ALU.mult, op1=ALU.mult),
                  [("XT", c, t), "GSC", ("SF", 0)], [("HT", hb, c)])

        for c in range(NCH):
            pro_mm(c, pro_sq(0, c))
        finish_stats(0, SF[0], ("SF", 0))
        for c in range(NCH):
            pro_h(0, c)
        for t in range(4):
            hb = t % 2
            HTt = HTB[hb]
            nxt = t + 1 if t + 1 < 4 else None
            pend_sq = {}
            for grp in range(11):
                si = wload([(0, NCH, 256, wgv[:, :, grp * 256:(grp + 1) * 256]), (2048, NCH, 256, wuv[:, :, grp * 256:(grp + 1) * 256])], 3)
                Wg = WS(si)[:, 0:2048].rearrange("p (c n) -> p c n", n=256)
                Wu = WS(si)[:, 2048:4096].rearrange("p (c n) -> p c n", n=256)
                for cc in range(2):
                    chunk = grp * 2 + cc
                    bg = rot("g", [1, 2])
                    bu = rot("u", [3, 4])
                    for c in range(NCH):
                        mm(ps[bg][:], Wg[:, c, cc * 128:(cc + 1) * 128], HTt[:, c, :], c == 0, c == NCH - 1, [("w", si, 0), ("HT", hb, c)], [("ps", bg)])
                    for c in range(NCH):
                        mm(ps[bu][:], Wu[:, c, cc * 128:(cc + 1) * 128], HTt[:, c, :], c == 0, c == NCH - 1, [("w", si, 1), ("HT", hb, c)], [("ps", bu)])
                    sg = rots("tmp", [2, 3])
                    act(SF[sg][:], ps[bg][:], AF.Silu, [("ps", bg)], [("SF", sg)])
                    P.add("dve", lambda e, sg=sg, bu=bu, chunk=chunk: e.tensor_tensor(out=AT[:, chunk, :], in0=SF[sg][:], in1=ps[bu][:], op=ALU.mult),
                          [("SF", sg), ("ps", bu)], [("AT", chunk)])
                    if nxt is not None:
                        if 3 <= chunk < 11:
                            pro_mm(chunk - 3, pend_sq[chunk - 3])
                        if 2 <= chunk < 10:
                            pend_sq[chunk - 2] = pro_sq(nxt, chunk - 2)
                        if chunk == 11:
                            finish_stats(0, SF[0], ("SF", 0))
                        if 12 <= chunk < 20:
                            pro_h(nxt, chunk - 12)
            pend = None
            for grp in range(4):
                si = wload([(0, 8, 256, wdv[:, 0:8, grp * 256:(grp + 1) * 256]), (2048, 8, 256, wdv[:, 8:16, grp * 256:(grp + 1) * 256]),
                            (4096, 6, 256, wdv[:, 16:22, grp * 256:(grp + 1) * 256])], 3)
                Wd = WS(si)[:, 0:5632].rearrange("p (c n) -> p c n", n=256)
                for cc in range(2):
                    dch = grp * 2 + cc
                    bf = rot("f", [5, 6])
                    for c in range(NFF):
                        mm(ps[bf][:], Wd[:, c, cc * 128:(cc + 1) * 128], AT[:, c, :], c == 0, c == NFF - 1, [("w", si, c // 8), ("AT", c)], [("ps", bf)])
                    if pend is not None:
                        mm(ps[7][:], ONES, SB_[pend[1]][:], pend[0] == 0, False, ["CBF", ("SB", pend[1])], [("ps", 7)])
                    P.add("dve", lambda e, dch=dch, bf=bf: e.tensor_copy(out=FT[:, dch, :], in_=ps[bf][:]), [("ps", bf)], [("FT", dch)])
                    sq = rots("sq", [0, 1])
                    act(SB_[sq][:], ps[bf][:], AF.Square, [("ps", bf)], [("SB", sq)])
                    pend = (dch, sq)
            mm(ps[7][:], ONES, SB_[pend[1]][:], False, True, ["CBF", ("SB", pend[1])], [("ps", 7)])
            finish_stats(7, SF[1], ("SF", 1))
            post_residual(l, gi_post, t, 0.5, ("SF", 1))

    def attn_phase(l, s):
        for t in range(4):
            tok = slice(t * 512, (t + 1) * 512)
            norm_stats(lambda c: XT[:, c, tok], lambda c: ("XT", c, t), 0, SF[0], ("SF", 0))
            for c in range(NCH):
                col = gcol(l, 2, c)
                P.add("dve", lambda e, c=c, col=col, tok=tok: e.scalar_tensor_tensor(out=HT[:, c, tok], in0=XT[:, c, tok], scalar=GSC[:, col:col + 1], in1=SF[0][:], op0=ALU.mult, op1=ALU.mult),
                      [("XT", c, t), "GSC", ("SF", 0)], [("HTf", c, t)])
        P.add("dve", lambda e: e.memset(VA[:, :, 64:65], 1.0), [], [("VA1", 0)])
        P.add("dve", lambda e: e.memset(VA[:, :, 132:133], 1.0), [], [("VA1", 1)])
        winv = win_d[l].rearrange("(c p) n -> p c n", p=128)
        for p in range(8):
            if p < 3:
                typ, q0, k0, v0, qscale = "A", p * 128, 384 + p * 128, 768 + p * 128, 0.125
            elif p < 5:
                typ, q0, k0, v0, qscale = "B", 1152 + (p - 3) * 128, 1408 + (p - 3) * 128, 1664 + (p - 3) * 128, 32.0 ** -0.5
            else:
                typ, q0, k0, v0, qscale = "C", 1920 + (p - 5) * 128, 2304 + (p - 5) * 128, 2688 + (p - 5) * 128, 0.125
            si = wload([(0, NCH, 128, winv[:, :, q0:q0 + 128]), (1024, NCH, 128, winv[:, :, k0:k0 + 128]), (2048, NCH, 128, winv[:, :, v0:v0 + 128])])
            Wq = W[:, si, 0:1024].rearrange("p (c n) -> p c n", n=128)
            Wk = W[:, si, 1024:2048].rearrange("p (c n) -> p c n", n=128)
            Wv = W[:, si, 2048:3072].rearrange("p (c n) -> p c n", n=128)
            if typ == "A":
                dma("pool", BAND[:], band_d[l, p], [], ["BAND"], ch_band)
                ch_band.close()
            for t in range(4):
                tok = slice(t * 512, (t + 1) * 512)
                bq = rot("ip", [1, 2])
                for c in range(NCH):
                    mm(ps[bq][:], Wq[:, c, :], HT[:, c, tok], c == 0, c == NCH - 1, [("w", si, 0), ("HTf", c, t)], [("ps", bq)])
                act(QT[:, tok], ps[bq][:], AF.Copy, [("ps", bq)], [("QT", t)], scale=qscale)
                bk = rot("ip", [1, 2])
                for c in range(NCH):
                    mm(ps[bk][:], Wk[:, c, :], HT[:, c, tok], c == 0, c == NCH - 1, [("w", si, 1), ("HTf", c, t)], [("ps", bk)])
                P.add("dve", lambda e, tok=tok, bk=bk: e.tensor_copy(out=KT[:, tok], in_=ps[bk][:]), [("ps", bk)], [("KT", t)])
            for i4 in range(4):
                bv = rot("ip", [1, 2])
                for ii in range(4):
                    i = i4 * 4 + ii
                    for c in range(NCH):
                        mm(ps[bv][:, ii * 128:(ii + 1) * 128], HT[:, c, i * 128:(i + 1) * 128], Wv[:, c, :], c == 0, c == NCH - 1,
                           [("w", si, 2), ("HTf", c, i // 4)], [("ps", bv)])
                src = ps[bv][:, :].rearrange("p (i n) -> p i n", i=4)
                P.add("dve", lambda e, i4=i4, src=src: e.tensor_copy(out=VA[:, i4 * 4:i4 * 4 + 4, 0:64], in_=src[:, :, 0:64]), [("ps", bv)], [("VA", i4, 0)])
                act(VA[:, i4 * 4:i4 * 4 + 4, 68:132], src[:, :, 64:128], AF.Copy, [("ps", bv)], [("VA", i4, 1)])
            HEADS = os.environ.get("KHEADS", "ABC")
            for hh in range(2):
                if typ not in HEADS:
                    continue
                if typ == "A":
                    head_A(l, p, hh)
                elif typ == "B":
                    head_B(l, (p - 3) * 2 + hh, hh)
                else:
                    head_C(l, hh)
            for half in range(2):
                bank = rot("ip", [1, 2])
                psb = ps[bank][:, :].bitcast(BF16)
                for ii in range(8):
                    i = half * 8 + ii
                    P.add("pe", lambda e, psb=psb, ii=ii, i=i: e.transpose(psb[:, ii * 128:(ii + 1) * 128], YP[:, i, :], IDB),
                          [("YP", i // 4, 0), ("YP", i // 4, 1), "CBF"], [("ps", bank)])
                if half == 0:
                    P.add("dve", lambda e, psb=psb, p=p: e.tensor_copy(out=YT[:, p, 0:1024], in_=psb[:, 0:1024]), [("ps", bank)], [("YT", p, 0), ("YT", p, 1)])
                else:
                    act(YT[:, p, 1024:2048], psb[:, 0:1024], AF.Copy, [("ps", bank)], [("YT", p, 2), ("YT", p, 3)])
        P.barrier(engines=("pe", "act", "dve"))
        woutv = wout_d[l].rearrange("(c p) n -> p c n", p=128)
        for t in range(4):
            tok = slice(t * 512, (t + 1) * 512)
            pend = None
            for grp in range(4):
                si = wload([(0, NCH, 256, woutv[:, :, grp * 256:(grp + 1) * 256])])
                Wo = W[:, si, 0:2048].rearrange("p (c n) -> p c n", n=256)
                for cc in range(2):
                    dch = grp * 2 + cc
                    bf = rot("f", [5, 6])
                    for c in range(NCH):
                        mm(ps[bf][:], Wo[:, c, cc * 128:(cc + 1) * 128], YT[:, c, tok], c == 0, c == NCH - 1, [("w", si, 0), ("YT", c, t)], [("ps", bf)])
                    if pend is not None:
                        mm(ps[7][:], ONES, SB_[pend[1]][:], pend[0] == 0, False, ["CBF", ("SB", pend[1])], [("ps", 7)])
                    P.add("dve", lambda e, dch=dch, bf=bf: e.tensor_copy(out=FT[:, dch, :], in_=ps[bf][:]), [("ps", bf)], [("FT", dch)])
                    sq = rots("sq", [0, 1])
                    act(SB_[sq][:], ps[bf][:], AF.Square, [("ps", bf)], [("SB", sq)])
                    pend = (dch, sq)
            mm(ps[7][:], ONES, SB_[pend[1]][:], False, True, ["CBF", ("SB", pend[1])], [("ps", 7)])
            finish_stats(7, SF[1], ("SF", 1))
            post_residual(l, 3, t, 1.0, ("SF", 1))

    def pv(obank, ocol, ow, Ptile, blk, j, hh, vw, first, last):
        mm(ps[obank][:, ocol:ocol + ow], SB_[Ptile][:, blk * 128:(blk + 1) * 128], VA[:, j, hh * 68:hh * 68 + vw], first, last,
           [("SB", Ptile), ("VA", j // 4, hh), ("VA1", hh)], [("ps", obank)])

    def pipeline(tiles, stA, stB, depth=2, stC=None, depthC=1):
        n = len(tiles)
        tot = depth + (depthC if stC is not None else 0)
        for idx in range(n + tot):
            if idx < n:
                stA(tiles[idx])
            if 0 <= idx - depth < n:
                stB(tiles[idx - depth])
            if stC is not None and 0 <= idx - tot < n:
                stC(tiles[idx - tot])

    def head_A(l, p, hh):
        pr = slice(hh * 64, hh * 64 + 64)
        tiles = []
        for qb in range(4):
            i0 = 4 * qb
            js = list(range(max(i0 - 4, 0), i0 + 4))
            for j in js:
                a = max(j, i0)
                b = min(j + 4, i0 + 3)
                tiles.append(dict(qb=qb, i0=i0, j=j, a=a, b=b, lo=(a - i0) * 128, hi=(b - i0 + 1) * 128, first=(j == js[0]), last=(j == js[-1])))

        def stA(t):
            j, i0, lo, hi, a, b = t["j"], t["i0"], t["lo"], t["hi"], t["a"], t["b"]
            sbk = rot("s", [3, 4, 5, 6])
            mm(ps[sbk][:, lo:hi], KT[pr, j * 128:(j + 1) * 128], QT[pr, i0 * 128 + lo:i0 * 128 + hi], True, False,
               [("KT", j // 4), ("QT", t["qb"])], [("ps", sbk)])
            mm(ps[sbk][:, lo:hi], IDB, BAND[:, hh * 640 + (a - j) * 128:hh * 640 + (b - j + 1) * 128], False, True, ["CBF", "BAND"], [("ps", sbk)])
            Pt = rots("P", [5, 6, 7])
            act(SB_[Pt][:, lo:hi], ps[sbk][:, lo:hi], AF.Exp, [("ps", sbk)], [("SB", Pt)])
            t["Pt"] = Pt

        def stB(t):
            j, i0, a, b, qb = t["j"], t["i0"], t["a"], t["b"], t["qb"]
            ob = (7, 0)[qb % 2]
            for blk in range(a - i0, b - i0 + 1):
                pv(ob, blk * 65, 65, t["Pt"], blk, j, hh, 65, t["first"] and blk == a - i0, t["last"] and blk == b - i0)
            if t["last"]:
                O = ps[ob][:, 0:260].rearrange("p (b n) -> p b n", b=4)
                P.add("dve", lambda e, O=O: e.reciprocal(out=SM[:, 0:4], in_=O[:, :, 64]), [("ps", ob)], ["SMr"])
                P.add("dve", lambda e, O=O, i0=i0: e.tensor_tensor(out=YP[:, i0:i0 + 4, hh * 64:(hh + 1) * 64], in0=O[:, :, 0:64],
                                                                    in1=SM[:, 0:4].unsqueeze(2).to_broadcast([128, 4, 64]), op=ALU.mult),
                      [("ps", ob), "SMr"], [("YP", qb, hh)])

        pipeline(tiles, stA, stB)

    def head_B(l, hB, hh):
        tiles = []
        for qb in range(4):
            nj = 4 * qb + 4
            for m in range(2):
                for j in range(nj):
                    tiles.append(dict(qb=qb, m=m, j=j, first=(j == 0), last=(j == nj - 1)))

        def stA(t):
            qb, m, j = t["qb"], t["m"], t["j"]
            base = hh * 64 + m * 32
            pr = slice(base, base + 32)
            tp = (96, 0) if base == 96 else None
            diag = j >= 4 * qb
            lo = max(j - 4 * qb, 0) * 128
            sbk = rot("s", [3, 4, 5, 6])
            rk = [("KT", j // 4), ("QT", qb)]
            lo2 = lo
            if diag:
                mm(ps[sbk][:, lo:lo + 128], KT[pr, j * 128:(j + 1) * 128], QT[pr, qb * 512 + lo:qb * 512 + lo + 128], True, False, rk, [("ps", sbk)], tp)
                mm(ps[sbk][:, lo:lo + 128], IDB, DIAGB(hB), False, True, ["CBF"], [("ps", sbk)])
                lo2 = lo + 128
            if lo2 < 512:
                c0 = qb * 512 + lo2 - 128 * j
                mm(ps[sbk][:, lo2:512], KT[pr, j * 128:(j + 1) * 128], QT[pr, qb * 512 + lo2:qb * 512 + 512], not diag, False, rk, [("ps", sbk)], tp)
                mm(ps[sbk][:, lo2:512], BL[:, hB * 128:(hB + 1) * 128], BR[:, c0:c0 + 512 - lo2], False, True, ["BL", "BR"], [("ps", sbk)])
            Pt = rots("P", [5, 6, 7])
            act(SB_[Pt][:, lo:512], ps[sbk][:, lo:512], AF.Exp, [("ps", sbk)], [("SB", Pt)])
            t["Pt"] = Pt
            t["lo"] = lo

        def stB(t):
            qb, m, j, lo = t["qb"], t["m"], t["j"], t["lo"]
            obank = 7 if m == 0 else 0
            for blk in range(lo // 128, 4):
                pv(obank, blk * 65, 65, t["Pt"], blk, j, hh, 65, t["first"] and blk == lo // 128, t["last"] and blk == 3)
            if not (t["last"] and m == 1):
                return
            O1 = ps[7][:, 0:260].rearrange("p (b n) -> p b n", b=4)
            O2 = ps[0][:, 0:260].rearrange("p (b n) -> p b n", b=4)
            T1 = SF[0][:, 0:256].rearrange("p (b n) -> p b n", b=4)
            T2 = SF[1][:, 0:256].rearrange("p (b n) -> p b n", b=4)
            P.add("dve", lambda e, O1=O1: e.reciprocal(out=SM[:, 0:4], in_=O1[:, :, 64]), [("ps", 7)], ["SMr"])
            P.add("dve", lambda e, O2=O2: e.reciprocal(out=SM[:, 4:8], in_=O2[:, :, 64]), [("ps", 0)], ["SMr2"])
            P.add("dve", lambda e: e.tensor_scalar(out=SM[:, 8:12], in0=SM[:, 4:8], scalar1=NEGLAM[:, l:l + 1], scalar2=None, op0=ALU.mult), ["SMr2", ("NEGLAM", l)], ["SMr2n"])
            P.add("dve", lambda e, O1=O1, T1=T1: e.tensor_tensor(out=T1, in0=O1[:, :, 0:64], in1=SM[:, 0:4].unsqueeze(2).to_broadcast([128, 4, 64]), op=ALU.mult),
                  [("ps", 7), "SMr"], [("SF", 0)])
            P.add("dve", lambda e, O2=O2, T2=T2: e.tensor_tensor(out=T2, in0=O2[:, :, 0:64], in1=SM[:, 8:12].unsqueeze(2).to_broadcast([128, 4, 64]), op=ALU.mult),
                  [("ps", 0), "SMr2n"], [("SF", 1)])
            P.add("dve", lambda e, T1=T1, T2=T2: e.tensor_tensor(out=T1, in0=T1, in1=T2, op=ALU.add), [("SF", 0), ("SF", 1)], [("SF", 0)])
            act(T2, T1, AF.Square, [("SF", 0)], [("SF", 1)])
            P.add("dve", lambda e, T2=T2: e.tensor_reduce(out=SM[:, 12:16], in_=T2, axis=AX.X, op=ALU.add), [("SF", 1)], ["SMss"])
            act(SM[:, 16:20], SM[:, 12:16], AF.Ln, ["SMss"], ["SMln"], scale=1.0 / 64.0, bias=EPS)
            act(SM[:, 20:24], SM[:, 16:20], AF.Exp, ["SMln"], ["SMrstd"], scale=-0.5)
            P.add("dve", lambda e, T1=T1: e.tensor_tensor(out=T1, in0=T1, in1=SM[:, 20:24].unsqueeze(2).to_broadcast([128, 4, 64]), op=ALU.mult), [("SF", 0), "SMrstd"], [("SF", 0)])
            P.add("dve", lambda e, T1=T1, qb=qb: e.tensor_tensor(out=YP[:, 4 * qb:4 * qb + 4, hh * 64:(hh + 1) * 64], in0=T1,
                                                                 in1=GSUB[:, l * 64:(l + 1) * 64].unsqueeze(1).to_broadcast([128, 4, 64]), op=ALU.mult),
                  [("SF", 0), ("GSUB", l)], [("YP", qb, hh)])

        pipeline(tiles, stA, stB)

    def head_C(l, hh):
        pr = slice(hh * 64, hh * 64 + 64)
        ACCS = [4, 0]
        tiles = []
        for qb in range(4):
            jtop = 4 * qb + 3
            for j in range(jtop, -1, -1):
                tiles.append(dict(qb=qb, j=j, jtop=jtop, diag=(j >= 4 * qb), lo=max(j - 4 * qb, 0) * 128))
        accstate = [0]

        def stA(t):
            qb, j, lo, diag = t["qb"], t["j"], t["lo"], t["diag"]
            qc = slice(qb * 512 + lo, qb * 512 + 512)
            kc = slice(j * 128, (j + 1) * 128)
            rk = [("KT", j // 4), ("QT", qb)]
            zs = rot("zs", [3, 4, 1])
            mm(ps[zs][:, lo:512], KT[pr, kc], QT[pr, qc], True, not diag, rk, [("ps", zs)])
            if diag:
                mm(ps[zs][:, lo:lo + 128], IDB, MASKC, False, True, ["CBF"], [("ps", zs)])
            et = rots("e", [2, 3])
            act(SF[et][:, lo:512], ps[zs][:, lo:512], AF.Exp, [("ps", zs)], [("SF", et)])
            Lt = rots("L", [1, 2, 3])
            act(SB_[Lt][:, lo:512], SF[et][:, lo:512], AF.Ln, [("SF", et)], [("SB", Lt)], bias=1.0)
            t["Lt"] = Lt

        def stB(t):
            qb, j, lo, diag, jtop, Lt = t["qb"], t["j"], t["lo"], t["diag"], t["jtop"], t["Lt"]
            qc = slice(qb * 512 + lo, qb * 512 + 512)
            kc = slice(j * 128, (j + 1) * 128)
            rk = [("KT", j // 4), ("QT", qb)]
            ob = (7, 0)[qb % 2]
            if j == jtop:
                for a_ in ACCS:
                    P.add("dve", lambda e, a_=a_: e.memset(SB_[a_][:], 0.0), [], [("SB", a_)])
            cur = ACCS[accstate[0] % 2]
            nxt = ACCS[(accstate[0] + 1) % 2]
            za = rot("za", [5, 6])
            mm(ps[za][:, lo:512], KT[pr, kc], QT[pr, qc], True, False, rk, [("ps", za)])
            mm(ps[za][:, lo:512], NEGTRI, SB_[Lt][:, lo:512], False, False, ["CBF", ("SB", Lt)], [("ps", za)])
            if j < jtop:
                mm(ps[za][:, lo:512], NEGONES, SB_[cur][:, lo:512], False, not diag, ["CBF", ("SB", cur)], [("ps", za)])
            if diag:
                mm(ps[za][:, lo:lo + 128], IDB, MASKC, False, True, ["CBF"], [("ps", za)])
            Pt = rots("P", [5, 6, 7])
            act(SB_[Pt][:, lo:512], ps[za][:, lo:512], AF.Exp, [("ps", za)], [("SB", Pt)])
            if j > 0:
                P.add("dve", lambda e, cur=cur, nxt=nxt: e.tensor_tensor(out=SB_[nxt][:, lo:512], in0=SB_[cur][:, lo:512], in1=SB_[Lt][:, lo:512], op=ALU.add),
                      [("SB", cur), ("SB", Lt)], [("SB", nxt)])
                accstate[0] += 1
            t["Pt"] = Pt

        def stC(t):
            qb, j, lo, jtop, Pt = t["qb"], t["j"], t["lo"], t["jtop"], t["Pt"]
            ob = (7, 0)[qb % 2]
            for blk in range(lo // 128, 4):
                pv(ob, blk * 64, 64, Pt, blk, j, hh, 64, (j == jtop) and blk == lo // 128, (j == 0) and (blk == 3))
            if j == 0:
                O = ps[ob][:, 0:256].rearrange("p (b n) -> p b n", b=4)
                P.add("dve", lambda e, O=O: e.tensor_copy(out=YP[:, 4 * qb:4 * qb + 4, hh * 64:(hh + 1) * 64], in_=O), [("ps", ob)], [("YP", qb, hh)])

        pipeline(tiles, stA, stB, 2, stC, 1)


    last_stores = []
    for s in range(NS):
        P.barrier()
        if STAGE >= 2:
            load_x(s)
        for l in range(L):
            if do_ffn:
                P.barrier(engines=("pe", "act", "dve"))
                ffn_phase(l, 0)
            if do_attn:
                P.barrier(engines=("pe", "act", "dve"))
                attn_phase(l, s)
            if do_ffn:
                P.barrier(engines=("pe", "act", "dve"))
                ffn_phase(l, 1)
        P.barrier(engines=("pe", "act", "dve"))
        if STAGE >= 3:
            last_stores = store_x(s)
    fin = Op("sp", None, False, None)
    for o in P.alias_dmas:
        fin.deps.add(o)
    P.ops.append(fin)
    P.emit(nc, es)
    es.close()
    return nc


def host_consts(L, rel_bias):
    k = np.arange(128)[:, None]
    q = np.arange(128)[None, :]
    idf = np.eye(128, dtype=np.float32)
    cbf = np.zeros((128, 1152), np.float32)
    cbf[:, 0:128] = idf
    cbf[:, 128:256] = 1.0
    cbf[:, 256:384] = -(k >= q).astype(np.float32)
    cbf[:, 384:512] = -1.0
    cbf[:, 512:640] = np.where(k < q, 0.0, NEG)
    slopes = [2.0 ** (-8.0 * (h + 1) / 4.0) for h in range(4)]
    for h in range(4):
        t = -slopes[h] * np.abs(q - k).astype(np.float32)
        t = np.where((k >= 64) & (q < 64), NEG, t)
        cbf[:, 640 + h * 128:640 + (h + 1) * 128] = t
    c = np.arange(2048)
    br = np.zeros((128, 2048), np.float32)
    br[0:3] = np.stack([(c // 128) * 128, c % 128, np.ones_like(c)]).astype(np.float32)
    bl = np.zeros((128, 512), np.float32)
    for h in range(4):
        bl[0, h * 128:(h + 1) * 128] = -slopes[h]
        bl[1, h * 128:(h + 1) * 128] = -slopes[h]
        bl[2, h * 128:(h + 1) * 128] = slopes[h] * np.arange(128)
    cc = np.arange(640)[None, :]
    r = cc // 128
    qq = cc % 128
    kk = np.arange(128)[:, None]
    rel = 128 * r + qq - kk
    idx = np.clip(rel, -128, 128) + 128
    masked = ((r == 0) & (kk >= 64) & (qq < 64)) | ((r == 4) & (kk < 64) & (qq >= 64))
    band = np.zeros((L, 3, 128, 1280), np.float32)
    for l in range(L):
        for p in range(3):
            for hh in range(2):
                g = rel_bias[l, 2 * p + hh][idx]
                band[l, p, :, hh * 640:(hh + 1) * 640] = np.where(masked, np.float32(NEG), g)
    return idf, cbf, br, bl, band


_CACHE = {}
NS_PER_LAUNCH = 4


def kernel(**inputs):
    L = 2
    NS = 4
    x = np.ascontiguousarray(inputs["x"], dtype=np.float32)
    gl = [inputs[k] for k in ("ffn1_pre_g", "ffn1_post_g", "mix_pre_g", "mix_post_g", "ffn2_pre_g", "ffn2_post_g")]
    gains = np.stack([np.asarray(g, np.float32) for g in gl], axis=1)
    gains = np.ascontiguousarray(gains.reshape(L, 6, NCH, 128).transpose(3, 0, 1, 2).reshape(128, L * 6 * NCH))
    idf, cbf, br, bl, band = host_consts(L, np.asarray(inputs["rel_bias"], np.float32))
    lamv = np.concatenate([np.asarray(inputs[k], np.float32) for k in ("diff_lambda_q1", "diff_lambda_k1", "diff_lambda_q2", "diff_lambda_k2")], axis=1)
    lamv = np.ascontiguousarray(lamv.reshape(1, L * 128))
    subln = np.ascontiguousarray(np.asarray(inputs["diff_subln_g"], np.float32).reshape(1, L * 64))
    common = {
        "wg1": inputs["ffn1_w_gate"], "wu1": inputs["ffn1_w_up"], "wd1": inputs["ffn1_w_down"],
        "wg2": inputs["ffn2_w_gate"], "wu2": inputs["ffn2_w_up"], "wd2": inputs["ffn2_w_down"],
        "w_in": inputs["w_in"], "w_out": inputs["w_out"], "gains": gains, "band": band,
        "idf": idf, "cbf": cbf, "br": br, "bl": bl, "lamv": lamv, "subln": subln,
    }
    common = {k: np.ascontiguousarray(np.asarray(v, np.float32)) for k, v in common.items()}
    NS_L = NS_PER_LAUNCH
    if "nc" not in _CACHE:
        _CACHE["nc"] = build(NS_L, L)
    nc = _CACHE["nc"]
    out = np.empty_like(x)
    for k in range(NS // NS_L):
        in_maps = []
        for c in range(N_CORES):
            m = dict(common)
            m["x"] = x[c * NS + k * NS_L:c * NS + (k + 1) * NS_L]
            in_maps.append(m)
        res = run_bass_kernel_spmd(nc, in_maps, core_ids=list(range(N_CORES)))
        for c in range(N_CORES):
            out[c * NS + k * NS_L:c * NS + (k + 1) * NS_L] = res.results[c]["out"]
    return out
```

```python
import math
from contextlib import ExitStack

import numpy as np
import concourse.bass as bass
import concourse.mybir as mybir
from concourse.bass_utils import run_bass_kernel_spmd

F32 = mybir.dt.float32
BF16 = mybir.dt.bfloat16
AF = mybir.ActivationFunctionType
ALU = mybir.AluOpType
AX = mybir.AxisListType

D = 1024
S = 2048
DFF = 2816
NCH = 8
NFF = 22
INC = 3072
EPS = 1e-6
NEG = -30000.0
N_CORES = 8
ENGNAME = {"pe": "tensor", "act": "scalar", "dve": "vector", "pool": "gpsimd", "sp": "sync"}


class Chan:
    def __init__(self, name):
        self.name = name
        self.count = 0
        self.sem = None
        self.cur = []

    def close(self):
        if self.cur:
            last = self.cur[-1]
            for o in self.cur:
                o.group_last = last
        self.cur = []


class Op:
    __slots__ = ("eng", "fn", "deps", "dma", "chan", "signal", "val", "sem", "group_last")

    def __init__(self, eng, fn, dma, chan):
        self.eng = eng
        self.fn = fn
        self.deps = set()
        self.dma = dma
        self.chan = chan
        self.signal = False
        self.val = 0
        self.sem = None
        self.group_last = None


class Prog:
    def __init__(self):
        self.ops = []
        self.lastw = {}
        self.rd_eng = {}
        self.rd_dma = {}
        self.last_op = {}
        self.alias_dmas = []
        self.chans = []

    def chan(self, name):
        c = Chan(name)
        self.chans.append(c)
        return c

    def _dep(self, op, p, kind):
        if p is op:
            return
        if p.fn is None and not p.dma:
            assert p.eng == op.eng
            return
        if (not p.dma) and (not op.dma) and p.eng == op.eng:
            if op.eng == "pe":
                return
            if kind != "raw":
                return
        op.deps.add(p)

    def add(self, eng, fn, reads=(), writes=(), dma=False, chan=None, alias=False):
        op = Op(eng, fn, dma, chan)
        for k in reads:
            w = self.lastw.get(k)
            if w is not None:
                self._dep(op, w, "raw")
            if isinstance(k, tuple) and k[0] == "ps":
                for r in self.rd_eng.get(k, {}).values():
                    if r.eng != eng:
                        self._dep(op, r, "raw")
        for k in writes:
            w = self.lastw.get(k)
            if w is not None:
                self._dep(op, w, "waw")
            for r in self.rd_eng.get(k, {}).values():
                self._dep(op, r, "war")
            for r in self.rd_dma.get(k, ()):
                self._dep(op, r, "war")
        for k in reads:
            if dma:
                self.rd_dma.setdefault(k, []).append(op)
            else:
                self.rd_eng.setdefault(k, {})[eng] = op
        for k in writes:
            self.lastw[k] = op
            self.rd_eng[k] = {}
            self.rd_dma[k] = []
        if dma:
            chan.cur.append(op)
            if alias:
                self.alias_dmas.append(op)
        elif fn is not None:
            self.last_op[eng] = op
        self.ops.append(op)
        return op

    def barrier(self, engines=("pe", "act", "dve", "pool", "sp")):
        lasts = [o for o in self.last_op.values()]
        al = list(self.alias_dmas)
        self.alias_dmas = []
        for e in engines:
            op = Op(e, None, False, None)
            for o in lasts:
                if o.eng != e:
                    op.deps.add(o)
            for o in al:
                op.deps.add(o)
            self.ops.append(op)

    def emit(self, nc, es):
        for c in self.chans:
            c.close()
        for op in self.ops:
            for d in op.deps:
                d.signal = True
        sems = {e: es.enter_context(nc.semaphore("s_" + e)) for e in ENGNAME}
        for c in self.chans:
            c.sem = es.enter_context(nc.semaphore("c_" + c.name))
            c.count = 0
        counts = {e: 0 for e in ENGNAME}
        for op in self.ops:
            if op.dma:
                op.chan.count += 16
                op.val = op.chan.count
                op.sem = op.chan.sem
            elif op.signal:
                counts[op.eng] += 1
                op.val = counts[op.eng]
                op.sem = sems[op.eng]
        block = es.enter_context(nc.Block())
        for e in ENGNAME:
            myops = [op for op in self.ops if op.eng == e]

            def body(eng, myops=myops):
                waited = {}
                for op in myops:
                    need = {}
                    for d in op.deps:
                        if d.dma:
                            g = d.group_last if d.group_last is not None else d
                            sem, val = g.sem, g.val
                        else:
                            sem, val = d.sem, d.val
                        k = id(sem)
                        if k not in need or need[k][1] < val:
                            need[k] = (sem, val)
                    for k, (sem, val) in need.items():
                        if waited.get(k, 0) < val:
                            eng.wait_ge(sem, val)
                            waited[k] = val
                    if op.fn is not None:
                        ins = op.fn(eng)
                        if op.dma:
                            ins.then_inc(op.sem, 16)
                        elif op.signal:
                            ins.then_inc(op.sem, 1)

            getattr(block, ENGNAME[e])(body)


def build(NS, L, do_ffn=True, do_attn=True):
    import os
    STAGE = int(os.environ.get("KSTAGE", "9"))
    nc = bass.Bass("TRN2", target_bir_lowering=False)
    es = ExitStack()
    P = Prog()

    def din(name, shape):
        return nc.dram_tensor(name, list(shape), F32, kind="ExternalInput").ap()

    x_d = din("x", [NS, S, D])
    out_d = nc.dram_tensor("out", [NS, S, D], F32, kind="ExternalOutput").ap()
    wg_d = [din("wg1", [L, D, DFF]), din("wg2", [L, D, DFF])]
    wu_d = [din("wu1", [L, D, DFF]), din("wu2", [L, D, DFF])]
    wd_d = [din("wd1", [L, DFF, D]), din("wd2", [L, DFF, D])]
    win_d = din("w_in", [L, D, INC])
    wout_d = din("w_out", [L, D, D])
    gains_d = din("gains", [128, L * 6 * NCH])
    band_d = din("band", [L, 3, 128, 1280])
    idf_d = din("idf", [128, 128])
    cbf_d = din("cbf", [128, 1152])
    br_d = din("br", [128, 2048])
    bl_d = din("bl", [128, 512])
    lamv_d = din("lamv", [1, L * 128])
    subln_d = din("subln", [1, L * 64])

    def sb(name, shape, dt):
        return es.enter_context(nc.sbuf_tensor(name, list(shape), dt))

    XT = sb("XT", [128, NCH, S], F32)
    HTFLAT = sb("HT", [128, NCH * S], BF16)
    HT = HTFLAT[:, :].rearrange("p (c t) -> p c t", c=NCH)
    HTB = [HTFLAT[:, i * 4096:(i + 1) * 4096].rearrange("p (c t) -> p c t", c=NCH) for i in range(2)]
    R1 = sb("R1", [128, 16384], BF16)
    R2 = sb("R2", [128, 5248], F32)
    W = sb("W", [128, 2, 5632], BF16)
    BAND = sb("BAND", [128, 1280], BF16)
    IDF = sb("IDF", [128, 128], F32)
    CBF = sb("CBF", [128, 1152], BF16)
    BR = sb("BR", [128, 2048], BF16)
    BL = sb("BL", [128, 512], BF16)
    GAINS = sb("GAINS", [128, L * 6 * NCH], F32)
    GSC = sb("GSC", [128, L * 6 * NCH], F32)
    LAMV = sb("LAMV", [128, L * 128], F32)
    SUBLN = sb("SUBLN", [128, L * 64], F32)
    GSUB = sb("GSUB", [128, L * 64], F32)
    NEGLAM = sb("NEGLAM", [128, L], F32)
    LTMP = sb("LTMP", [128, 64], F32)
    LS = sb("LS", [128, 4], F32)
    SF = [sb(f"SF{i}", [128, 512], F32) for i in range(4)]
    SB_ = [sb(f"SB{i}", [128, 512], BF16) for i in range(8)]
    SM = sb("SM", [128, 64], F32)
    EP = [sb(f"EP{i}", [128, 512], F32) for i in range(2)]
    ps = [es.enter_context(nc.psum_tensor(f"ps{i}", [128, 512], F32)) for i in range(8)]

    IDB = CBF[:, 0:128]
    ONES = CBF[:, 128:256]
    NEGTRI = CBF[:, 256:384]
    NEGONES = CBF[:, 384:512]
    MASKC = CBF[:, 512:640]

    def DIAGB(h):
        return CBF[:, 640 + h * 128:640 + (h + 1) * 128]

    YT = R1[:, :].rearrange("p (c t) -> p c t", c=NCH)
    AT = R1[:, 0:NFF * 512].rearrange("p (c t) -> p c t", c=NFF)
    FT = R2[:, 0:4096].rearrange("p (c t) -> p c t", c=NCH)
    FTB = [FT, HTFLAT[:, 0:8192].bitcast(F32).rearrange("p (c t) -> p c t", c=NCH)]
    XIO = [R2[:, i * 1024:(i + 1) * 1024] for i in range(4)]
    R2B = R2[:, :].bitcast(BF16)
    QT = R2B[:, 0:2048]
    KT = R2B[:, 2048:4096]
    VA = R2B[:, 4096:4096 + 2176].rearrange("p (i n) -> p i n", i=16)
    YP = R2B[:, 6272:6272 + 2048].rearrange("p (i n) -> p i n", i=16)

    ch_const = P.chan("const")
    ch_w = [P.chan("w0"), P.chan("w1"), P.chan("w2")]
    ch_xin = [P.chan("xin0"), P.chan("xin1")]
    ch_xout = [P.chan("xout0"), P.chan("xout1")]
    ch_band = P.chan("band")

    def mm(out, lhsT, rhs, start, stop, reads, writes, tp=None):
        kw = {}
        if tp is not None:
            kw["tile_position"] = tp
        P.add("pe", lambda e: e.matmul(out, lhsT=lhsT, rhs=rhs, start=start, stop=stop, **kw), reads, writes)

    def act(out, in_, func, reads, writes, **kw):
        P.add("act", lambda e: e.activation(out=out, in_=in_, func=func, **kw), reads, writes)

    def dma(eng, out, in_, reads, writes, chan, alias=False):
        P.add(eng, lambda e: e.dma_start(out=out, in_=in_), reads, writes, dma=True, chan=chan, alias=alias)

    wslot_ctr = [0]

    def WS(si):
        return W[:, si, :] if si < 2 else HTFLAT[:, 8192:8192 + 5632]

    def wload(parts, nslots=2):
        si = wslot_ctr[0] % nslots
        wslot_ctr[0] += 1
        P.add("pool", None, [], [("w", si, k) for k in range(3)])
        for idx, (col0, nch, n, src) in enumerate(parts):
            dst = WS(si)[:, col0:col0 + nch * n].rearrange("p (c n) -> p c n", n=n)
            wk = [("w", si, idx)]
            if si == 2 and idx == 0:
                wk += [("HTf", c, t) for c in range(4, 7) for t in range(4)]
            dma("pool", dst, src, [], wk, ch_w[si])
        ch_w[si].close()
        return si

    psrot = {}

    def rot(name, banks):
        i = psrot.get(name, 0)
        psrot[name] = i + 1
        return banks[i % len(banks)]

    sfrot = {}

    def rots(name, items):
        i = sfrot.get(name, 0)
        sfrot[name] = i + 1
        return items[i % len(items)]

    dma("sp", IDF[:], idf_d, [], ["IDF"], ch_const)
    dma("pool", CBF[:], cbf_d, [], ["CBF"], ch_const)
    dma("pool", BR[:], br_d, [], ["BR"], ch_const)
    dma("pool", BL[:], bl_d, [], ["BL"], ch_const)
    dma("sp", GAINS[:], gains_d, [], ["GAINS"], ch_const)
    dma("sp", LAMV[:], lamv_d.partition_broadcast(128), [], ["LAMV"], ch_const)
    dma("sp", SUBLN[:], subln_d.partition_broadcast(128), [], ["SUBLN"], ch_const)
    ch_const.close()
    P.add("dve", lambda e: e.tensor_scalar(out=GSC[:], in0=GAINS[:], scalar1=32.0, scalar2=None, op0=ALU.mult), ["GAINS"], ["GSC"])
    for l in range(L):
        lam_init = 0.8 - 0.6 * math.exp(-0.3 * l)
        b = l * 128
        P.add("dve", lambda e, b=b: e.tensor_tensor(out=LTMP[:, 0:32], in0=LAMV[:, b:b + 32], in1=LAMV[:, b + 32:b + 64], op=ALU.mult), ["LAMV"], ["LTMP0"])
        P.add("dve", lambda e, b=b: e.tensor_tensor(out=LTMP[:, 32:64], in0=LAMV[:, b + 64:b + 96], in1=LAMV[:, b + 96:b + 128], op=ALU.mult), ["LAMV"], ["LTMP1"])
        P.add("dve", lambda e: e.tensor_reduce(out=LS[:, 0:2], in_=LTMP[:, :].rearrange("p (a b) -> p a b", a=2), axis=AX.X, op=ALU.add), ["LTMP0", "LTMP1"], ["LS01"])
        act(LS[:, 2:4], LS[:, 0:2], AF.Exp, ["LS01"], ["LS23"])
        P.add("dve", lambda e: e.tensor_tensor(out=LS[:, 0:1], in0=LS[:, 3:4], in1=LS[:, 2:3], op=ALU.subtract), ["LS23"], ["LS01"])
        P.add("dve", lambda e, l=l, li=lam_init: e.tensor_scalar(out=NEGLAM[:, l:l + 1], in0=LS[:, 0:1], scalar1=-li, scalar2=None, op0=ALU.add), ["LS01"], [("NEGLAM", l)])
        P.add("dve", lambda e, l=l, li=lam_init: e.tensor_scalar(out=GSUB[:, l * 64:(l + 1) * 64], in0=SUBLN[:, l * 64:(l + 1) * 64], scalar1=1.0 - li, scalar2=None, op0=ALU.mult), ["SUBLN"], [("GSUB", l)])

    def gcol(l, gi, c):
        return (l * 6 + gi) * NCH + c

    def norm_stats(src_fn, src_keys, ss_bank, rs_tile, rs_key):
        for c in range(NCH):
            sq = rots("sq", [0, 1])
            act(SB_[sq][:], src_fn(c), AF.Square, [src_keys(c)], [("SB", sq)])
            mm(ps[ss_bank][:], ONES, SB_[sq][:], c == 0, c == NCH - 1, ["CBF", ("SB", sq)], [("ps", ss_bank)])
        finish_stats(ss_bank, rs_tile, rs_key)

    def finish_stats(ss_bank, rs_tile, rs_key):
        act(rs_tile[:], ps[ss_bank][:], AF.Ln, [("ps", ss_bank)], [rs_key], bias=1024.0 * EPS)
        act(rs_tile[:], rs_tile[:], AF.Exp, [rs_key], [rs_key], scale=-0.5)

    def load_x(s):
        for i in range(16):
            sl = i % 2
            dma("sp", XIO[sl], x_d[s, i * 128:(i + 1) * 128, :], [], [("xio", sl)], ch_xin[sl], alias=True)
            ch_xin[sl].close()
            for hb in range(2):
                bank = rot("xl", [1, 2, 3, 4])
                for cc in range(4):
                    c = hb * 4 + cc
                    P.add("pe", lambda e, bank=bank, cc=cc, c=c, sl=sl: e.transpose(ps[bank][:, cc * 128:(cc + 1) * 128], XIO[sl][:, c * 128:(c + 1) * 128], IDF[:]),
                          [("xio", sl), "IDF"], [("ps", bank)])
                dst = XT[:, hb * 4:hb * 4 + 4, i * 128:(i + 1) * 128]
                src = ps[bank][:, :].rearrange("p (c t) -> p c t", c=4)
                wk = [("XT", c, i // 4) for c in range(hb * 4, hb * 4 + 4)]
                if hb == 0:
                    P.add("dve", lambda e, dst=dst, src=src: e.tensor_copy(out=dst, in_=src), [("ps", bank)], wk)
                else:
                    P.add("act", lambda e, dst=dst, src=src: e.activation(out=dst, in_=src, func=AF.Copy), [("ps", bank)], wk)

    def store_x(s):
        last = []
        for i in range(16):
            sl = 2 + (i % 2)
            for hb in range(2):
                bank = rot("xl", [1, 2, 3, 4])
                for cc in range(4):
                    c = hb * 4 + cc
                    P.add("pe", lambda e, bank=bank, cc=cc, c=c, i=i: e.transpose(ps[bank][:, cc * 128:(cc + 1) * 128], XT[:, c, i * 128:(i + 1) * 128], IDF[:]),
                          [("XT", c, i // 4), "IDF"], [("ps", bank)])
                dst = XIO[sl][:, hb * 512:(hb + 1) * 512]
                if hb == 0:
                    P.add("dve", lambda e, dst=dst, bank=bank: e.tensor_copy(out=dst, in_=ps[bank][:]), [("ps", bank)], [("xio", sl, hb)])
                else:
                    P.add("act", lambda e, dst=dst, bank=bank: e.activation(out=dst, in_=ps[bank][:], func=AF.Copy), [("ps", bank)], [("xio", sl, hb)])
            if STAGE >= 4:
                dma(os.environ.get("KSTQ", "sp"), out_d[s, i * 128:(i + 1) * 128, :], XIO[sl], [("xio", sl, 0), ("xio", sl, 1)], [], ch_xout[i % 2], alias=True)
                ch_xout[i % 2].close()
                last.append(P.ops[-1])
        return last

    def post_piece(l, gi, t, factor, rs_key, c, fb=0):
        rs2 = SF[1]
        tmp = rots("ep", [0, 1])
        col = gcol(l, gi, c)
        P.add("dve", lambda e: e.scalar_tensor_tensor(out=EP[tmp][:], in0=FTB[fb][:, c, :], scalar=GSC[:, col:col + 1], in1=rs2[:], op0=ALU.mult, op1=ALU.mult),
              [("FT", fb, c), "GSC", rs_key], [("EP", tmp)])
        xs = XT[:, c, t * 512:(t + 1) * 512]
        P.add("dve", lambda e: e.scalar_tensor_tensor(out=xs, in0=EP[tmp][:], scalar=factor, in1=xs, op0=ALU.mult, op1=ALU.add),
              [("EP", tmp), ("XT", c, t)], [("XT", c, t)])

    def post_residual(l, gi, t, factor, rs_key, fb=0):
        for c in range(NCH):
            post_piece(l, gi, t, factor, rs_key, c, fb)

    def ffn_phase(l, which):
        gi_pre, gi_post = (0, 1) if which == 0 else (4, 5)
        wgv = wg_d[which][l].rearrange("(c p) n -> p c n", p=128)
        wuv = wu_d[which][l].rearrange("(c p) n -> p c n", p=128)
        wdv = wd_d[which][l].rearrange("(c p) n -> p c n", p=128)

        def toks(t):
            return slice(t * 512, (t + 1) * 512)

        def pro_sq(t, c):
            sq = rots("sq", [0, 1])
            act(SB_[sq][:], XT[:, c, toks(t)], AF.Square, [("XT", c, t)], [("SB", sq)])
            return sq

        def pro_mm(c, sq):
            mm(ps[0][:], ONES, SB_[sq][:], c == 0, c == NCH - 1, ["CBF", ("SB", sq)], [("ps", 0)])

        def pro_h(t, c):
            col = gcol(l, gi_pre, c)
            hb = t % 2
            P.add("dve", lambda e: e.scalar_tensor_tensor(out=HTB[hb][:, c, :], in0=XT[:, c, toks(t)], scalar=GSC[:, col:col + 1], in1=SF[0][:], op0=ALU.mult, op1=ALU.mult),
                  [("XT", c, t), "GSC", ("SF", 0)], [("HT", hb, c)])

        for c in range(NCH):
            pro_mm(c, pro_sq(0, c))
        finish_stats(0, SF[0], ("SF", 0))
        for c in range(NCH):
            pro_h(0, c)
        for t in range(4):
            hb = t % 2
            HTt = HTB[hb]
            nxt = t + 1 if t + 1 < 4 else None
            pend_sq = {}
            for grp in range(11):
                si = wload([(0, NCH, 256, wgv[:, :, grp * 256:(grp + 1) * 256]), (2048, NCH, 256, wuv[:, :, grp * 256:(grp + 1) * 256])], 3)
                Wg = WS(si)[:, 0:2048].rearrange("p (c n) -> p c n", n=256)
                Wu = WS(si)[:, 2048:4096].rearrange("p (c n) -> p c n", n=256)
                for cc in range(2):
                    chunk = grp * 2 + cc
                    bg = rot("g", [1, 2])
                    bu = rot("u", [3, 4])
                    for c in range(NCH):
                        mm(ps[bg][:], Wg[:, c, cc * 128:(cc + 1) * 128], HTt[:, c, :], c == 0, c == NCH - 1, [("w", si, 0), ("HT", hb, c)], [("ps", bg)])
                    for c in range(NCH):
                        mm(ps[bu][:], Wu[:, c, cc * 128:(cc + 1) * 128], HTt[:, c, :], c == 0, c == NCH - 1, [("w", si, 1), ("HT", hb, c)], [("ps", bu)])
                    sg = rots("tmp", [2, 3])
                    act(SF[sg][:], ps[bg][:], AF.Silu, [("ps", bg)], [("SF", sg)])
                    P.add("dve", lambda e, sg=sg, bu=bu, chunk=chunk: e.tensor_tensor(out=AT[:, chunk, :], in0=SF[sg][:], in1=ps[bu][:], op=ALU.mult),
                          [("SF", sg), ("ps", bu)], [("AT", chunk)])
                    if nxt is not None:
                        if 3 <= chunk < 11:
                            pro_mm(chunk - 3, pend_sq[chunk - 3])
                        if 2 <= chunk < 10:
                            pend_sq[chunk - 2] = pro_sq(nxt, chunk - 2)
                        if chunk == 11:
                            finish_stats(0, SF[0], ("SF", 0))
                        if 12 <= chunk < 20:
                            pro_h(nxt, chunk - 12)
                    if t > 0 and chunk < NCH:
                        post_piece(l, gi_post, t - 1, 0.5, ("SF", 1), chunk)
            pend = None
            for grp in range(4):
                si = wload([(0, 8, 256, wdv[:, 0:8, grp * 256:(grp + 1) * 256]), (2048, 8, 256, wdv[:, 8:16, grp * 256:(grp + 1) * 256]),
                            (4096, 6, 256, wdv[:, 16:22, grp * 256:(grp + 1) * 256])], 3)
                Wd = WS(si)[:, 0:5632].rearrange("p (c n) -> p c n", n=256)
                for cc in range(2):
                    dch = grp * 2 + cc
                    bf = rot("f", [5, 6])
                    for c in range(NFF):
                        mm(ps[bf][:], Wd[:, c, cc * 128:(cc + 1) * 128], AT[:, c, :], c == 0, c == NFF - 1, [("w", si, c // 8), ("AT", c)], [("ps", bf)])
                    if pend is not None:
                        mm(ps[7][:], ONES, SB_[pend[1]][:], pend[0] == 0, False, ["CBF", ("SB", pend[1])], [("ps", 7)])
                    P.add("dve", lambda e, dch=dch, bf=bf: e.tensor_copy(out=FT[:, dch, :], in_=ps[bf][:]), [("ps", bf)], [("FT", 0, dch)])
                    sq = rots("sq", [0, 1])
                    act(SB_[sq][:], ps[bf][:], AF.Square, [("ps", bf)], [("SB", sq)])
                    pend = (dch, sq)
            mm(ps[7][:], ONES, SB_[pend[1]][:], False, True, ["CBF", ("SB", pend[1])], [("ps", 7)])
            finish_stats(7, SF[1], ("SF", 1))
            if t == 3:
                post_residual(l, gi_post, t, 0.5, ("SF", 1))

    def attn_phase(l, s):
        for t in range(4):
            tok = slice(t * 512, (t + 1) * 512)
            norm_stats(lambda c: XT[:, c, tok], lambda c: ("XT", c, t), 0, SF[0], ("SF", 0))
            for c in range(NCH):
                col = gcol(l, 2, c)
                P.add("dve", lambda e, c=c, col=col, tok=tok: e.scalar_tensor_tensor(out=HT[:, c, tok], in0=XT[:, c, tok], scalar=GSC[:, col:col + 1], in1=SF[0][:], op0=ALU.mult, op1=ALU.mult),
                      [("XT", c, t), "GSC", ("SF", 0)], [("HTf", c, t)])
        P.add("dve", lambda e: e.memset(VA[:, :, 64:65], 1.0), [], [("VA1", 0)])
        P.add("dve", lambda e: e.memset(VA[:, :, 132:133], 1.0), [], [("VA1", 1)])
        winv = win_d[l].rearrange("(c p) n -> p c n", p=128)
        for p in range(8):
            if p < 3:
                typ, q0, k0, v0, qscale = "A", p * 128, 384 + p * 128, 768 + p * 128, 0.125
            elif p < 5:
                typ, q0, k0, v0, qscale = "B", 1152 + (p - 3) * 128, 1408 + (p - 3) * 128, 1664 + (p - 3) * 128, 32.0 ** -0.5
            else:
                typ, q0, k0, v0, qscale = "C", 1920 + (p - 5) * 128, 2304 + (p - 5) * 128, 2688 + (p - 5) * 128, 0.125
            si = wload([(0, NCH, 128, winv[:, :, q0:q0 + 128]), (1024, NCH, 128, winv[:, :, k0:k0 + 128]), (2048, NCH, 128, winv[:, :, v0:v0 + 128])])
            Wq = W[:, si, 0:1024].rearrange("p (c n) -> p c n", n=128)
            Wk = W[:, si, 1024:2048].rearrange("p (c n) -> p c n", n=128)
            Wv = W[:, si, 2048:3072].rearrange("p (c n) -> p c n", n=128)
            if typ == "A":
                dma("pool", BAND[:], band_d[l, p], [], ["BAND"], ch_band)
                ch_band.close()
            for t in range(4):
                tok = slice(t * 512, (t + 1) * 512)
                bq = rot("ip", [1, 2])
                for c in range(NCH):
                    mm(ps[bq][:], Wq[:, c, :], HT[:, c, tok], c == 0, c == NCH - 1, [("w", si, 0), ("HTf", c, t)], [("ps", bq)])
                act(QT[:, tok], ps[bq][:], AF.Copy, [("ps", bq)], [("QT", t)], scale=qscale)
                bk = rot("ip", [1, 2])
                for c in range(NCH):
                    mm(ps[bk][:], Wk[:, c, :], HT[:, c, tok], c == 0, c == NCH - 1, [("w", si, 1), ("HTf", c, t)], [("ps", bk)])
                P.add("dve", lambda e, tok=tok, bk=bk: e.tensor_copy(out=KT[:, tok], in_=ps[bk][:]), [("ps", bk)], [("KT", t)])
            for i4 in range(4):
                bv = rot("ip", [1, 2])
                for ii in range(4):
                    i = i4 * 4 + ii
                    for c in range(NCH):
                        mm(ps[bv][:, ii * 128:(ii + 1) * 128], HT[:, c, i * 128:(i + 1) * 128], Wv[:, c, :], c == 0, c == NCH - 1,
                           [("w", si, 2), ("HTf", c, i // 4)], [("ps", bv)])
                src = ps[bv][:, :].rearrange("p (i n) -> p i n", i=4)
                P.add("dve", lambda e, i4=i4, src=src: e.tensor_copy(out=VA[:, i4 * 4:i4 * 4 + 4, 0:64], in_=src[:, :, 0:64]), [("ps", bv)], [("VA", i4, 0)])
                act(VA[:, i4 * 4:i4 * 4 + 4, 68:132], src[:, :, 64:128], AF.Copy, [("ps", bv)], [("VA", i4, 1)])
            HEADS = os.environ.get("KHEADS", "ABC")
            for hh in range(2):
                if typ not in HEADS:
                    continue
                if typ == "A":
                    head_A(l, p, hh)
                elif typ == "B":
                    head_B(l, (p - 3) * 2 + hh, hh)
                else:
                    head_C(l, hh)
            for half in range(2):
                bank = rot("ip", [1, 2])
                psb = ps[bank][:, :].bitcast(BF16)
                for ii in range(8):
                    i = half * 8 + ii
                    P.add("pe", lambda e, psb=psb, ii=ii, i=i: e.transpose(psb[:, ii * 128:(ii + 1) * 128], YP[:, i, :], IDB),
                          [("YP", i // 4, 0), ("YP", i // 4, 1), "CBF"], [("ps", bank)])
                if half == 0:
                    P.add("dve", lambda e, psb=psb, p=p: e.tensor_copy(out=YT[:, p, 0:1024], in_=psb[:, 0:1024]), [("ps", bank)], [("YT", p, 0), ("YT", p, 1)])
                else:
                    act(YT[:, p, 1024:2048], psb[:, 0:1024], AF.Copy, [("ps", bank)], [("YT", p, 2), ("YT", p, 3)])
        P.barrier(engines=("pe", "act", "dve"))
        woutv = wout_d[l].rearrange("(c p) n -> p c n", p=128)
        for t in range(4):
            tok = slice(t * 512, (t + 1) * 512)
            pend = None
            fb = t % 2
            for grp in range(4):
                si = wload([(0, NCH, 256, woutv[:, :, grp * 256:(grp + 1) * 256])])
                Wo = W[:, si, 0:2048].rearrange("p (c n) -> p c n", n=256)
                for cc in range(2):
                    dch = grp * 2 + cc
                    bf = rot("f", [5, 6])
                    for c in range(NCH):
                        mm(ps[bf][:], Wo[:, c, cc * 128:(cc + 1) * 128], YT[:, c, tok], c == 0, c == NCH - 1, [("w", si, 0), ("YT", c, t)], [("ps", bf)])
                    if pend is not None:
                        mm(ps[7][:], ONES, SB_[pend[1]][:], pend[0] == 0, False, ["CBF", ("SB", pend[1])], [("ps", 7)])
                    P.add("dve", lambda e, dch=dch, bf=bf, fb=fb: e.tensor_copy(out=FTB[fb][:, dch, :], in_=ps[bf][:]), [("ps", bf)], [("FT", fb, dch)])
                    sq = rots("sq", [0, 1])
                    act(SB_[sq][:], ps[bf][:], AF.Square, [("ps", bf)], [("SB", sq)])
                    pend = (dch, sq)
                    if t > 0:
                        post_piece(l, 3, t - 1, 1.0, ("SF", 1), dch, 1 - fb)
            mm(ps[7][:], ONES, SB_[pend[1]][:], False, True, ["CBF", ("SB", pend[1])], [("ps", 7)])
            finish_stats(7, SF[1], ("SF", 1))
            if t == 3:
                post_residual(l, 3, t, 1.0, ("SF", 1), fb)

    def pv(obank, ocol, ow, Ptile, blk, j, hh, vw, first, last):
        mm(ps[obank][:, ocol:ocol + ow], SB_[Ptile][:, blk * 128:(blk + 1) * 128], VA[:, j, hh * 68:hh * 68 + vw], first, last,
           [("SB", Ptile), ("VA", j // 4, hh), ("VA1", hh)], [("ps", obank)])

    def pipeline(tiles, stA, stB, depth=2, stC=None, depthC=1):
        n = len(tiles)
        tot = depth + (depthC if stC is not None else 0)
        for idx in range(n + tot):
            if idx < n:
                stA(tiles[idx])
            if 0 <= idx - depth < n:
                stB(tiles[idx - depth])
            if stC is not None and 0 <= idx - tot < n:
                stC(tiles[idx - tot])

    def head_A(l, p, hh):
        pr = slice(hh * 64, hh * 64 + 64)
        tiles = []
        for qb in range(4):
            i0 = 4 * qb
            js = list(range(max(i0 - 4, 0), i0 + 4))
            for j in js:
                a = max(j, i0)
                b = min(j + 4, i0 + 3)
                tiles.append(dict(qb=qb, i0=i0, j=j, a=a, b=b, lo=(a - i0) * 128, hi=(b - i0 + 1) * 128, first=(j == js[0]), last=(j == js[-1])))

        def stA(t):
            j, i0, lo, hi, a, b = t["j"], t["i0"], t["lo"], t["hi"], t["a"], t["b"]
            sbk = rot("s", [3, 4, 5, 6])
            mm(ps[sbk][:, lo:hi], KT[pr, j * 128:(j + 1) * 128], QT[pr, i0 * 128 + lo:i0 * 128 + hi], True, False,
               [("KT", j // 4), ("QT", t["qb"])], [("ps", sbk)])
            mm(ps[sbk][:, lo:hi], IDB, BAND[:, hh * 640 + (a - j) * 128:hh * 640 + (b - j + 1) * 128], False, True, ["CBF", "BAND"], [("ps", sbk)])
            Pt = rots("P", [5, 6, 7])
            act(SB_[Pt][:, lo:hi], ps[sbk][:, lo:hi], AF.Exp, [("ps", sbk)], [("SB", Pt)])
            t["Pt"] = Pt

        def stB(t):
            j, i0, a, b, qb = t["j"], t["i0"], t["a"], t["b"], t["qb"]
            ob = (7, 0)[qb % 2]
            for blk in range(a - i0, b - i0 + 1):
                pv(ob, blk * 65, 65, t["Pt"], blk, j, hh, 65, t["first"] and blk == a - i0, t["last"] and blk == b - i0)
            if t["last"]:
                O = ps[ob][:, 0:260].rearrange("p (b n) -> p b n", b=4)
                P.add("dve", lambda e, O=O: e.reciprocal(out=SM[:, 0:4], in_=O[:, :, 64]), [("ps", ob)], ["SMr"])
                P.add("dve", lambda e, O=O, i0=i0: e.tensor_tensor(out=YP[:, i0:i0 + 4, hh * 64:(hh + 1) * 64], in0=O[:, :, 0:64],
                                                                    in1=SM[:, 0:4].unsqueeze(2).to_broadcast([128, 4, 64]), op=ALU.mult),
                      [("ps", ob), "SMr"], [("YP", qb, hh)])

        pipeline(tiles, stA, stB)

    def head_B(l, hB, hh):
        tiles = []
        for qb in range(4):
            nj = 4 * qb + 4
            for m in range(2):
                for j in range(nj):
                    tiles.append(dict(qb=qb, m=m, j=j, first=(j == 0), last=(j == nj - 1)))

        def stA(t):
            qb, m, j = t["qb"], t["m"], t["j"]
            base = hh * 64 + m * 32
            pr = slice(base, base + 32)
            tp = (96, 0) if base == 96 else None
            diag = j >= 4 * qb
            lo = max(j - 4 * qb, 0) * 128
            sbk = rot("s", [3, 4, 5, 6])
            rk = [("KT", j // 4), ("QT", qb)]
            lo2 = lo
            if diag:
                mm(ps[sbk][:, lo:lo + 128], KT[pr, j * 128:(j + 1) * 128], QT[pr, qb * 512 + lo:qb * 512 + lo + 128], True, False, rk, [("ps", sbk)], tp)
                mm(ps[sbk][:, lo:lo + 128], IDB, DIAGB(hB), False, True, ["CBF"], [("ps", sbk)])
                lo2 = lo + 128
            if lo2 < 512:
                c0 = qb * 512 + lo2 - 128 * j
                mm(ps[sbk][:, lo2:512], KT[pr, j * 128:(j + 1) * 128], QT[pr, qb * 512 + lo2:qb * 512 + 512], not diag, False, rk, [("ps", sbk)], tp)
                mm(ps[sbk][:, lo2:512], BL[:, hB * 128:(hB + 1) * 128], BR[:, c0:c0 + 512 - lo2], False, True, ["BL", "BR"], [("ps", sbk)])
            Pt = rots("P", [5, 6, 7])
            act(SB_[Pt][:, lo:512], ps[sbk][:, lo:512], AF.Exp, [("ps", sbk)], [("SB", Pt)])
            t["Pt"] = Pt
            t["lo"] = lo

        def stB(t):
            qb, m, j, lo = t["qb"], t["m"], t["j"], t["lo"]
            obank = 7 if m == 0 else 0
            for blk in range(lo // 128, 4):
                pv(obank, blk * 65, 65, t["Pt"], blk, j, hh, 65, t["first"] and blk == lo // 128, t["last"] and blk == 3)
            if not (t["last"] and m == 1):
                return
            O1 = ps[7][:, 0:260].rearrange("p (b n) -> p b n", b=4)
            O2 = ps[0][:, 0:260].rearrange("p (b n) -> p b n", b=4)
            T1 = SF[0][:, 0:256].rearrange("p (b n) -> p b n", b=4)
            T2 = SF[1][:, 0:256].rearrange("p (b n) -> p b n", b=4)
            P.add("dve", lambda e, O1=O1: e.reciprocal(out=SM[:, 0:4], in_=O1[:, :, 64]), [("ps", 7)], ["SMr"])
            P.add("dve", lambda e, O2=O2: e.reciprocal(out=SM[:, 4:8], in_=O2[:, :, 64]), [("ps", 0)], ["SMr2"])
            P.add("dve", lambda e: e.tensor_scalar(out=SM[:, 8:12], in0=SM[:, 4:8], scalar1=NEGLAM[:, l:l + 1], scalar2=None, op0=ALU.mult), ["SMr2", ("NEGLAM", l)], ["SMr2n"])
            P.add("dve", lambda e, O1=O1, T1=T1: e.tensor_tensor(out=T1, in0=O1[:, :, 0:64], in1=SM[:, 0:4].unsqueeze(2).to_broadcast([128, 4, 64]), op=ALU.mult),
                  [("ps", 7), "SMr"], [("SF", 0)])
            P.add("dve", lambda e, O2=O2, T2=T2: e.tensor_tensor(out=T2, in0=O2[:, :, 0:64], in1=SM[:, 8:12].unsqueeze(2).to_broadcast([128, 4, 64]), op=ALU.mult),
                  [("ps", 0), "SMr2n"], [("SF", 1)])
            P.add("dve", lambda e, T1=T1, T2=T2: e.tensor_tensor(out=T1, in0=T1, in1=T2, op=ALU.add), [("SF", 0), ("SF", 1)], [("SF", 0)])
            act(T2, T1, AF.Square, [("SF", 0)], [("SF", 1)])
            P.add("dve", lambda e, T2=T2: e.tensor_reduce(out=SM[:, 12:16], in_=T2, axis=AX.X, op=ALU.add), [("SF", 1)], ["SMss"])
            act(SM[:, 16:20], SM[:, 12:16], AF.Ln, ["SMss"], ["SMln"], scale=1.0 / 64.0, bias=EPS)
            act(SM[:, 20:24], SM[:, 16:20], AF.Exp, ["SMln"], ["SMrstd"], scale=-0.5)
            P.add("dve", lambda e, T1=T1: e.tensor_tensor(out=T1, in0=T1, in1=SM[:, 20:24].unsqueeze(2).to_broadcast([128, 4, 64]), op=ALU.mult), [("SF", 0), "SMrstd"], [("SF", 0)])
            P.add("dve", lambda e, T1=T1, qb=qb: e.tensor_tensor(out=YP[:, 4 * qb:4 * qb + 4, hh * 64:(hh + 1) * 64], in0=T1,
                                                                 in1=GSUB[:, l * 64:(l + 1) * 64].unsqueeze(1).to_broadcast([128, 4, 64]), op=ALU.mult),
                  [("SF", 0), ("GSUB", l)], [("YP", qb, hh)])

        pipeline(tiles, stA, stB)

    def head_C(l, hh):
        pr = slice(hh * 64, hh * 64 + 64)
        ACCS = [4, 0]
        tiles = []
        for qb in range(4):
            jtop = 4 * qb + 3
            for j in range(jtop, -1, -1):
                tiles.append(dict(qb=qb, j=j, jtop=jtop, diag=(j >= 4 * qb), lo=max(j - 4 * qb, 0) * 128))
        accstate = [0]

        def stA(t):
            qb, j, lo, diag = t["qb"], t["j"], t["lo"], t["diag"]
            qc = slice(qb * 512 + lo, qb * 512 + 512)
            kc = slice(j * 128, (j + 1) * 128)
            rk = [("KT", j // 4), ("QT", qb)]
            zs = rot("zs", [3, 4, 1])
            mm(ps[zs][:, lo:512], KT[pr, kc], QT[pr, qc], True, not diag, rk, [("ps", zs)])
            if diag:
                mm(ps[zs][:, lo:lo + 128], IDB, MASKC, False, True, ["CBF"], [("ps", zs)])
            et = rots("e", [2, 3])
            act(SF[et][:, lo:512], ps[zs][:, lo:512], AF.Exp, [("ps", zs)], [("SF", et)])
            Lt = rots("L", [1, 2, 3])
            act(SB_[Lt][:, lo:512], SF[et][:, lo:512], AF.Ln, [("SF", et)], [("SB", Lt)], bias=1.0)
            t["Lt"] = Lt

        def stB(t):
            qb, j, lo, diag, jtop, Lt = t["qb"], t["j"], t["lo"], t["diag"], t["jtop"], t["Lt"]
            qc = slice(qb * 512 + lo, qb * 512 + 512)
            kc = slice(j * 128, (j + 1) * 128)
            rk = [("KT", j // 4), ("QT", qb)]
            ob = (7, 0)[qb % 2]
            if j == jtop:
                for a_ in ACCS:
                    P.add("dve", lambda e, a_=a_: e.memset(SB_[a_][:], 0.0), [], [("SB", a_)])
            cur = ACCS[accstate[0] % 2]
            nxt = ACCS[(accstate[0] + 1) % 2]
            za = rot("za", [5, 6])
            mm(ps[za][:, lo:512], KT[pr, kc], QT[pr, qc], True, False, rk, [("ps", za)])
            mm(ps[za][:, lo:512], NEGTRI, SB_[Lt][:, lo:512], False, False, ["CBF", ("SB", Lt)], [("ps", za)])
            if j < jtop:
                mm(ps[za][:, lo:512], NEGONES, SB_[cur][:, lo:512], False, not diag, ["CBF", ("SB", cur)], [("ps", za)])
            if diag:
                mm(ps[za][:, lo:lo + 128], IDB, MASKC, False, True, ["CBF"], [("ps", za)])
            Pt = rots("P", [5, 6, 7])
            act(SB_[Pt][:, lo:512], ps[za][:, lo:512], AF.Exp, [("ps", za)], [("SB", Pt)])
            if j > 0:
                P.add("dve", lambda e, cur=cur, nxt=nxt: e.tensor_tensor(out=SB_[nxt][:, lo:512], in0=SB_[cur][:, lo:512], in1=SB_[Lt][:, lo:512], op=ALU.add),
                      [("SB", cur), ("SB", Lt)], [("SB", nxt)])
                accstate[0] += 1
            t["Pt"] = Pt

        def stC(t):
            qb, j, lo, jtop, Pt = t["qb"], t["j"], t["lo"], t["jtop"], t["Pt"]
            ob = (7, 0)[qb % 2]
            for blk in range(lo // 128, 4):
                pv(ob, blk * 64, 64, Pt, blk, j, hh, 64, (j == jtop) and blk == lo // 128, (j == 0) and (blk == 3))
            if j == 0:
                O = ps[ob][:, 0:256].rearrange("p (b n) -> p b n", b=4)
                P.add("dve", lambda e, O=O: e.tensor_copy(out=YP[:, 4 * qb:4 * qb + 4, hh * 64:(hh + 1) * 64], in_=O), [("ps", ob)], [("YP", qb, hh)])

        pipeline(tiles, stA, stB, 2, stC, 1)


    last_stores = []
    for s in range(NS):
        P.barrier()
        if STAGE >= 2:
            load_x(s)
        for l in range(L):
            if do_ffn:
                P.barrier(engines=("pe", "act", "dve"))
                ffn_phase(l, 0)
            if do_attn:
                P.barrier(engines=("pe", "act", "dve"))
                attn_phase(l, s)
            if do_ffn:
                P.barrier(engines=("pe", "act", "dve"))
                ffn_phase(l, 1)
        P.barrier(engines=("pe", "act", "dve"))
        if STAGE >= 3:
            last_stores = store_x(s)
    fin = Op("sp", None, False, None)
    for o in P.alias_dmas:
        fin.deps.add(o)
    P.ops.append(fin)
    P.emit(nc, es)
    es.close()
    return nc


def host_consts(L, rel_bias):
    k = np.arange(128)[:, None]
    q = np.arange(128)[None, :]
    idf = np.eye(128, dtype=np.float32)
    cbf = np.zeros((128, 1152), np.float32)
    cbf[:, 0:128] = idf
    cbf[:, 128:256] = 1.0
    cbf[:, 256:384] = -(k >= q).astype(np.float32)
    cbf[:, 384:512] = -1.0
    cbf[:, 512:640] = np.where(k < q, 0.0, NEG)
    slopes = [2.0 ** (-8.0 * (h + 1) / 4.0) for h in range(4)]
    for h in range(4):
        t = -slopes[h] * np.abs(q - k).astype(np.float32)
        t = np.where((k >= 64) & (q < 64), NEG, t)
        cbf[:, 640 + h * 128:640 + (h + 1) * 128] = t
    c = np.arange(2048)
    br = np.zeros((128, 2048), np.float32)
    br[0:3] = np.stack([(c // 128) * 128, c % 128, np.ones_like(c)]).astype(np.float32)
    bl = np.zeros((128, 512), np.float32)
    for h in range(4):
        bl[0, h * 128:(h + 1) * 128] = -slopes[h]
        bl[1, h * 128:(h + 1) * 128] = -slopes[h]
        bl[2, h * 128:(h + 1) * 128] = slopes[h] * np.arange(128)
    cc = np.arange(640)[None, :]
    r = cc // 128
    qq = cc % 128
    kk = np.arange(128)[:, None]
    rel = 128 * r + qq - kk
    idx = np.clip(rel, -128, 128) + 128
    masked = ((r == 0) & (kk >= 64) & (qq < 64)) | ((r == 4) & (kk < 64) & (qq >= 64))
    band = np.zeros((L, 3, 128, 1280), np.float32)
    for l in range(L):
        for p in range(3):
            for hh in range(2):
                g = rel_bias[l, 2 * p + hh][idx]
                band[l, p, :, hh * 640:(hh + 1) * 640] = np.where(masked, np.float32(NEG), g)
    return idf, cbf, br, bl, band


_CACHE = {}
NS_PER_LAUNCH = 4


def kernel(**inputs):
    L = 2
    NS = 4
    x = np.ascontiguousarray(inputs["x"], dtype=np.float32)
    gl = [inputs[k] for k in ("ffn1_pre_g", "ffn1_post_g", "mix_pre_g", "mix_post_g", "ffn2_pre_g", "ffn2_post_g")]
    gains = np.stack([np.asarray(g, np.float32) for g in gl], axis=1)
    gains = np.ascontiguousarray(gains.reshape(L, 6, NCH, 128).transpose(3, 0, 1, 2).reshape(128, L * 6 * NCH))
    idf, cbf, br, bl, band = host_consts(L, np.asarray(inputs["rel_bias"], np.float32))
    lamv = np.concatenate([np.asarray(inputs[k], np.float32) for k in ("diff_lambda_q1", "diff_lambda_k1", "diff_lambda_q2", "diff_lambda_k2")], axis=1)
    lamv = np.ascontiguousarray(lamv.reshape(1, L * 128))
    subln = np.ascontiguousarray(np.asarray(inputs["diff_subln_g"], np.float32).reshape(1, L * 64))
    common = {
        "wg1": inputs["ffn1_w_gate"], "wu1": inputs["ffn1_w_up"], "wd1": inputs["ffn1_w_down"],
        "wg2": inputs["ffn2_w_gate"], "wu2": inputs["ffn2_w_up"], "wd2": inputs["ffn2_w_down"],
        "w_in": inputs["w_in"], "w_out": inputs["w_out"], "gains": gains, "band": band,
        "idf": idf, "cbf": cbf, "br": br, "bl": bl, "lamv": lamv, "subln": subln,
    }
    common = {k: np.ascontiguousarray(np.asarray(v, np.float32)) for k, v in common.items()}
    NS_L = NS_PER_LAUNCH
    if "nc" not in _CACHE:
        _CACHE["nc"] = build(NS_L, L)
    nc = _CACHE["nc"]
    out = np.empty_like(x)
    for k in range(NS // NS_L):
        in_maps = []
        for c in range(N_CORES):
            m = dict(common)
            m["x"] = x[c * NS + k * NS_L:c * NS + (k + 1) * NS_L]
            in_maps.append(m)
        res = run_bass_kernel_spmd(nc, in_maps, core_ids=list(range(N_CORES)))
        for c in range(N_CORES):
            out[c * NS + k * NS_L:c * NS + (k + 1) * NS_L] = res.results[c]["out"]
    return out
```
